# Optimizing a Trainium2 kernel written in Bass

```python
import math
import jax
import jax.numpy as jnp
from jax import lax
import numpy as np

D_MODEL = 1024
BATCH = 8
SEQ = 2048
DEPTH = 1

CTX_LEN = 256
GRID_W = 64

ATTN_HEADS = 8
ATTN_KV_HEADS = 2
ATTN_GROUP = ATTN_HEADS // ATTN_KV_HEADS
HEAD_DIM = 128
ROPE_AXIS_DIM = HEAD_DIM // 2
ROPE_THETA = 10000.0
Q_BLOCK = 128

GDN_HEADS = 8
GDN_DK = 128
GDN_DV = 128
GDN_CHUNK = 64
SHORT_CONV = 3

D_FF = 2816
FFN_CONV = 3

NORM_EPS = 1e-6

ATTN_Q_W = ATTN_HEADS * HEAD_DIM
ATTN_KV_W = ATTN_KV_HEADS * HEAD_DIM
GDN_QK_W = GDN_HEADS * GDN_DK
GDN_V_W = GDN_HEADS * GDN_DV
GDN_CONV_W = 2 * GDN_QK_W + GDN_V_W
IN_SPLITS = (ATTN_KV_W, ATTN_KV_W, GDN_CONV_W, 2 * GDN_HEADS, 2 * GDN_HEADS, ATTN_Q_W, GDN_V_W, 2 * D_MODEL)
CTX_COLS = 2 * ATTN_KV_W + GDN_CONV_W + 4 * GDN_HEADS
IN_COLS = CTX_COLS + ATTN_Q_W + GDN_V_W + 2 * D_MODEL

kernel_name = 'hybrid_gqa_gdn_convffn_prefix_block'


def rms_norm(x, gain=None):
    x32 = x.astype(jnp.float32)
    y = x32 * lax.rsqrt(jnp.mean(x32 * x32, axis=-1, keepdims=True) + NORM_EPS)
    if gain is not None:
        y = y * gain.astype(jnp.float32)
    return y.astype(x.dtype)


def l2_normalize(x):
    return x * lax.rsqrt(jnp.sum(x * x, axis=-1, keepdims=True) + NORM_EPS)


def modulate(h, shift, scale):
    return h * (1.0 + scale) + shift


def split_cols(p):
    parts, off = [], 0
    for size in IN_SPLITS:
        if off >= p.shape[-1]:
            break
        parts.append(p[..., off:off + size])
        off += size
    return parts


def dwconv_centred(x, w, b=None):
    width = w.shape[0]
    pad = width // 2
    n = x.shape[1]
    xp = jnp.pad(x, ((0, 0), (pad, pad), (0, 0)))
    y = xp[:, 0:n] * w[0]
    for j in range(1, width):
        y = y + xp[:, j:j + n] * w[j]
    return y if b is None else y + b


def axial_rope_tables(n):
    rows = n // GRID_W
    row_ids = jnp.broadcast_to(jnp.arange(rows, dtype=jnp.float32)[:, None], (rows, GRID_W)).reshape(n)
    col_ids = jnp.broadcast_to(jnp.arange(GRID_W, dtype=jnp.float32)[None, :], (rows, GRID_W)).reshape(n)
    inv_freq = ROPE_THETA ** (-jnp.arange(0, ROPE_AXIS_DIM, 2, dtype=jnp.float32) / ROPE_AXIS_DIM)
    ang = jnp.concatenate([row_ids[:, None] * inv_freq, col_ids[:, None] * inv_freq], axis=-1)
    ang = ang.reshape(n, 2, ROPE_AXIS_DIM // 2)
    return jnp.cos(ang), jnp.sin(ang)


def apply_axial_rope(x, cos, sin):
    x32 = x.astype(jnp.float32).reshape(*x.shape[:-1], 2, 2, ROPE_AXIS_DIM // 2)
    x1, x2 = x32[..., 0, :], x32[..., 1, :]
    cos = cos[None, :, None]
    sin = sin[None, :, None]
    out = jnp.stack([x1 * cos - x2 * sin, x2 * cos + x1 * sin], axis=-2)
    return out.reshape(x.shape).astype(x.dtype)


def sdpa_block(q, k, v):
    s = jnp.einsum('bqhgd,bkhd->bhgqk', q, k).astype(jnp.float32) * (HEAD_DIM ** -0.5)
    p = jax.nn.softmax(s, axis=-1).astype(v.dtype)
    return jnp.einsum('bhgqk,bkhd->bqhgd', p, v)


def latent_attention(q, k_all, v_all):
    b, n = q.shape[:2]
    nb = n // Q_BLOCK
    qb = jnp.moveaxis(q.reshape(b, nb, Q_BLOCK, *q.shape[2:]), 1, 0)
    o = lax.map(lambda q_blk: sdpa_block(q_blk, k_all, v_all), qb)
    return jnp.moveaxis(o, 0, 1).reshape(b, n, ATTN_Q_W)


def gdn_chunked(q, k, v, log_a, beta, s0, with_output):
    b, h, n, dk = q.shape
    dv = v.shape[-1]
    nc = n // GDN_CHUNK
    q = q.reshape(b, h, nc, GDN_CHUNK, dk)
    k = k.reshape(b, h, nc, GDN_CHUNK, dk)
    v = v.reshape(b, h, nc, GDN_CHUNK, dv)
    log_a = log_a.reshape(b, h, nc, GDN_CHUNK)
    beta = beta.reshape(b, h, nc, GDN_CHUNK)
    gam = jnp.cumsum(log_a, axis=-1)
    idx = jnp.arange(GDN_CHUNK)
    strict = idx[:, None] > idx[None, :]
    incl = idx[:, None] >= idx[None, :]
    dec = jnp.exp(jnp.where(incl, gam[..., :, None] - gam[..., None, :], -jnp.inf))
    kk = jnp.einsum('bhncd,bhnsd->bhncs', k, k)
    a_mat = jnp.where(strict, beta[..., :, None] * dec * kk, 0.0) + jnp.eye(GDN_CHUNK, dtype=jnp.float32)
    rhs = jnp.concatenate([beta[..., None] * v, (beta * jnp.exp(gam))[..., None] * k], axis=-1)
    uw = lax.linalg.triangular_solve(a_mat, rhs, left_side=True, lower=True, unit_diagonal=True)
    u, w = uw[..., :dv], uw[..., dv:]
    k_dec = k * jnp.exp(gam[..., -1:] - gam)[..., None]
    g_last = jnp.exp(gam[..., -1])
    xs = [u, w, k_dec, g_last]
    if with_output:
        q_dec = q * jnp.exp(gam)[..., None]
        p = dec * jnp.einsum('bhncd,bhnsd->bhncs', q, k)
        xs = xs + [q_dec, p]
    xs = tuple(jnp.moveaxis(t, 2, 0) for t in xs)

    def step(s, inp):
        u_c, w_c, kd_c, gl_c = inp[:4]
        delta = u_c - jnp.einsum('bhcd,bhde->bhce', w_c, s)
        s_new = gl_c[..., None, None] * s + jnp.einsum('bhcd,bhce->bhde', kd_c, delta)
        if with_output:
            qd_c, p_c = inp[4], inp[5]
            o = jnp.einsum('bhcd,bhde->bhce', qd_c, s) + jnp.einsum('bhcs,bhse->bhce', p_c, delta)
            return s_new, o
        return s_new, None

    s_fin, o = lax.scan(step, s0, xs)
    if with_output:
        o = jnp.moveaxis(o, 0, 2).reshape(b, h, n, dv)
    return o, s_fin


def gdn_prepare(qkv, db, da, conv_w, a_log, dt_bias):
    b, n, _ = qkv.shape
    qkv = jax.nn.silu(dwconv_centred(qkv, conv_w)).astype(jnp.float32)
    q, k, v = jnp.split(qkv, [GDN_QK_W, 2 * GDN_QK_W], axis=-1)
    q = l2_normalize(q.reshape(b, n, GDN_HEADS, GDN_DK)) * (GDN_DK ** -0.5)
    k = l2_normalize(k.reshape(b, n, GDN_HEADS, GDN_DK))
    v = v.reshape(b, n, GDN_HEADS, GDN_DV)
    q, k, v = (t.transpose(0, 2, 1, 3) for t in (q, k, v))
    beta = jax.nn.sigmoid(db.astype(jnp.float32)).reshape(b, n, 2, GDN_HEADS).transpose(2, 0, 3, 1)
    da = da.astype(jnp.float32).reshape(b, n, 2, GDN_HEADS).transpose(2, 0, 3, 1)
    log_a = -jnp.exp(a_log.astype(jnp.float32))[:, None, :, None] * jax.nn.softplus(
        da + dt_bias.astype(jnp.float32)[:, None, :, None])
    return q, k, v, log_a, beta


def gdn_bidirectional(ctx_in, lat_in, with_ctx_output):
    qc, kc, vc, lac, bc = ctx_in
    ql, kl, vl, lal, bl = lat_in
    s0 = jnp.zeros((ql.shape[0], GDN_HEADS, GDN_DK, GDN_DV), jnp.float32)
    o_lat, o_ctx = None, None
    for d in range(2):
        rev = (lambda t: jnp.flip(t, axis=2)) if d == 1 else (lambda t: t)
        oc, sc = gdn_chunked(rev(qc), rev(kc), rev(vc), rev(lac[d]), rev(bc[d]), s0, with_ctx_output)
        ol, _ = gdn_chunked(rev(ql), rev(kl), rev(vl), rev(lal[d]), rev(bl[d]), sc, True)
        o_lat = rev(ol) if o_lat is None else o_lat + rev(ol)
        if with_ctx_output:
            o_ctx = rev(oc) if o_ctx is None else o_ctx + rev(oc)
    return o_lat, o_ctx


def gdn_output(o, z, norm_w):
    b, n = z.shape[:2]
    o = rms_norm(o.transpose(0, 2, 1, 3), norm_w)
    y = o * jax.nn.silu(z.astype(jnp.float32).reshape(b, n, GDN_HEADS, GDN_DV))
    return y.reshape(b, n, GDN_V_W).astype(z.dtype)


def merge_branches(attn, gdn, gates, w_pa, w_pd, w_out):
    g_a, g_d = jnp.split(gates, 2, axis=-1)
    y = jax.nn.sigmoid(g_a) * (attn @ w_pa) + jax.nn.sigmoid(g_d) * (gdn @ w_pd)
    return y @ w_out


def conv_ffn(h, w_up, conv_w, conv_b, w_down):
    u = dwconv_centred(h @ w_up, conv_w, conv_b)
    g, val = jnp.split(u, 2, axis=-1)
    return (jax.nn.silu(g) * val) @ w_down


def hybrid_layer(x, ctx, c, c_ctx, w_mod, b_mod, w_in, q_norm_w, k_norm_w, conv_qkv_w, a_log, dt_bias,
                 gdn_norm_w, w_pa, w_pd, w_out, w_up, ffn_conv_w, ffn_conv_b, w_down, update_ctx):
    b, n, _ = x.shape
    cl = ctx.shape[1]
    mod_lat = jax.nn.silu(c) @ w_mod + b_mod
    mod_ctx = jax.nn.silu(c_ctx) @ w_mod + b_mod
    sh1, sc1, g1, sh2, sc2, g2 = [m[:, None, :] for m in jnp.split(mod_lat, 6, axis=-1)]
    csh1, csc1, cg1, csh2, csc2, cg2 = jnp.split(mod_ctx, 6, axis=-1)

    hx = modulate(rms_norm(x), sh1, sc1)
    hc = modulate(rms_norm(ctx), csh1, csc1)
    ak_x, av_x, qkv_x, db_x, da_x, aq_x, z_x, gate_x = split_cols(hx @ w_in)
    ak_c, av_c, qkv_c, db_c, da_c, *rest_c = split_cols(hc @ (w_in if update_ctx else w_in[:, :CTX_COLS]))

    cos, sin = axial_rope_tables(n)
    q_x = apply_axial_rope(rms_norm(aq_x.reshape(b, n, ATTN_HEADS, HEAD_DIM), q_norm_w), cos, sin)
    q_x = q_x.reshape(b, n, ATTN_KV_HEADS, ATTN_GROUP, HEAD_DIM)
    k_x = apply_axial_rope(rms_norm(ak_x.reshape(b, n, ATTN_KV_HEADS, HEAD_DIM), k_norm_w), cos, sin)
    v_x = av_x.reshape(b, n, ATTN_KV_HEADS, HEAD_DIM)
    k_c = rms_norm(ak_c.reshape(b, cl, ATTN_KV_HEADS, HEAD_DIM), k_norm_w)
    v_c = av_c.reshape(b, cl, ATTN_KV_HEADS, HEAD_DIM)
    attn_x = latent_attention(q_x, jnp.concatenate([k_c, k_x], axis=1), jnp.concatenate([v_c, v_x], axis=1))

    gdn_x_in = gdn_prepare(qkv_x, db_x, da_x, conv_qkv_w, a_log, dt_bias)
    gdn_c_in = gdn_prepare(qkv_c, db_c, da_c, conv_qkv_w, a_log, dt_bias)
    o_x, o_c = gdn_bidirectional(gdn_c_in, gdn_x_in, update_ctx)
    gdn_x = gdn_output(o_x, z_x, gdn_norm_w)

    x = x + g1 * merge_branches(attn_x, gdn_x, gate_x, w_pa, w_pd, w_out)
    x = x + g2 * conv_ffn(modulate(rms_norm(x), sh2, sc2), w_up, ffn_conv_w, ffn_conv_b, w_down)

    if update_ctx:
        aq_c, z_c, gate_c = rest_c
        q_c = rms_norm(aq_c.reshape(b, cl, ATTN_HEADS, HEAD_DIM), q_norm_w)
        q_c = q_c.reshape(b, cl, ATTN_KV_HEADS, ATTN_GROUP, HEAD_DIM)
        attn_c = sdpa_block(q_c, k_c, v_c).reshape(b, cl, ATTN_Q_W)
        gdn_c = gdn_output(o_c, z_c, gdn_norm_w)
        ctx = ctx + cg1 * merge_branches(attn_c, gdn_c, gate_c, w_pa, w_pd, w_out)
        ctx = ctx + cg2 * conv_ffn(modulate(rms_norm(ctx), csh2, csc2), w_up, ffn_conv_w, ffn_conv_b, w_down)
    return x, ctx


def setup_inputs(seed: int = 0) -> dict:
    key = jax.random.key(seed)
    ks = jax.random.split(key, 22)
    f32 = jnp.float32

    def dense(k, shape, fan_in, s=1.0):
        return s * (fan_in ** -0.5) * jax.random.normal(k, shape, f32)

    dt = jnp.exp(jax.random.uniform(ks[10], (DEPTH, 2, GDN_HEADS), f32,
                                    minval=math.log(1e-3), maxval=math.log(1e-1)))
    return {
        'x': jax.random.normal(ks[0], (BATCH, SEQ, D_MODEL), f32),
        'c': jax.random.normal(ks[1], (BATCH, D_MODEL), f32),
        'ctx': jax.random.normal(ks[2], (BATCH, CTX_LEN, D_MODEL), f32),
        'c_ctx': jax.random.normal(ks[3], (D_MODEL,), f32),
        'w_mod': dense(ks[4], (DEPTH, D_MODEL, 6 * D_MODEL), D_MODEL, 0.5),
        'b_mod': 0.02 * jax.random.normal(ks[5], (DEPTH, 6 * D_MODEL), f32),
        'w_in': dense(ks[6], (DEPTH, D_MODEL, IN_COLS), D_MODEL),
        'q_norm_w': 1.0 + 0.05 * jax.random.normal(ks[7], (DEPTH, HEAD_DIM), f32),
        'k_norm_w': 1.0 + 0.05 * jax.random.normal(ks[8], (DEPTH, HEAD_DIM), f32),
        'conv_qkv_w': dense(ks[9], (DEPTH, SHORT_CONV, GDN_CONV_W), SHORT_CONV),
        'a_log': jnp.log(jax.random.uniform(ks[11], (DEPTH, 2, GDN_HEADS), f32, minval=1.0, maxval=16.0)),
        'dt_bias': dt + jnp.log(-jnp.expm1(-dt)),
        'gdn_norm_w': 1.0 + 0.05 * jax.random.normal(ks[12], (DEPTH, GDN_DV), f32),
        'w_pa': dense(ks[13], (DEPTH, ATTN_Q_W, D_MODEL), ATTN_Q_W),
        'w_pd': dense(ks[14], (DEPTH, GDN_V_W, D_MODEL), GDN_V_W),
        'w_out': dense(ks[15], (DEPTH, D_MODEL, D_MODEL), D_MODEL),
        'w_up': dense(ks[16], (DEPTH, D_MODEL, 2 * D_FF), D_MODEL),
        'ffn_conv_w': dense(ks[17], (DEPTH, FFN_CONV, 2 * D_FF), FFN_CONV),
        'ffn_conv_b': 0.02 * jax.random.normal(ks[18], (DEPTH, 2 * D_FF), f32),
        'w_down': dense(ks[19], (DEPTH, D_FF, D_MODEL), D_FF),
        'final_norm_w': 1.0 + 0.05 * jax.random.normal(ks[20], (D_MODEL,), f32),
    }


def reference(x, c, ctx, c_ctx, w_mod, b_mod, w_in, q_norm_w, k_norm_w, conv_qkv_w, a_log, dt_bias,
              gdn_norm_w, w_pa, w_pd, w_out, w_up, ffn_conv_w, ffn_conv_b, w_down, final_norm_w):
    for layer in range(DEPTH):
        x, ctx = hybrid_layer(
            x, ctx, c, c_ctx, w_mod[layer], b_mod[layer], w_in[layer], q_norm_w[layer], k_norm_w[layer],
            conv_qkv_w[layer], a_log[layer], dt_bias[layer], gdn_norm_w[layer], w_pa[layer], w_pd[layer],
            w_out[layer], w_up[layer], ffn_conv_w[layer], ffn_conv_b[layer], w_down[layer],
            update_ctx=layer < DEPTH - 1)
    return rms_norm(x, final_norm_w)
```

```python
import numpy as np
import concourse.bass as bass
import concourse.mybir as mybir

ENG_NAMES = ("pe", "act", "dve", "pool", "sp")
_SEM_CTR = [0]


def _box(ap):
    t = ap.tensor
    shp = list(t.shape)
    psz = 1
    for s in shp[1:]:
        psz *= int(s)
    off = int(ap.offset)
    dims = ap.ap
    st0, n0 = dims[0]
    if st0 != 0:
        psz = st0
    p0 = off // psz
    f0 = off % psz
    if st0 == 0:
        np_ = 1
    else:
        np_ = n0
    ext = 0
    for st, n in dims[1:]:
        ext += abs(st) * (n - 1)
    return (ap.name, p0, p0 + np_, f0, f0 + ext + 1)


def _ovl(a, b):
    return a[0] == b[0] and a[1] < b[2] and b[1] < a[2] and a[3] < b[4] and b[3] < a[4]


def _cov(a, b):
    return a[0] == b[0] and a[1] <= b[1] and b[2] <= a[2] and a[3] <= b[3] and b[4] <= a[4]


class Op:
    __slots__ = ("eng", "fn", "deps", "is_dma", "sig", "idx", "waits", "pos", "key")

    def __init__(self, eng, fn, is_dma):
        self.eng = eng
        self.fn = fn
        self.deps = set()
        self.is_dma = is_dma
        self.sig = None
        self.waits = []


class Prog:
    def __init__(self, nc):
        self.nc = nc
        self.ops = []
        self.recs = {}
        self.psum_names = set()

    def whole(self, name):
        self.psum_names.add(name)

    def _access(self, ap, op, is_write):
        if ap is None:
            return
        if isinstance(ap, tuple):
            box = ap
        else:
            if str(ap.space) not in ("SB", "PSUM"):
                return
            box = _box(ap)
        is_ps = box[0] in self.psum_names
        if is_ps:
            box = (box[0], 0, 128, 0, 1 << 30)
        recs = self.recs.setdefault(box[0], [])
        new = []
        for r in recs:
            rb, rop, rw = r
            if _ovl(rb, box):
                if (is_write or rw or (is_ps and rop.eng != op.eng)) and rop is not op:
                    op.deps.add(rop)
                if is_write and _cov(box, rb):
                    continue
                if (not is_write) and (not rw) and rb == box and rop.eng == op.eng and not rop.is_dma:
                    continue
            new.append(r)
        new.append([box, op, is_write])
        self.recs[box[0]] = new

    def add(self, eng, fn, reads=(), writes=(), is_dma=False):
        op = Op(eng, fn, is_dma)
        op.idx = len(self.ops)
        for ap in reads:
            self._access(ap, op, False)
        for ap in writes:
            self._access(ap, op, True)
        self.ops.append(op)
        return op

    def mm(self, out, lhsT, rhs, start=True, stop=True, **kw):
        return self.add("pe", lambda e: e.matmul(out, lhsT, rhs, start=start, stop=stop, **kw),
                        reads=[lhsT, rhs], writes=[out])

    def tr(self, out, in_, ident):
        return self.add("pe", lambda e: e.transpose(out, in_, ident), reads=[in_, ident], writes=[out])

    def act(self, out, in_, func, bias=None, scale=None, accum_out=None, eng="act"):
        kw = {}
        rd = [in_]
        if bias is not None:
            kw["bias"] = bias
            if not isinstance(bias, (int, float)):
                rd.append(bias)
        if scale is not None:
            kw["scale"] = scale
            if not isinstance(scale, (int, float)):
                rd.append(scale)
        wr = [out]
        if accum_out is not None:
            kw["accum_out"] = accum_out
            wr.append(accum_out)
        return self.add(eng, lambda e: e.activation(out, in_, func, **kw), reads=rd, writes=wr)

    def tt(self, eng, out, in0, in1, op):
        return self.add(eng, lambda e: e.tensor_tensor(out, in0, in1, op), reads=[in0, in1], writes=[out])

    def ts(self, eng, out, in0, s1, s2, op0, op1=None, accum_out=None):
        rd = [in0]
        if not isinstance(s1, (int, float)):
            rd.append(s1)
        if s2 is not None and not isinstance(s2, (int, float)):
            rd.append(s2)
        kw = {}
        wr = [out]
        if accum_out is not None:
            kw["accum_out"] = accum_out
            wr.append(accum_out)
        if op1 is None:
            return self.add(eng, lambda e: e.tensor_scalar(out, in0, s1, None, op0, **kw), reads=rd, writes=wr)
        return self.add(eng, lambda e: e.tensor_scalar(out, in0, s1, s2, op0, op1, **kw), reads=rd, writes=wr)

    def stt(self, eng, out, in0, scalar, in1, op0, op1):
        rd = [in0, in1]
        if not isinstance(scalar, (int, float)):
            rd.append(scalar)
        return self.add(eng, lambda e: e.scalar_tensor_tensor(out, in0, scalar, in1, op0, op1), reads=rd, writes=[out])

    def copy(self, eng, out, in_):
        if eng == "act":
            return self.add(eng, lambda e: e.copy(out, in_), reads=[in_], writes=[out])
        return self.add(eng, lambda e: e.tensor_copy(out, in_), reads=[in_], writes=[out])

    def memset(self, eng, out, val):
        return self.add(eng, lambda e: e.memset(out, val), reads=[], writes=[out])

    def dma(self, eng, out, in_, **kw):
        sb = out if str(out.space) in ("SB", "PSUM") else in_
        op = self.add(eng, lambda e: e.dma_start(out=out, in_=in_, **kw), reads=[in_], writes=[out], is_dma=True)
        op.key = _box(sb)
        return op

    def build(self, block, stack, final_waits=()):
        nc = self.nc
        ops = self.ops
        SEM_CAP = 30000
        need = set()
        for op in ops:
            for d in op.deps:
                if d.eng == "pe" and op.eng == "pe" and not d.is_dma:
                    continue
                need.add(d.idx)
        for _, op in final_waits:
            need.add(op.idx)
        nsem = [0]

        def new_sem(name):
            nsem[0] += 1
            _SEM_CTR[0] += 1
            return stack.enter_context(nc.semaphore("%s_%d" % (name, _SEM_CTR[0])))

        cur, cnt = {}, {}
        dsem = {}
        for op in ops:
            if op.idx not in need and not op.is_dma:
                continue
            if op.is_dma:
                if op.key not in dsem:
                    dsem[op.key] = [new_sem("d"), 0]
                ent = dsem[op.key]
                ent[1] += 16
                op.sig = (ent[0], ent[1])
            else:
                e = op.eng
                if e not in cur or cnt[e] >= SEM_CAP:
                    cur[e] = new_sem("s" + e)
                    cnt[e] = 0
                cnt[e] += 1
                op.sig = (cur[e], cnt[e])
        self.n_sems = nsem[0]
        seen = {e: {} for e in ENG_NAMES}
        per_eng = {e: [] for e in ENG_NAMES}
        nwaits = 0
        for op in ops:
            sn = seen[op.eng]
            for d in sorted(op.deps, key=lambda o: o.idx):
                if d.eng == "pe" and op.eng == "pe" and not d.is_dma:
                    continue
                sem, val = d.sig
                k = id(sem)
                if sn.get(k, 0) >= val:
                    continue
                sn[k] = val
                op.waits.append((sem, val))
                nwaits += 1
            per_eng[op.eng].append(op)
        self.n_waits = nwaits
        fw = {e: [] for e in ENG_NAMES}
        for e, op in final_waits:
            fw[e].append(op.sig)
        for e in fw:
            best = {}
            for sem, val in fw[e]:
                if id(sem) not in best or best[id(sem)][1] < val:
                    best[id(sem)] = (sem, val)
            fw[e] = [v for v in best.values() if seen[e].get(id(v[0]), 0) < v[1]]

        def emit(eng_obj, name):
            for op in per_eng[name]:
                for sem, val in op.waits:
                    eng_obj.wait_ge(sem, val)
                ins = op.fn(eng_obj)
                if op.sig is not None:
                    ins.then_inc(op.sig[0], 16 if op.is_dma else 1)
            for sem, val in fw[name]:
                eng_obj.wait_ge(sem, val)

        @block.tensor
        def _(e):
            emit(e, "pe")

        @block.scalar
        def _(e):
            emit(e, "act")

        @block.vector
        def _(e):
            emit(e, "dve")

        @block.gpsimd
        def _(e):
            emit(e, "pool")

        @block.sync
        def _(e):
            emit(e, "sp")

import contextlib
import ml_dtypes
from concourse.bass_utils import run_bass_kernel_spmd

F32 = mybir.dt.float32
BF16 = mybir.dt.bfloat16
AF = mybir.ActivationFunctionType
ALU = mybir.AluOpType
AX = mybir.AxisListType

D = 1024
NTOK = 2048
CL = 256
NT = NTOK // 128
NTC = CL // 128
NTA = NT + NTC
DFF = 2816
EPS = 1e-6
IN_COLS = 7712
C_AK, C_AV, C_GQ, C_GK, C_GV, C_DB, C_DA, C_AQ, C_Z, C_GA, C_GD = 0, 256, 512, 1536, 2560, 3584, 3600, 3616, 4640, 5664, 6688

PRM = {}
_o = 0
for _n, _w in (("bmod", 48), ("cqkv", 72), ("cffw", 132), ("cffb", 44), ("gnw", 128), ("qnw", 128),
               ("knw", 128), ("fnw", 1024), ("alog", 16), ("dtb", 16), ("cvec", 16)):
    PRM[_n] = (_o, _o + _w)
    _o += _w
NP = _o
CST = {}
_o = 0
for _n, _w in (("ident", 128), ("ustrict", 128), ("uincl", 128), ("lstrict", 128), ("lincl", 128),
               ("ones", 128), ("sel", 2048), ("cos2", 2048), ("sins", 2048)):
    CST[_n] = (_o, _o + _w)
    _o += _w
NCST = _o

DEBUG_TAPS = None
STOP_AFTER = None
SKIP_GDN = None


def host_constants():
    cst = np.zeros((128, NCST), np.float32)
    i = np.arange(128)
    cst[:, CST["ident"][0]:CST["ident"][1]] = np.eye(128, dtype=np.float32)
    s, c = i[:, None], i[None, :]
    cst[:, CST["ustrict"][0]:CST["ustrict"][1]] = (c > s)
    cst[:, CST["uincl"][0]:CST["uincl"][1]] = (c >= s)
    cst[:, CST["lstrict"][0]:CST["lstrict"][1]] = (c < s)
    cst[:, CST["lincl"][0]:CST["lincl"][1]] = (c <= s)
    cst[:, CST["ones"][0]:CST["ones"][1]] = 1.0
    sel = np.zeros((128, 16, 128), np.float32)
    for r in range(16):
        sel[r, r, :] = 1.0
    cst[:, CST["sel"][0]:CST["sel"][1]] = sel.reshape(128, 2048)
    t = np.arange(NTOK)
    row = (t // 64).astype(np.float32)
    col = (t % 64).astype(np.float32)
    inv = (np.float32(10000.0) ** (-np.arange(0, 64, 2, dtype=np.float32) / np.float32(64))).astype(np.float32)
    ang = np.stack([row[:, None] * inv[None, :], col[:, None] * inv[None, :]], axis=1).astype(np.float32)
    cos = np.cos(ang).astype(np.float32)
    sin = np.sin(ang).astype(np.float32)
    cos2 = np.stack([cos, cos], axis=2).reshape(NTOK, 128)
    sins = np.stack([-sin, sin], axis=2).reshape(NTOK, 128)
    cst[:, CST["cos2"][0]:CST["cos2"][1]] = cos2.reshape(NT, 128, 128).transpose(1, 0, 2).reshape(128, 2048)
    cst[:, CST["sins"][0]:CST["sins"][1]] = sins.reshape(NT, 128, 128).transpose(1, 0, 2).reshape(128, 2048)
    cbf = np.zeros((128, 256), np.float32)
    cbf[:, 0:128] = np.eye(128)
    cbf[:, 128:256] = 1.0
    return cst, cbf.astype(ml_dtypes.bfloat16)


def host_params(b, c, c_ctx, b_mod, conv_qkv_w, ffn_conv_w, ffn_conv_b, gdn_norm_w, q_norm_w, k_norm_w,
                final_norm_w, a_log, dt_bias):
    prm = np.zeros((128, NP), np.float32)

    def put(name, arr):
        a, e = PRM[name]
        prm[:, a:e] = np.asarray(arr, np.float32).reshape(128, e - a)

    put("bmod", b_mod[0].reshape(48, 128).T)
    put("cqkv", conv_qkv_w[0].reshape(3, 24, 128).transpose(2, 0, 1))
    put("cffw", ffn_conv_w[0].reshape(3, 44, 128).transpose(2, 0, 1))
    put("cffb", ffn_conv_b[0].reshape(44, 128).T)
    put("gnw", np.broadcast_to(gdn_norm_w[0][None, :], (128, 128)))
    put("qnw", np.broadcast_to(q_norm_w[0][None, :], (128, 128)))
    put("knw", np.broadcast_to(k_norm_w[0][None, :], (128, 128)))
    put("fnw", np.broadcast_to(final_norm_w[None, :], (128, 1024)))
    put("alog", np.broadcast_to(a_log[0].reshape(1, 16), (128, 16)))
    put("dtb", np.broadcast_to(dt_bias[0].reshape(1, 16), (128, 16)))
    cv = np.stack([c[b].reshape(8, 128).T, c_ctx.reshape(8, 128).T], axis=2)
    put("cvec", cv)
    return prm


class Ctx:
    pass


def build_nc():
    nc = bass.Bass("TRN2", target_bir_lowering=False)
    K = Ctx()
    K.nc = nc
    dt_in = lambda name, shape, dt=F32: nc.dram_tensor(name, shape, dt, kind="ExternalInput").ap()
    K.x = dt_in("x", [NTOK, D])
    K.ctx = dt_in("ctx", [CL, D])
    K.w_mod = dt_in("w_mod", [D, 6 * D])
    K.w_in = dt_in("w_in", [D, IN_COLS])
    K.w_pa = dt_in("w_pa", [D, D])
    K.w_pd = dt_in("w_pd", [D, D])
    K.w_out = dt_in("w_out", [D, D])
    K.w_up = dt_in("w_up", [D, 2 * DFF])
    K.w_down = dt_in("w_down", [DFF, D])
    K.prm_d = dt_in("prm", [128, NP])
    K.cst_d = dt_in("cst", [128, NCST])
    K.cbf_d = dt_in("cbf", [128, 256], BF16)
    K.out = nc.dram_tensor("out", [NTOK, D], F32, kind="ExternalOutput").ap()
    K.taps = {}
    K.sem_stack = contextlib.ExitStack()
    with K.sem_stack, contextlib.ExitStack() as top:
        K.top = top
        sb = lambda name, shape, dt: top.enter_context(nc.sbuf_tensor(name, shape, dt))
        K.prm = sb("prm_sb", [128, NP], F32)
        K.identf = sb("identf", [128, 128], F32)
        K.cbf = sb("cbf_sb", [128, 256], BF16)
        K.identb = K.cbf[:, 0:128]
        K.onesb = K.cbf[:, 128:256]
        K.modv = sb("modv", [128, 2, 6, 8], F32)
        K.hxT = sb("hxT", [128, 8, NTOK], BF16)
        K.GG = sb("GG", [128, 16384], BF16)
        K.gdnT = K.GG[:].rearrange("p (b h t) -> p b h t", b=4, h=8)
        K.gm = K.GG[:].rearrange("p (i f) -> p i f", i=16)
        K.hcT = sb("hcT", [128, 8, CL], BF16)
        phase_mixer(K)
        if STOP_AFTER in ("p0", "p1"):
            return nc
        if SKIP_GDN is None:
            phase_gdn(K)
        if STOP_AFTER == "a2":
            return nc
        with contextlib.ExitStack() as mid:
            K.AG = mid.enter_context(nc.sbuf_tensor("AG", [128, 16384], BF16))
            K.attnT = K.AG[:].rearrange("p (b h t) -> p b h t", b=4, h=8)
            phase_attn_outer(K)
            if STOP_AFTER == "a1":
                return nc
            phase_merge(K)
        if STOP_AFTER == "b":
            return nc
        phase_ffn(K)
    return nc


def P_new(K, psum_names):
    P = Prog(K.nc)
    for n in psum_names:
        P.whole(n)
    return P


def finish_phase(K, P, st, extra_final=()):
    nc = K.nc
    finals = list(extra_final)
    for name, ap in getattr(P, "tap_list", []):
        shp = list(ap.shape)
        fl = 1
        for v_ in shp[1:]:
            fl *= v_
        d = nc.dram_tensor("tap_" + name, [shp[0], fl], ap.dtype, kind="ExternalOutput").ap()
        K.taps[name] = (shp, ap.dtype)
        letters = "abcdefg"[:len(shp) - 1]
        src = ap if len(shp) == 2 else ap.rearrange("p %s -> p (%s)" % (" ".join(letters), " ".join(letters)))
        finals.append(P.dma("sp", d, src))
    last = {}
    for op in P.ops:
        if not op.is_dma:
            last[op.eng] = op
    dmas = [op for op in P.ops if op.is_dma]
    fw = []
    for e in ENG_NAMES:
        for e2, op in last.items():
            if e2 != e:
                fw.append((e, op))
        for op in dmas:
            fw.append((e, op))
    block = st.enter_context(nc.Block())
    P.build(block, K.sem_stack, final_waits=fw)
    print("phase ops", len(P.ops), "sems", P.n_sems, "waits", P.n_waits, "sbuf_left", nc.sbuf_bytes_remaining)


def tap(P, name, ap):
    if DEBUG_TAPS is not None and name in DEBUG_TAPS:
        if not hasattr(P, "tap_list"):
            P.tap_list = []
        P.tap_list.append((name, ap))


def wview(w, c0, c1):
    return w.rearrange("(k p) n -> p k n", p=128)[:, :, c0:c1]


def rsqrt_act(P, out, in_, scale, bias):
    P.act(out, in_, AF.Ln, scale=scale, bias=bias)
    P.act(out, out, AF.Exp, scale=-0.5)


def phase_mixer(K):
    nc = K.nc
    with contextlib.ExitStack() as st:
        sb = lambda name, shape, dt: st.enter_context(nc.sbuf_tensor(name + "_m", shape, dt))
        pbs = [st.enter_context(nc.psum_tensor("pbm%d" % i, [128, 512], F32)) for i in range(8)]
        P = P_new(K, ["pbm%d" % i for i in range(8)])
        cst = sb("cst_sb", [128, 128], F32)
        cs = lambda n: cst[:, CST[n][0]:CST[n][1]]
        prm = K.prm
        pr = lambda n: prm[:, PRM[n][0]:PRM[n][1]]
        P.dma("sp", prm[:], K.prm_d[:, :])
        P.dma("sp", cst[:], K.cst_d[:, 0:128])
        P.dma("sp", K.cbf[:], K.cbf_d[:, :])
        P.copy("dve", K.identf[:], cs("ident"))
        wb = [sb("wb%d" % i, [128, 8, 512], BF16) for i in range(3)]
        scb = sb("scb", [128, 16], BF16)
        P.act(scb[:], pr("cvec"), AF.Silu)
        scv = scb[:].rearrange("p (k w) -> p k w", w=2)
        pmod = pbs[0]
        for bi in range(12):
            w = wb[bi % 3]
            P.dma("pool", w[:], wview(K.w_mod, bi * 512, (bi + 1) * 512))
            for jj in range(4):
                j = bi * 4 + jj
                for k in range(8):
                    P.mm(pmod[:, 2 * j:2 * j + 2], w[:, k, jj * 128:(jj + 1) * 128], scv[:, k, :],
                         start=(k == 0), stop=(k == 7))
        pmv = pmod[:, 0:96].rearrange("p (j w) -> p j w", w=2)
        for wch in range(2):
            P.tt("dve", K.modv[:, wch].rearrange("p g k -> p (g k)"), pmv[:, :, wch], pr("bmod"), ALU.add)
        for wch in range(2):
            for g in (1, 4):
                P.ts("dve", K.modv[:, wch, g, :], K.modv[:, wch, g, :], 1.0, None, ALU.add)
        tap(P, "modv", K.modv[:])
        if STOP_AFTER == "p0":
            finish_phase(K, P, st)
            return
        xts = [sb("xt%d" % i, [128, D], F32) for i in range(2)]
        xns = [sb("xn%d" % i, [128, D], BF16) for i in range(2)]
        junk = sb("junk", [128, D], BF16)
        ss = sb("ss", [128, NTA], F32)
        rs = sb("rs", [128, NTA], F32)
        for t in range(NTA):
            isctx = t < NTC
            wch = 1 if isctx else 0
            src = K.ctx[t * 128:(t + 1) * 128, :] if isctx else K.x[(t - NTC) * 128:(t - NTC + 1) * 128, :]
            dstT = K.hcT if isctx else K.hxT
            c0 = t * 128 if isctx else (t - NTC) * 128
            xt, xn = xts[t % 2], xns[t % 2]
            pT = pbs[1 + (t % 2)][:].bitcast(BF16)
            P.dma("sp", xt[:], src)
            P.act(junk[:], xt[:], AF.Square, accum_out=ss[:, t:t + 1])
            rsqrt_act(P, rs[:, t:t + 1], ss[:, t:t + 1], 1.0 / D, EPS)
            P.ts("dve", xn[:], xt[:], rs[:, t:t + 1], None, ALU.mult)
            for k in range(8):
                P.tr(pT[:, k * 128:(k + 1) * 128], xn[:, k * 128:(k + 1) * 128], K.identb)
            for k in range(8):
                o = dstT[:, k, c0:c0 + 128]
                i_ = pT[:, k * 128:(k + 1) * 128]
                P.ts("dve", o, i_, K.modv[:, wch, 1, k:k + 1], K.modv[:, wch, 0, k:k + 1], ALU.mult, ALU.add)
        tap(P, "hxT", K.hxT[:])
        tap(P, "hcT", K.hcT[:])
        finish_phase(K, P, st)


def phase_attn_outer(K):
    nc = K.nc
    with contextlib.ExitStack() as st:
        sb = lambda name, shape, dt: st.enter_context(nc.sbuf_tensor(name + "_a", shape, dt))
        pbs = [st.enter_context(nc.psum_tensor("pba%d" % i, [128, 512], F32)) for i in range(8)]
        P = P_new(K, ["pba%d" % i for i in range(8)])
        cst = sb("cst_sb", [128, 4096], F32)
        P.dma("sp", cst[:], K.cst_d[:, CST["cos2"][0]:CST["sins"][1]])
        off = CST["cos2"][0]
        cs = lambda n: cst[:, CST[n][0] - off:CST[n][1] - off]
        pr = lambda n: K.prm[:, PRM[n][0]:PRM[n][1]]
        wb = [sb("wb%d" % i, [128, 8, 512], BF16) for i in range(3)]
        phase_attn(K, P, st, sb, pbs, cs, pr, wb)
        finish_phase(K, P, st)


def prep_qk(P, K, src_ps, H, w_rep, rope_tile, out_bf, bufs, cs):
    xq, sq, t1, t2, ssq, rsq = bufs
    W = H * 128
    xqf = xq[:, 0:W]
    v3 = lambda a: a.rearrange("p (h d) -> p h d", h=H)
    pcs = []
    pcs.append(lambda: P.copy("act", xqf, src_ps))
    pcs.append(lambda: P.tt("dve", sq[:, 0:W], xqf, xqf, ALU.mult))
    pcs.append(lambda: P.add("dve", lambda e: e.tensor_reduce(ssq[:, 0:H], v3(sq[:, 0:W]), AX.X, ALU.add),
                             reads=[sq[:, 0:W]], writes=[ssq[:, 0:H]]))
    pcs.append(lambda: P.act(rsq[:, 0:H], ssq[:, 0:H], AF.Ln, scale=1.0 / 128, bias=EPS))
    pcs.append(lambda: P.act(rsq[:, 0:H], rsq[:, 0:H], AF.Exp, scale=-0.5))
    pcs.append(lambda: P.tt("dve", v3(xqf), v3(xqf), rsq[:, 0:H].unsqueeze(2).broadcast_to([128, H, 128]), ALU.mult))
    if rope_tile is None:
        pcs.append(lambda: P.tt("dve", v3(out_bf), v3(xqf), w_rep.unsqueeze(1).broadcast_to([128, H, 128]), ALU.mult))
        return pcs
    pcs.append(lambda: P.tt("dve", v3(xqf), v3(xqf), w_rep.unsqueeze(1).broadcast_to([128, H, 128]), ALU.mult))
    cos_t = cs("cos2")[:, rope_tile * 128:(rope_tile + 1) * 128]
    sin_t = cs("sins")[:, rope_tile * 128:(rope_tile + 1) * 128]
    pcs.append(lambda: P.tt("dve", v3(t1[:, 0:W]), v3(xqf), cos_t.unsqueeze(1).broadcast_to([128, H, 128]), ALU.mult))
    v5 = lambda a: a.rearrange("p (h a b f) -> p h a b f", h=H, a=2, b=2)
    sv = sin_t.rearrange("p (a b f) -> p a b f", a=2, b=2)
    for half in range(2):
        pcs.append(lambda half=half: P.tt("pool", v5(t2[:, 0:W])[:, :, :, half, :], v5(xqf)[:, :, :, 1 - half, :],
                                          sv[:, :, half, :].unsqueeze(1).broadcast_to([128, H, 2, 32]), ALU.mult))
    pcs.append(lambda: P.tt("dve", out_bf, t1[:, 0:W], t2[:, 0:W], ALU.add))
    return pcs


def interleave(A, B):
    na, nb = len(A), len(B)
    ia = ib = 0
    while ia < na or ib < nb:
        if ib < nb and (ia >= na or (ib + 1) * na <= ia * nb + nb // 2 * 0 + 0 and False):
            B[ib]()
            ib += 1
        elif ib < nb and (ia >= na or ib * na <= ia * nb):
            B[ib]()
            ib += 1
        else:
            A[ia]()
            ia += 1


def phase_attn(K, P, st, sb, pbs, cs, pr, wb):
    kT = sb("kT", [128, 2, NTA * 128], BF16)
    V = sb("V", [128, NTA, 256], BF16)
    bufsK2 = [(sb("xqk%d" % i, [128, 256], F32), sb("sqk%d" % i, [128, 256], F32), sb("t1k%d" % i, [128, 256], F32),
               sb("t2k%d" % i, [128, 256], F32), sb("ssqk%d" % i, [128, 8], F32), sb("rsqk%d" % i, [128, 8], F32))
              for i in range(2)]
    _bq = (sb("xq", [128, 512], F32), sb("sq", [128, 512], F32), sb("t1", [128, 512], F32),
           sb("t2", [128, 512], F32), sb("ssq", [128, 8], F32), sb("rsq", [128, 8], F32))
    bufsQ2 = [_bq, _bq]
    kbf2 = [sb("kbf%d" % i, [128, 256], BF16) for i in range(2)]
    wkv = wb[0]
    P.dma("pool", wkv[:], wview(K.w_in, 0, 512))
    wq = [wb[1], wb[2]]
    for i in range(2):
        P.dma("pool", wq[i][:], wview(K.w_in, C_AQ + i * 512, C_AQ + (i + 1) * 512))
    qT2 = [sb("qT%d" % i, [128, 8, 512], BF16) for i in range(2)]
    _qbf = sb("qbf", [128, 512], BF16)
    qbf2 = [_qbf, _qbf]
    PTs = [sb("PT%d" % i, [128, 512], BF16) for i in range(3)]
    rc = sb("rc", [128, 512], F32)
    scl = 128.0 ** -0.5

    def kv_pieces(par):
        pcs = []
        bufsK = bufsK2[par]
        kbf = kbf2[par]
        for t in range(par, NTA, 2):
            isctx = t < NTC
            pm = pbs[2 + (t % 2)]

            def proj(t=t, isctx=isctx, pm=pm):
                for k in range(8):
                    lhsT = K.hcT[:, k, t * 128:(t + 1) * 128] if isctx else K.hxT[:, k, (t - NTC) * 128:(t - NTC + 1) * 128]
                    P.mm(pm[:, :], lhsT, wkv[:, k, :], start=(k == 0), stop=(k == 7))
                P.copy("act", V[:, t, :], pm[:, 256:512])
            pcs.append(proj)
            pcs += prep_qk(P, K, pm[:, 0:256], 2, pr("knw"), None if isctx else t - NTC, kbf[:], bufsK, cs)

            def trs(t=t, kbf=kbf):
                pT = pbs[4 + (t % 2)][:].bitcast(BF16)
                for g in range(2):
                    P.tr(pT[:, g * 128:(g + 1) * 128], kbf[:, g * 128:(g + 1) * 128], K.identb)
                P.copy("dve", kT[:, :, t * 128:(t + 1) * 128], pT[:, 0:256].rearrange("p (g d) -> p g d", g=2))
            pcs.append(trs)
        return pcs

    def q_pieces_half(qb, half):
        qT = qT2[qb % 2]
        pcs = []
        bufsQ = bufsQ2[half]
        qbf = qbf2[half]
        for tt in range(4):
            t = qb * 4 + tt
            for half in (half,):
                pm = pbs[half]

                def proj(t=t, half=half, pm=pm):
                    for k in range(8):
                        P.mm(pm[:, :], K.hxT[:, k, t * 128:(t + 1) * 128], wq[half][:, k, :], start=(k == 0), stop=(k == 7))
                pcs.append(proj)
                pcs += prep_qk(P, K, pm[:, :], 4, pr("qnw"), t, qbf[:], bufsQ, cs)

                def trs(tt=tt, half=half, pm=pm, qT=qT, qbf=qbf):
                    pT = pm[:].bitcast(BF16)
                    for hh in range(4):
                        P.tr(pT[:, hh * 128:(hh + 1) * 128], qbf[:, hh * 128:(hh + 1) * 128], K.identb)
                    P.copy("act", qT[:, half * 4:(half + 1) * 4, tt * 128:(tt + 1) * 128],
                           pT[:, 0:512].rearrange("p (h d) -> p h d", h=4))
                pcs.append(trs)
        return pcs

    def core_pieces(qb):
        qT = qT2[qb % 2]
        pcs = []
        for h in range(8):
            g = h // 4
            pO = pbs[4 + (h % 2)]
            pR = pbs[6 + (h % 2)]

            def score(j, g=g, h=h):
                P.mm(pbs[2 + (j % 2)][:, :], kT[:, g, j * 128:(j + 1) * 128], qT[:, h, :])
                P.act(PTs[j % 3][:], pbs[2 + (j % 2)][:, :], AF.Exp, scale=scl, bias=-6.0)

            def accum(j, g=g, pO=pO, pR=pR):
                P.mm(pO[:, :], V[:, j, g * 128:(g + 1) * 128], PTs[j % 3][:], start=(j == 0), stop=(j == NTA - 1))
                P.mm(pR[:, :], K.onesb, PTs[j % 3][:], start=(j == 0), stop=(j == NTA - 1))

            pcs.append(lambda score=score: score(0))
            for j in range(NTA):
                def stp(j=j, score=score, accum=accum):
                    if j + 1 < NTA:
                        score(j + 1)
                    accum(j)
                pcs.append(stp)

            def fin(h=h, pO=pO, pR=pR):
                P.add("dve", lambda e, a=rc[:], b=pR[:, :]: e.reciprocal(a, b), reads=[pR[:, :]], writes=[rc[:]])
                P.tt("dve", K.attnT[:, qb, h, :], pO[:, :], rc[:], ALU.mult)
            pcs.append(fin)
        return pcs

    def merge2(A, B):
        out = []
        ia = ib = 0
        while ia < len(A) or ib < len(B):
            if ib < len(B) and (ia >= len(A) or ib * len(A) <= ia * len(B)):
                out.append(B[ib])
                ib += 1
            else:
                out.append(A[ia])
                ia += 1
        return out

    def q_pieces(qb):
        a, b = q_pieces_half(qb, 0), q_pieces_half(qb, 1)
        n = len(a) // 4
        out = []
        for tt in range(4):
            out += a[tt * n:(tt + 1) * n] + b[tt * n:(tt + 1) * n]
        return out

    interleave(merge2(kv_pieces(0), kv_pieces(1)), q_pieces(0))
    for qb in range(4):
        interleave(core_pieces(qb), q_pieces(qb + 1) if qb + 1 < 4 else [])
    tap(P, "kT", kT[:])
    tap(P, "V", V[:])
    tap(P, "attnT", K.attnT)


def make_rowrep(P, K, dst, group, scr, onesf, pbank):
    for k in range(8):
        P.ts("dve", scr[:, k * 128:(k + 1) * 128], K.identf[:], K.modv[:, 0, group, k:k + 1], None, ALU.mult)
    for cb in range(2):
        P.mm(pbank[:, :], onesf[:], scr[:, cb * 512:(cb + 1) * 512])
        P.copy("act", dst[:, cb * 512:(cb + 1) * 512], pbank[:, :])


def phase_merge(K):
    nc = K.nc
    with contextlib.ExitStack() as st:
        sb = lambda name, shape, dt: st.enter_context(nc.sbuf_tensor(name + "_b", shape, dt))
        pbs = [st.enter_context(nc.psum_tensor("pbb%d" % i, [128, 512], F32)) for i in range(8)]
        P = P_new(K, ["pbb%d" % i for i in range(8)])
        wout = sb("wout", [128, 8, 1024], BF16)
        for cb in range(2):
            P.dma("pool", wout[:, :, cb * 512:(cb + 1) * 512], wview(K.w_out, cb * 512, (cb + 1) * 512))
        wm = [[sb("wm%d_%d" % (i, b), [128, 8, 128], BF16) for b in range(2)] for i in range(4)]
        onesf = sb("onesf", [128, 128], F32)
        P.memset("dve", onesf[:], 1.0)
        g1rep = sb("g1rep", [128, D], F32)
        dscr = sb("dscr", [128, D], F32)
        make_rowrep(P, K, g1rep, 2, dscr, onesf, pbs[7])
        yT = sb("yT", [128, 8, 512], BF16)
        s1 = sb("s1", [128, 512], F32)
        s2 = sb("s2", [128, 512], F32)
        u1 = sb("u1", [128, 512], F32)
        u2 = sb("u2", [128, 512], F32)
        xts = [sb("xt%d" % i, [128, D], F32) for i in range(2)]
        gmf = sb("gmf", [128, D], F32)
        xn = sb("xn", [128, D], BF16)
        junk = sb("junk", [128, D], BF16)
        ss = sb("ss", [128, NT], F32)
        rs = sb("rs", [128, NT], F32)
        srcs = ((K.w_in, C_GA), (K.w_in, C_GD), (K.w_pa, 0), (K.w_pd, 0))
        def load_wm(n):
            if n >= 32:
                return
            f_ = n % 8
            for i_, (wsrc, c0) in enumerate(srcs):
                P.dma("pool", wm[i_][n % 2][:], wview(wsrc, c0 + f_ * 128, c0 + (f_ + 1) * 128))

        load_wm(0)
        yT2 = [yT, sb("yTb", [128, 8, 512], BF16)]

        def floop_pieces(tb):
            blk = slice(tb * 512, (tb + 1) * 512)
            yTc = yT2[tb % 2]
            pcs = []
            for f in range(8):
                def pf(f=f):
                    b = (tb * 8 + f) % 2
                    load_wm(tb * 8 + f + 1)
                    rh = (lambda k: K.hxT[:, k, blk], lambda k: K.hxT[:, k, blk],
                          lambda k: K.attnT[:, tb, k, :], lambda k: K.gdnT[:, tb, k, :])
                    for i in range(4):
                        for k in range(8):
                            P.mm(pbs[i][:, :], wm[i][b][:, k, :], rh[i](k), start=(k == 0), stop=(k == 7))
                    P.act(s1[:], pbs[0][:, :], AF.Sigmoid)
                    P.act(s2[:], pbs[1][:, :], AF.Sigmoid)
                    P.tt("dve", u1[:], pbs[2][:, :], s1[:], ALU.mult)
                    P.tt("dve", u2[:], pbs[3][:, :], s2[:], ALU.mult)
                    P.tt("dve", yTc[:, f, :], u1[:], u2[:], ALU.add)
                pcs.append(pf)
            return pcs

        def tail_pieces(tb):
            yTc = yT2[tb % 2]
            pcs = []
            for i in range(4):
                t = tb * 4 + i
                xt = xts[t % 2]

                def p1(t=t, i=i, xt=xt):
                    P.dma("sp", xt[:], K.x[t * 128:(t + 1) * 128, :])
                    for cb in range(2):
                        pm = pbs[4 + cb]
                        for k in range(8):
                            P.mm(pm[:, :], yTc[:, k, i * 128:(i + 1) * 128], wout[:, k, cb * 512:(cb + 1) * 512],
                                 start=(k == 0), stop=(k == 7))
                        P.tt("dve", gmf[:, cb * 512:(cb + 1) * 512], pm[:, :], g1rep[:, cb * 512:(cb + 1) * 512], ALU.mult)
                    P.copy("act", K.gm[:, t, :], gmf[:])
                    P.tt("dve", xt[:], xt[:], K.gm[:, t, :], ALU.add)
                    P.act(junk[:], xt[:], AF.Square, accum_out=ss[:, t:t + 1])
                    rsqrt_act(P, rs[:, t:t + 1], ss[:, t:t + 1], 1.0 / D, EPS)
                    P.ts("dve", xn[:], xt[:], rs[:, t:t + 1], None, ALU.mult)
                pcs.append(p1)

                def p2(t=t):
                    pT = pbs[6 + (t % 2)][:].bitcast(BF16)
                    for k in range(8):
                        P.tr(pT[:, k * 128:(k + 1) * 128], xn[:, k * 128:(k + 1) * 128], K.identb)
                    for k in range(8):
                        P.ts("dve", K.hxT[:, k, t * 128:(t + 1) * 128], pT[:, k * 128:(k + 1) * 128],
                             K.modv[:, 0, 4, k:k + 1], K.modv[:, 0, 3, k:k + 1], ALU.mult, ALU.add)
                pcs.append(p2)
            return pcs

        for pc_ in floop_pieces(0):
            pc_()
        for tb in range(4):
            interleave(floop_pieces(tb + 1) if tb + 1 < 4 else [], tail_pieces(tb)) if tb + 1 < 4 else [pc_() for pc_ in tail_pieces(tb)]
        tap(P, "gm", K.gm)
        tap(P, "h2T", K.hxT[:])
        finish_phase(K, P, st)


def phase_ffn(K):
    nc = K.nc
    with contextlib.ExitStack() as st:
        sb = lambda name, shape, dt: st.enter_context(nc.sbuf_tensor(name + "_c", shape, dt))
        pbs = [st.enter_context(nc.psum_tensor("pbc%d" % i, [128, 512], F32)) for i in range(8)]
        P = P_new(K, ["pbc%d" % i for i in range(8)])
        pr = lambda n: K.prm[:, PRM[n][0]:PRM[n][1]]
        wd = sb("wd", [128, 22, D], BF16)
        wdv = K.w_down.rearrange("(j p) n -> p j n", p=128)
        for j0, j1 in ((0, 6), (6, 12), (12, 17), (17, 22)):
            P.dma("pool", wd[:, j0:j1, :], wdv[:, j0:j1, :])
        onesf = sb("onesf", [128, 128], F32)
        P.memset("dve", onesf[:], 1.0)
        g2rep = sb("g2rep", [128, D], F32)
        dscr = sb("dscr", [128, D], F32)
        make_rowrep(P, K, g2rep, 5, dscr, onesf, pbs[7])
        wu = [[sb("wu%d_%d" % (i, b), [128, 8, 128], BF16) for b in range(2)] for i in range(2)]
        RW = 516
        raws = [[sb("raw%d_%d" % (i, b), [128, RW], F32) for b in range(2)] for i in range(2)]
        accs = [[sb("acc%d_%d" % (i, b), [128, 512], F32) for b in range(2)] for i in range(2)]
        sg = [sb("sg%d" % b, [128, 512], F32) for b in range(2)]
        gvT = sb("gvT", [128, 22, 512], BF16)
        xts = [sb("xt%d" % i, [128, D], F32) for i in range(2)]
        f2 = sb("f2", [128, D], F32)
        ot = [sb("ot%d" % i, [128, D], F32) for i in range(2)]
        junk = sb("junk", [128, D], BF16)
        ss = sb("ss", [128, NT], F32)
        rs = sb("rs", [128, NT], F32)
        outs = []
        nmm = 0

        def load_w(n):
            if n >= 4 * 22:
                return
            j_ = n % 22
            for i_ in range(2):
                c0 = i_ * DFF + j_ * 128
                P.dma("pool", wu[i_][n % 2][:], wview(K.w_up, c0, c0 + 128))

        load_w(0)
        for q in range(4):
            t0 = q * 512
            lo = max(t0 - 2, 0)
            hi = min(t0 + 514, NTOK)
            i0 = lo - (t0 - 2)
            ncol = hi - lo
            for j in range(22):
                b = (q * 22 + j) % 2
                load_w(q * 22 + j + 1)
                for i in range(2):
                    raw = raws[i][b]
                    if q == 0:
                        P.memset("pool", raw[:, 1:2], 0.0)
                    if q == 3:
                        P.memset("pool", raw[:, 514:515], 0.0)
                    for (g0, n) in ((0, 258), (258, ncol - 258)):
                        pm = pbs[nmm % 4]
                        nmm += 1
                        for k in range(8):
                            P.mm(pm[:, 0:n], wu[i][b][:, k, :], K.hxT[:, k, lo + g0:lo + g0 + n],
                                 start=(k == 0), stop=(k == 7))
                        P.copy("act", raw[:, i0 + g0:i0 + g0 + n], pm[:, 0:n])
                    jj = i * 22 + j
                    cw = lambda tap_: K.prm[:, PRM["cffw"][0] + tap_ * 44 + jj:PRM["cffw"][0] + tap_ * 44 + jj + 1]
                    cb_ = K.prm[:, PRM["cffb"][0] + jj:PRM["cffb"][0] + jj + 1]
                    acc = accs[i][b]
                    P.act(acc[:], raw[:, 1:513], AF.Identity, scale=cw(0), bias=cb_)
                    P.stt("dve", acc[:], raw[:, 2:514], cw(1), acc[:], ALU.mult, ALU.add)
                    P.stt("dve", acc[:], raw[:, 3:515], cw(2), acc[:], ALU.mult, ALU.add)
                P.act(sg[b][:], accs[0][b][:], AF.Silu)
                P.tt("dve", gvT[:, j, :], sg[b][:], accs[1][b][:], ALU.mult)
            for i in range(4):
                t = q * 4 + i
                xt = xts[t % 2]
                o = ot[t % 2]
                P.dma("sp", xt[:], K.x[t * 128:(t + 1) * 128, :])
                P.tt("dve", xt[:], xt[:], K.gm[:, t, :], ALU.add)
                for cb in range(2):
                    pm = pbs[4 + cb]
                    for j in range(22):
                        P.mm(pm[:, :], gvT[:, j, i * 128:(i + 1) * 128], wd[:, j, cb * 512:(cb + 1) * 512],
                             start=(j == 0), stop=(j == 21))
                    P.tt("dve", f2[:, cb * 512:(cb + 1) * 512], pm[:, :], g2rep[:, cb * 512:(cb + 1) * 512], ALU.mult)
                P.tt("dve", xt[:], xt[:], f2[:], ALU.add)
                P.act(junk[:], xt[:], AF.Square, accum_out=ss[:, t:t + 1])
                rsqrt_act(P, rs[:, t:t + 1], ss[:, t:t + 1], 1.0 / D, EPS)
                P.stt("dve", o[:], xt[:], rs[:, t:t + 1], pr("fnw"), ALU.mult, ALU.mult)
                outs.append(P.dma("sp", K.out[t * 128:(t + 1) * 128, :], o[:]))
        finish_phase(K, P, st)


_NC_CACHE = {}


def make_in_maps(inputs, cores):
    cst, cbf = host_constants()
    f = lambda a: np.ascontiguousarray(np.asarray(a, np.float32))
    shared = {
        "w_mod": f(inputs["w_mod"][0]), "w_in": f(inputs["w_in"][0]), "w_pa": f(inputs["w_pa"][0]),
        "w_pd": f(inputs["w_pd"][0]), "w_out": f(inputs["w_out"][0]), "w_up": f(inputs["w_up"][0]),
        "w_down": f(inputs["w_down"][0]), "cst": cst, "cbf": cbf,
    }
    maps = []
    for b in cores:
        m = dict(shared)
        m["x"] = f(inputs["x"][b])
        m["ctx"] = f(inputs["ctx"][b])
        m["prm"] = host_params(b, np.asarray(inputs["c"]), np.asarray(inputs["c_ctx"]), np.asarray(inputs["b_mod"]),
                               np.asarray(inputs["conv_qkv_w"]), np.asarray(inputs["ffn_conv_w"]),
                               np.asarray(inputs["ffn_conv_b"]), np.asarray(inputs["gdn_norm_w"]),
                               np.asarray(inputs["q_norm_w"]), np.asarray(inputs["k_norm_w"]),
                               np.asarray(inputs["final_norm_w"]), np.asarray(inputs["a_log"]),
                               np.asarray(inputs["dt_bias"]))
        maps.append(m)
    return maps


def kernel(**inputs):
    if "nc" not in _NC_CACHE:
        _NC_CACHE["nc"] = build_nc()
    nc = _NC_CACHE["nc"]
    maps = make_in_maps(inputs, list(range(8)))
    res = run_bass_kernel_spmd(nc, maps, core_ids=list(range(8)))
    return np.stack([np.asarray(r["out"], np.float32) for r in res.results], axis=0)


def phase_gdn(K):
    nc = K.nc
    with contextlib.ExitStack() as st:
        sb = lambda name, shape, dt: st.enter_context(nc.sbuf_tensor(name + "_g", shape, dt))
        pbs = [st.enter_context(nc.psum_tensor("pbg%d" % i, [128, 512], F32)) for i in range(8)]
        P = P_new(K, ["pbg%d" % i for i in range(8)])
        pr = lambda n: K.prm[:, PRM[n][0]:PRM[n][1]]
        cst = sb("cstg", [128, 768], F32)
        P.dma("sp", cst[:], K.cst_d[:, 0:768])
        cs = lambda n: cst[:, CST[n][0]:CST[n][1]]
        identf = cs("ident")
        onesf = cs("ones")
        mstrict = (cs("ustrict"), cs("lstrict"))
        mincl = (cs("uincl"), cs("lincl"))
        twoI = sb("twoI", [128, 128], F32)
        P.ts("dve", twoI[:], identf, 2.0, None, ALU.mult)

        def hT(k, t):
            return K.hcT[:, k, t * 128:(t + 1) * 128] if t < NTC else K.hxT[:, k, (t - NTC) * 128:(t - NTC + 1) * 128]

        wdd = sb("wdd", [128, 8, 32], BF16)
        P.dma("pool", wdd[:], wview(K.w_in, C_DB, C_DB + 32))
        dbda = sb("dbda", [128, NTA, 32], F32)
        for t in range(NTA):
            pm = pbs[t % 2]
            for k in range(8):
                P.mm(pm[:, 0:32], hT(k, t), wdd[:, k, :], start=(k == 0), stop=(k == 7))
            P.copy("act", dbda[:, t, :], pm[:, 0:32])
        beta = sb("beta", [128, NTA, 16], F32)
        la = sb("la", [128, NTA, 16], F32)
        gam = sb("gam", [128, NTA, 16], F32)
        negeg = sb("negeg", [128, NTA, 16], F32)
        kdsc = sb("kdsc", [128, NTA, 16], F32)
        gl = sb("gl", [128, NTA, 16], F32)
        nexpA = sb("nexpA", [128, 16], F32)
        P.act(beta[:], dbda[:, :, 0:16], AF.Sigmoid)
        P.act(nexpA[:], pr("alog"), AF.Exp)
        P.ts("dve", nexpA[:], nexpA[:], -1.0, None, ALU.mult)
        P.tt("dve", la[:], dbda[:, :, 16:32], pr("dtb").unsqueeze(1).broadcast_to([128, NTA, 16]), ALU.add)
        P.act(la[:], la[:], AF.Exp)
        P.act(la[:], la[:], AF.Ln, bias=1.0)
        P.tt("dve", la[:], la[:], nexpA[:].unsqueeze(1).broadcast_to([128, NTA, 16]), ALU.mult)
        pg, ptot = pbs[2], pbs[3]
        for t in range(NTA):
            P.mm(pg[:, t * 16:t * 16 + 8], mincl[0], la[:, t, 0:8])
            P.mm(pg[:, t * 16 + 8:t * 16 + 16], mincl[1], la[:, t, 8:16])
            P.mm(ptot[:, t * 16:(t + 1) * 16], onesf, la[:, t, :])
        gflat = lambda a: a[:].rearrange("p t c -> p (t c)")
        P.copy("dve", gflat(gam), pg[:, 0:NTA * 16])
        P.act(negeg[:], gam[:], AF.Exp)
        P.ts("dve", gflat(negeg), gflat(negeg), -1.0, None, ALU.mult)
        P.tt("dve", gflat(kdsc), ptot[:, 0:NTA * 16], gflat(gam), ALU.subtract)
        P.act(kdsc[:], kdsc[:], AF.Exp)
        P.act(gflat(gl), ptot[:, 0:NTA * 16], AF.Exp)
        tap(P, "beta", beta[:])
        tap(P, "la", la[:])
        tap(P, "gam", gam[:])
        wh = [sb("wh%d" % i, [128, 8, 128], BF16) for i in range(4)]
        RAWN = 2308
        raw = sb("raw", [128, RAWN], F32)
        acc = sb("acc", [128, RAWN], F32)
        sqb = sb("sqb", [128, 512], BF16)
        rinv = sb("rinv", [128, 512], F32)
        qTn2 = [sb("qTn%d" % i, [128, NTA * 128], BF16) for i in range(2)]
        kTn2 = [sb("kTn%d" % i, [128, NTA * 128], BF16) for i in range(2)]
        vT = sb("vT", [128, NTA * 128], BF16)
        Ktok2 = [sb("Ktok%d" % i, [128, NTA, 128], BF16) for i in range(2)]
        Vtok2 = [sb("Vtok%d" % i, [128, NTA, 128], BF16) for i in range(2)]
        o_acc2 = [sb("o_acc%d" % i, [128, NT, 128], F32) for i in range(2)]
        S = [sb("S%d" % d, [128, 128], F32) for d in range(2)]
        Sb = [sb("Sb%d" % d, [128, 128], BF16) for d in range(2)]
        GS = 3
        NR = 2 * GS
        Zb = [[sb("Zb%d_%d" % (d, i), [128, 128], BF16) for i in range(NR)] for d in range(2)]
        PmT = [[sb("PmT%d_%d" % (d, i), [128, 128], BF16) for i in range(NR)] for d in range(2)]
        qdT = [[sb("qdT%d_%d" % (d, i), [128, 128], BF16) for i in range(NR)] for d in range(2)]
        kd = [[sb("kd%d_%d" % (d, i), [128, 128], BF16) for i in range(NR)] for d in range(2)]
        scrI = [{n: sb("%s%d" % (n, i), [128, 128], F32) for n in ("bA", "bB", "bC", "bD")}
                for i in range(2 * GS)]
        for c_ in scrI:
            c_["bE"] = c_["bA"]
        rt = [sb("rt%d" % d, [128, 128], BF16) for d in range(2)]
        dl = [sb("dl%d" % d, [128, 128], BF16) for d in range(2)]
        yv = sb("yv", [128, 512], F32)
        ysq = yv
        sz = sb("sz", [128, 512], F32)
        ybf = sb("ybf", [128, 512], BF16)
        yss = sb("yss", [128, 4], F32)
        yrs = sb("yrs", [128, 4], F32)
        P.memset("dve", raw[:], 0.0)
        segs = [(1, 0, 256)] + [(259 + b * 512, 256 + b * 512, 512) for b in range(4)]

        def tokcols(c0, n):
            if c0 < 256:
                return lambda k: K.hcT[:, k, c0:c0 + n]
            return lambda k: K.hxT[:, k, c0 - 256:c0 - 256 + n]

        def rawcol(tok):
            return tok + 1 if tok < 256 else tok + 3

        def g1_pieces(hh):
            qT_, kT_, Kt_, Vt_ = qTn2[hh % 2], kTn2[hh % 2], Ktok2[hh % 2], Vtok2[hh % 2]
            pcs = []
            pm = pbs[7]

            def w_():
                for i, cb in enumerate((C_GQ, C_GK, C_GV)):
                    P.dma("pool", wh[i][:], wview(K.w_in, cb + hh * 128, cb + (hh + 1) * 128))
            pcs.append(w_)
            W = RAWN - 2
            for xi, dst in enumerate((qT_, kT_, vT)):
                j = xi * 8 + hh
                cw = lambda tap_, j=j: K.prm[:, PRM["cqkv"][0] + tap_ * 24 + j:PRM["cqkv"][0] + tap_ * 24 + j + 1]
                for (rc0, tc0, n) in segs:
                    def proj(xi=xi, tc0=tc0, n=n, rc0=rc0):
                        rhs = tokcols(tc0, n)
                        for k in range(8):
                            P.mm(pm[:, 0:n], wh[xi][:, k, :], rhs(k), start=(k == 0), stop=(k == 7))
                        P.copy("act", raw[:, rc0:rc0 + n], pm[:, 0:n])
                    pcs.append(proj)
                NCH = 6
                bnds = [1 + (W * i) // NCH for i in range(NCH + 1)]
                for ci in range(NCH):
                    ca, cb_ = bnds[ci], bnds[ci + 1]
                    pcs.append(lambda ca=ca, cb_=cb_, cw=cw: P.act(acc[:, ca:cb_], raw[:, ca - 1:cb_ - 1], AF.Copy, scale=cw(0)))
                    pcs.append(lambda ca=ca, cb_=cb_, cw=cw: P.stt("dve", acc[:, ca:cb_], raw[:, ca:cb_], cw(1), acc[:, ca:cb_], ALU.mult, ALU.add))
                    pcs.append(lambda ca=ca, cb_=cb_, cw=cw: P.stt("dve", acc[:, ca:cb_], raw[:, ca + 1:cb_ + 1], cw(2), acc[:, ca:cb_], ALU.mult, ALU.add))
                if xi == 2:
                    pcs.append(lambda: P.act(vT[:, 0:256], acc[:, 1:257], AF.Silu))
                    for c0 in range(0, 2048, 512):
                        pcs.append(lambda c0=c0: P.act(vT[:, 256 + c0:256 + c0 + 512], acc[:, 259 + c0:259 + c0 + 512], AF.Silu))
                    continue
                for ci in range(NCH):
                    ca, cb_ = bnds[ci], bnds[ci + 1]
                    pcs.append(lambda ca=ca, cb_=cb_: P.act(acc[:, ca:cb_], acc[:, ca:cb_], AF.Silu))
                for (rc0, tc0, n) in segs:
                    pcs.append(lambda rc0=rc0, n=n: P.tt("dve", sqb[:, 0:n], acc[:, rc0:rc0 + n], acc[:, rc0:rc0 + n], ALU.mult))

                    def nrm1(n=n):
                        P.mm(pm[:, 0:n], K.onesb, sqb[:, 0:n])
                        P.act(rinv[:, 0:n], pm[:, 0:n], AF.Ln, bias=EPS)
                    pcs.append(nrm1)
                    pcs.append(lambda n=n, xi=xi: P.act(rinv[:, 0:n], rinv[:, 0:n], AF.Exp, scale=-0.5,
                                                        bias=(-0.5 * float(np.log(128.0)) if xi == 0 else 0.0)))
                    pcs.append(lambda dst=dst, rc0=rc0, tc0=tc0, n=n: P.tt("dve", dst[:, tc0:tc0 + n], acc[:, rc0:rc0 + n], rinv[:, 0:n], ALU.mult))
            for src, dstk in ((kT_, Kt_), (vT, Vt_)):
                for g0 in range(0, NTA, 8):
                    def trp(src=src, dstk=dstk, g0=g0):
                        ng = min(8, NTA - g0)
                        pT = pm[:].bitcast(BF16)
                        for i in range(ng):
                            P.tr(pT[:, i * 128:(i + 1) * 128], src[:, (g0 + i) * 128:(g0 + i + 1) * 128], K.identb)
                        P.copy("act", dstk[:, g0:g0 + ng, :], pT[:, 0:ng * 128].rearrange("p (i d) -> p i d", i=ng))
                    pcs.append(trp)
            return pcs

        def tail_pieces(hh):
            oa = o_acc2[hh % 2]
            pz = pbs[7]
            pcs = [lambda: P.dma("pool", wh[3][:], wview(K.w_in, C_Z + hh * 128, C_Z + (hh + 1) * 128))]
            v3 = lambda a: a.rearrange("p (i d) -> p i d", i=4)
            for tb in range(4):
                ov = oa[:, tb * 4:(tb + 1) * 4, :]
                def zp(tb=tb):
                    for i in range(4):
                        t = tb * 4 + i
                        for k in range(8):
                            P.mm(pz[:, i * 128:(i + 1) * 128], K.hxT[:, k, t * 128:(t + 1) * 128], wh[3][:, k, :],
                                 start=(k == 0), stop=(k == 7))
                    P.act(sz[:], pz[:, :], AF.Silu)
                pcs.append(zp)
                pcs.append(lambda ov=ov: P.tt("dve", v3(ysq[:]), ov, ov, ALU.mult))
                pcs.append(lambda: P.add("dve", lambda e, a=yss[:], b=v3(ysq[:]): e.tensor_reduce(a, b, AX.X, ALU.add),
                                         reads=[ysq[:]], writes=[yss[:]]))
                pcs.append(lambda: P.act(yrs[:], yss[:], AF.Ln, scale=1.0 / 128, bias=EPS))
                pcs.append(lambda: P.act(yrs[:], yrs[:], AF.Exp, scale=-0.5))
                pcs.append(lambda ov=ov: P.tt("dve", v3(yv[:]), ov, yrs[:].unsqueeze(2).broadcast_to([128, 4, 128]), ALU.mult))
                pcs.append(lambda: P.tt("dve", v3(yv[:]), v3(yv[:]), pr("gnw").unsqueeze(1).broadcast_to([128, 4, 128]), ALU.mult))
                pcs.append(lambda: P.tt("dve", ybf[:], yv[:], sz[:], ALU.mult))

                def trs(tb=tb):
                    pT = pz[:].bitcast(BF16)
                    for i in range(4):
                        P.tr(pT[:, i * 128:(i + 1) * 128], ybf[:, i * 128:(i + 1) * 128], K.identb)
                    P.copy("act", K.gdnT[:, tb, hh, :], pT[:, 0:512])
                pcs.append(trs)
            return pcs

        NH = 8 if GDN_HEADS_LIMIT is None else GDN_HEADS_LIMIT
        for pc_ in g1_pieces(0):
            pc_()
        for h in range(NH):
            qTn, kTn, Ktok, Vtok = qTn2[h % 2], kTn2[h % 2], Ktok2[h % 2], Vtok2[h % 2]
            o_acc = o_acc2[h % 2]
            if h == 0:
                tap(P, "qTn0", qTn[:])
                tap(P, "kTn0", kTn[:])
                tap(P, "vT0", vT[:])
            P.memset("dve", o_acc[:], 0.0)
            for d in range(2):
                P.memset("dve", S[d][:], 0.0)
                P.memset("dve", Sb[d][:], 0.0)
            order = [list(range(NTA)), [1, 0] + list(range(NTA - 1, NTC - 1, -1))]

            def prep_stages(g):
                insts = []
                for s_ in range(g * GS, min((g + 1) * GS, NTA)):
                    for d in range(2):
                        insts.append((s_, d, order[d][s_], (s_ - g * GS) * 2 + d))
                stages = []

                def each(fn):
                    def run_():
                        for (s_, d, t, ii) in insts:
                            fn(s_, d, t, ii, scrI[ii], pbs[ii], s_ % NR, d * 8 + h, t >= NTC)
                    stages.append(run_)

                def st1(s_, d, t, ii, c, pb, slot, r, lat):
                    tc = slice(t * 128, (t + 1) * 128)
                    P.act(c["bA"][:], identf, AF.Copy, scale=gam[:, t, r:r + 1])
                    P.mm(pb[:, 0:128], kTn[:, tc], kTn[:, tc])
                    if lat:
                        P.mm(pb[:, 128:256], kTn[:, tc], qTn[:, tc])
                    P.mm(pb[:, 256:384], onesf, c["bA"][:])
                each(st1)

                def st3(s_, d, t, ii, c, pb, slot, r, lat):
                    P.ts("dve", c["bB"][:], pb[:, 256:384], gam[:, t, r:r + 1], 0.0, ALU.subtract, ALU.min)
                    if lat:
                        P.act(c["bD"][:], pb[:, 256:384], AF.Exp)
                    P.act(c["bB"][:], c["bB"][:], AF.Exp)
                each(st3)

                def st5(s_, d, t, ii, c, pb, slot, r, lat):
                    P.tt("dve", c["bC"][:], pb[:, 0:128], c["bB"][:], ALU.mult)
                    if lat:
                        P.tt("dve", c["bE"][:], pb[:, 128:256], c["bB"][:], ALU.mult)
                each(st5)

                def st6(s_, d, t, ii, c, pb, slot, r, lat):
                    P.stt("dve", c["bC"][:], c["bC"][:], beta[:, t, r:r + 1], mstrict[d], ALU.mult, ALU.mult)
                    if lat:
                        P.tt("pool", PmT[d][slot][:], c["bE"][:], mincl[d], ALU.mult)
                        P.tt("pool", qdT[d][slot][:], qTn[:, t * 128:(t + 1) * 128], c["bD"][:], ALU.mult)
                    P.act(kd[d][slot][:], Ktok[:, t, :], AF.Copy, scale=kdsc[:, t, r:r + 1])
                    P.tr(pb[:, 384:512], c["bC"][:], identf)
                each(st6)

                def st8(s_, d, t, ii, c, pb, slot, r, lat):
                    P.tt("dve", c["bD"][:], pb[:, 384:512], identf, ALU.add)
                    P.tt("pool", c["bB"][:], identf, c["bC"][:], ALU.subtract)
                each(st8)
                for it in range(6):
                    def sA(s_, d, t, ii, c, pb, slot, r, lat):
                        P.mm(pb[:, 0:128], c["bB"][:], c["bD"][:])
                    each(sA)

                    def sB(s_, d, t, ii, c, pb, slot, r, lat):
                        P.tt("dve", c["bA"][:], twoI[:], pb[:, 0:128], ALU.subtract)
                    each(sB)

                    def sC(s_, d, t, ii, c, pb, slot, r, lat):
                        P.mm(pb[:, 128:256], c["bA"][:], c["bB"][:])
                    each(sC)

                    def sD(s_, d, t, ii, c, pb, slot, r, lat, it=it):
                        if it < 5:
                            P.copy("act", c["bB"][:], pb[:, 128:256])
                        else:
                            P.copy("act", Zb[d][slot][:], pb[:, 128:256])
                    each(sD)
                return stages

            def scan_micro(s_):
                ms = []
                info = []
                for d in range(2):
                    t = order[d][s_]
                    info.append((d, t, s_ % NR, d * 8 + h, t >= NTC, pbs[6][:, d * 256:(d + 1) * 256], slice(t * 128, (t + 1) * 128)))

                def m1():
                    for (d, t, slot, r, lat, pc, tc) in info:
                        P.mm(pc[:, 0:128], kTn[:, tc], Sb[d][:])
                def m2():
                    for (d, t, slot, r, lat, pc, tc) in info:
                        P.stt("dve", rt[d][:], pc[:, 0:128], negeg[:, t, r:r + 1], Vtok[:, t, :], ALU.mult, ALU.add)
                def m3():
                    for (d, t, slot, r, lat, pc, tc) in info:
                        P.mm(pc[:, 128:256], Zb[d][slot][:], rt[d][:])
                def m4():
                    for (d, t, slot, r, lat, pc, tc) in info:
                        P.ts("dve", dl[d][:], pc[:, 128:256], beta[:, t, r:r + 1], None, ALU.mult)
                def m5():
                    for (d, t, slot, r, lat, pc, tc) in info:
                        if lat:
                            P.mm(pc[:, 0:128], qdT[d][slot][:], Sb[d][:], start=True, stop=False)
                            P.mm(pc[:, 0:128], PmT[d][slot][:], dl[d][:], start=False, stop=True)
                        P.mm(pc[:, 128:256], kd[d][slot][:], dl[d][:])
                def m6():
                    for (d, t, slot, r, lat, pc, tc) in info:
                        if lat:
                            P.tt("dve", o_acc[:, t - NTC, :], pc[:, 0:128], o_acc[:, t - NTC, :], ALU.add)
                        P.stt("dve", S[d][:], S[d][:], gl[:, t, r:r + 1], pc[:, 128:256], ALU.mult, ALU.add)
                def m7():
                    for (d, t, slot, r, lat, pc, tc) in info:
                        P.copy("act", Sb[d][:], S[d][:])
                return [m1, m2, m3, m4, m5, m6, m7]

            NG = (NTA + GS - 1) // GS
            L = list(prep_stages(0))
            for g in range(NG):
                nxt = prep_stages(g + 1) if g + 1 < NG else []
                mic = []
                for s_ in range(g * GS, min((g + 1) * GS, NTA)):
                    mic += scan_micro(s_)
                na, nb = len(nxt), len(mic)
                ia = ib = 0
                while ia < na or ib < nb:
                    if ib < nb and (ia >= na or ib * na <= ia * nb):
                        L.append(mic[ib])
                        ib += 1
                    else:
                        L.append(nxt[ia])
                        ia += 1
            G1n = g1_pieces(h + 1) if h + 1 < NH else []
            Tl = tail_pieces(h - 1) if h >= 1 else []
            mrg = []
            ia = ib = 0
            while ia < len(G1n) or ib < len(Tl):
                if ib < len(Tl) and (ia >= len(G1n) or ib * len(G1n) <= ia * len(Tl)):
                    mrg.append(Tl[ib])
                    ib += 1
                else:
                    mrg.append(G1n[ia])
                    ia += 1
            G1n = mrg
            na, nb = len(L), len(G1n)
            ia = ib = 0
            while ia < na or ib < nb:
                if ib < nb and (ia >= na or (ib + 1) * na <= ia * nb):
                    G1n[ib]()
                    ib += 1
                else:
                    L[ia]()
                    ia += 1
            if h == 0:
                tap(P, "oacc0", o_acc[:])
        for pc_ in tail_pieces(NH - 1):
            pc_()
        tap(P, "gdnT", K.gdnT)
        finish_phase(K, P, st)


GDN_HEADS_LIMIT = None
```

```python
import numpy as np
import concourse.bass as bass
import concourse.mybir as mybir

ENG_NAMES = ("pe", "act", "dve", "pool", "sp")
_SEM_CTR = [0]


def _box(ap):
    t = ap.tensor
    shp = list(t.shape)
    psz = 1
    for s in shp[1:]:
        psz *= int(s)
    off = int(ap.offset)
    dims = ap.ap
    st0, n0 = dims[0]
    if st0 != 0:
        psz = st0
    p0 = off // psz
    f0 = off % psz
    if st0 == 0:
        np_ = 1
    else:
        np_ = n0
    ext = 0
    for st, n in dims[1:]:
        ext += abs(st) * (n - 1)
    return (ap.name, p0, p0 + np_, f0, f0 + ext + 1)


def _ovl(a, b):
    return a[0] == b[0] and a[1] < b[2] and b[1] < a[2] and a[3] < b[4] and b[3] < a[4]


def _cov(a, b):
    return a[0] == b[0] and a[1] <= b[1] and b[2] <= a[2] and a[3] <= b[3] and b[4] <= a[4]


class Op:
    __slots__ = ("eng", "fn", "deps", "is_dma", "sig", "idx", "waits", "pos", "key")

    def __init__(self, eng, fn, is_dma):
        self.eng = eng
        self.fn = fn
        self.deps = set()
        self.is_dma = is_dma
        self.sig = None
        self.waits = []


class Prog:
    def __init__(self, nc):
        self.nc = nc
        self.ops = []
        self.recs = {}
        self.psum_names = set()

    def whole(self, name):
        self.psum_names.add(name)

    def _access(self, ap, op, is_write):
        if ap is None:
            return
        if isinstance(ap, tuple):
            box = ap
        else:
            if str(ap.space) not in ("SB", "PSUM"):
                return
            box = _box(ap)
        is_ps = box[0] in self.psum_names
        if is_ps:
            box = (box[0], 0, 128, 0, 1 << 30)
        recs = self.recs.setdefault(box[0], [])
        new = []
        for r in recs:
            rb, rop, rw = r
            if _ovl(rb, box):
                if (is_write or rw or (is_ps and rop.eng != op.eng)) and rop is not op:
                    op.deps.add(rop)
                if is_write and _cov(box, rb):
                    continue
                if (not is_write) and (not rw) and rb == box and rop.eng == op.eng and not rop.is_dma:
                    continue
            new.append(r)
        new.append([box, op, is_write])
        self.recs[box[0]] = new

    def add(self, eng, fn, reads=(), writes=(), is_dma=False):
        op = Op(eng, fn, is_dma)
        op.idx = len(self.ops)
        for ap in reads:
            self._access(ap, op, False)
        for ap in writes:
            self._access(ap, op, True)
        self.ops.append(op)
        return op

    def mm(self, out, lhsT, rhs, start=True, stop=True, **kw):
        return self.add("pe", lambda e: e.matmul(out, lhsT, rhs, start=start, stop=stop, **kw),
                        reads=[lhsT, rhs], writes=[out])

    def tr(self, out, in_, ident):
        return self.add("pe", lambda e: e.transpose(out, in_, ident), reads=[in_, ident], writes=[out])

    def act(self, out, in_, func, bias=None, scale=None, accum_out=None, eng="act"):
        kw = {}
        rd = [in_]
        if bias is not None:
            kw["bias"] = bias
            if not isinstance(bias, (int, float)):
                rd.append(bias)
        if scale is not None:
            kw["scale"] = scale
            if not isinstance(scale, (int, float)):
                rd.append(scale)
        wr = [out]
        if accum_out is not None:
            kw["accum_out"] = accum_out
            wr.append(accum_out)
        return self.add(eng, lambda e: e.activation(out, in_, func, **kw), reads=rd, writes=wr)

    def tt(self, eng, out, in0, in1, op):
        return self.add(eng, lambda e: e.tensor_tensor(out, in0, in1, op), reads=[in0, in1], writes=[out])

    def ts(self, eng, out, in0, s1, s2, op0, op1=None, accum_out=None):
        rd = [in0]
        if not isinstance(s1, (int, float)):
            rd.append(s1)
        if s2 is not None and not isinstance(s2, (int, float)):
            rd.append(s2)
        kw = {}
        wr = [out]
        if accum_out is not None:
            kw["accum_out"] = accum_out
            wr.append(accum_out)
        if op1 is None:
            return self.add(eng, lambda e: e.tensor_scalar(out, in0, s1, None, op0, **kw), reads=rd, writes=wr)
        return self.add(eng, lambda e: e.tensor_scalar(out, in0, s1, s2, op0, op1, **kw), reads=rd, writes=wr)

    def stt(self, eng, out, in0, scalar, in1, op0, op1):
        rd = [in0, in1]
        if not isinstance(scalar, (int, float)):
            rd.append(scalar)
        return self.add(eng, lambda e: e.scalar_tensor_tensor(out, in0, scalar, in1, op0, op1), reads=rd, writes=[out])

    def copy(self, eng, out, in_):
        if eng == "act":
            return self.add(eng, lambda e: e.copy(out, in_), reads=[in_], writes=[out])
        return self.add(eng, lambda e: e.tensor_copy(out, in_), reads=[in_], writes=[out])

    def memset(self, eng, out, val):
        return self.add(eng, lambda e: e.memset(out, val), reads=[], writes=[out])

    def dma(self, eng, out, in_, **kw):
        sb = out if str(out.space) in ("SB", "PSUM") else in_
        op = self.add(eng, lambda e: e.dma_start(out=out, in_=in_, **kw), reads=[in_], writes=[out], is_dma=True)
        op.key = _box(sb)
        return op

    def build(self, block, stack, final_waits=()):
        nc = self.nc
        ops = self.ops
        SEM_CAP = 30000
        need = set()
        for op in ops:
            for d in op.deps:
                if d.eng == "pe" and op.eng == "pe" and not d.is_dma:
                    continue
                need.add(d.idx)
        for _, op in final_waits:
            need.add(op.idx)
        nsem = [0]

        def new_sem(name):
            nsem[0] += 1
            _SEM_CTR[0] += 1
            return stack.enter_context(nc.semaphore("%s_%d" % (name, _SEM_CTR[0])))

        cur, cnt = {}, {}
        dsem = {}
        for op in ops:
            if op.idx not in need and not op.is_dma:
                continue
            if op.is_dma:
                if op.key not in dsem:
                    dsem[op.key] = [new_sem("d"), 0]
                ent = dsem[op.key]
                ent[1] += 16
                op.sig = (ent[0], ent[1])
            else:
                e = op.eng
                if e not in cur or cnt[e] >= SEM_CAP:
                    cur[e] = new_sem("s" + e)
                    cnt[e] = 0
                cnt[e] += 1
                op.sig = (cur[e], cnt[e])
        self.n_sems = nsem[0]
        seen = {e: {} for e in ENG_NAMES}
        per_eng = {e: [] for e in ENG_NAMES}
        nwaits = 0
        for op in ops:
            sn = seen[op.eng]
            for d in sorted(op.deps, key=lambda o: o.idx):
                if d.eng == "pe" and op.eng == "pe" and not d.is_dma:
                    continue
                sem, val = d.sig
                k = id(sem)
                if sn.get(k, 0) >= val:
                    continue
                sn[k] = val
                op.waits.append((sem, val))
                nwaits += 1
            per_eng[op.eng].append(op)
        self.n_waits = nwaits
        fw = {e: [] for e in ENG_NAMES}
        for e, op in final_waits:
            fw[e].append(op.sig)
        for e in fw:
            best = {}
            for sem, val in fw[e]:
                if id(sem) not in best or best[id(sem)][1] < val:
                    best[id(sem)] = (sem, val)
            fw[e] = [v for v in best.values() if seen[e].get(id(v[0]), 0) < v[1]]

        def emit(eng_obj, name):
            for op in per_eng[name]:
                for sem, val in op.waits:
                    eng_obj.wait_ge(sem, val)
                ins = op.fn(eng_obj)
                if op.sig is not None:
                    ins.then_inc(op.sig[0], 16 if op.is_dma else 1)
            for sem, val in fw[name]:
                eng_obj.wait_ge(sem, val)

        @block.tensor
        def _(e):
            emit(e, "pe")

        @block.scalar
        def _(e):
            emit(e, "act")

        @block.vector
        def _(e):
            emit(e, "dve")

        @block.gpsimd
        def _(e):
            emit(e, "pool")

        @block.sync
        def _(e):
            emit(e, "sp")

import contextlib
import ml_dtypes
from concourse.bass_utils import run_bass_kernel_spmd

F32 = mybir.dt.float32
BF16 = mybir.dt.bfloat16
AF = mybir.ActivationFunctionType
ALU = mybir.AluOpType
AX = mybir.AxisListType

D = 1024
NTOK = 2048
CL = 256
NT = NTOK // 128
NTC = CL // 128
NTA = NT + NTC
DFF = 2816
EPS = 1e-6
IN_COLS = 7712
C_AK, C_AV, C_GQ, C_GK, C_GV, C_DB, C_DA, C_AQ, C_Z, C_GA, C_GD = 0, 256, 512, 1536, 2560, 3584, 3600, 3616, 4640, 5664, 6688

PRM = {}
_o = 0
for _n, _w in (("bmod", 48), ("cqkv", 72), ("cffw", 132), ("cffb", 44), ("gnw", 128), ("qnw", 128),
               ("knw", 128), ("fnw", 1024), ("alog", 16), ("dtb", 16), ("cvec", 16)):
    PRM[_n] = (_o, _o + _w)
    _o += _w
NP = _o
CST = {}
_o = 0
for _n, _w in (("ident", 128), ("ustrict", 128), ("uincl", 128), ("lstrict", 128), ("lincl", 128),
               ("ones", 128), ("sel", 2048), ("cos2", 2048), ("sins", 2048)):
    CST[_n] = (_o, _o + _w)
    _o += _w
NCST = _o

DEBUG_TAPS = None
STOP_AFTER = None
SKIP_GDN = None


def host_constants():
    cst = np.zeros((128, NCST), np.float32)
    i = np.arange(128)
    cst[:, CST["ident"][0]:CST["ident"][1]] = np.eye(128, dtype=np.float32)
    s, c = i[:, None], i[None, :]
    cst[:, CST["ustrict"][0]:CST["ustrict"][1]] = (c > s)
    cst[:, CST["uincl"][0]:CST["uincl"][1]] = (c >= s)
    cst[:, CST["lstrict"][0]:CST["lstrict"][1]] = (c < s)
    cst[:, CST["lincl"][0]:CST["lincl"][1]] = (c <= s)
    cst[:, CST["ones"][0]:CST["ones"][1]] = 1.0
    sel = np.zeros((128, 16, 128), np.float32)
    for r in range(16):
        sel[r, r, :] = 1.0
    cst[:, CST["sel"][0]:CST["sel"][1]] = sel.reshape(128, 2048)
    t = np.arange(NTOK)
    row = (t // 64).astype(np.float32)
    col = (t % 64).astype(np.float32)
    inv = (np.float32(10000.0) ** (-np.arange(0, 64, 2, dtype=np.float32) / np.float32(64))).astype(np.float32)
    ang = np.stack([row[:, None] * inv[None, :], col[:, None] * inv[None, :]], axis=1).astype(np.float32)
    cos = np.cos(ang).astype(np.float32)
    sin = np.sin(ang).astype(np.float32)
    cos2 = np.stack([cos, cos], axis=2).reshape(NTOK, 128)
    sins = np.stack([-sin, sin], axis=2).reshape(NTOK, 128)
    cst[:, CST["cos2"][0]:CST["cos2"][1]] = cos2.reshape(NT, 128, 128).transpose(1, 0, 2).reshape(128, 2048)
    cst[:, CST["sins"][0]:CST["sins"][1]] = sins.reshape(NT, 128, 128).transpose(1, 0, 2).reshape(128, 2048)
    cbf = np.zeros((128, 256), np.float32)
    cbf[:, 0:128] = np.eye(128)
    cbf[:, 128:256] = 1.0
    return cst, cbf.astype(ml_dtypes.bfloat16)


def host_params(b, c, c_ctx, b_mod, conv_qkv_w, ffn_conv_w, ffn_conv_b, gdn_norm_w, q_norm_w, k_norm_w,
                final_norm_w, a_log, dt_bias):
    prm = np.zeros((128, NP), np.float32)

    def put(name, arr):
        a, e = PRM[name]
        prm[:, a:e] = np.asarray(arr, np.float32).reshape(128, e - a)

    put("bmod", b_mod[0].reshape(48, 128).T)
    put("cqkv", conv_qkv_w[0].reshape(3, 24, 128).transpose(2, 0, 1))
    put("cffw", ffn_conv_w[0].reshape(3, 44, 128).transpose(2, 0, 1))
    put("cffb", ffn_conv_b[0].reshape(44, 128).T)
    put("gnw", np.broadcast_to(gdn_norm_w[0][None, :], (128, 128)))
    put("qnw", np.broadcast_to(q_norm_w[0][None, :], (128, 128)))
    put("knw", np.broadcast_to(k_norm_w[0][None, :], (128, 128)))
    put("fnw", np.broadcast_to(final_norm_w[None, :], (128, 1024)))
    put("alog", np.broadcast_to(a_log[0].reshape(1, 16), (128, 16)))
    put("dtb", np.broadcast_to(dt_bias[0].reshape(1, 16), (128, 16)))
    cv = np.stack([c[b].reshape(8, 128).T, c_ctx.reshape(8, 128).T], axis=2)
    put("cvec", cv)
    return prm


class Ctx:
    pass


def build_nc():
    nc = bass.Bass("TRN2", target_bir_lowering=False)
    K = Ctx()
    K.nc = nc
    dt_in = lambda name, shape, dt=F32: nc.dram_tensor(name, shape, dt, kind="ExternalInput").ap()
    K.x = dt_in("x", [NTOK, D])
    K.ctx = dt_in("ctx", [CL, D])
    K.w_mod = dt_in("w_mod", [D, 6 * D])
    K.w_in = dt_in("w_in", [D, IN_COLS])
    K.w_pa = dt_in("w_pa", [D, D])
    K.w_pd = dt_in("w_pd", [D, D])
    K.w_out = dt_in("w_out", [D, D])
    K.w_up = dt_in("w_up", [D, 2 * DFF])
    K.w_down = dt_in("w_down", [DFF, D])
    K.prm_d = dt_in("prm", [128, NP])
    K.cst_d = dt_in("cst", [128, NCST])
    K.cbf_d = dt_in("cbf", [128, 256], BF16)
    K.out = nc.dram_tensor("out", [NTOK, D], F32, kind="ExternalOutput").ap()
    K.taps = {}
    K.sem_stack = contextlib.ExitStack()
    with K.sem_stack, contextlib.ExitStack() as top:
        K.top = top
        sb = lambda name, shape, dt: top.enter_context(nc.sbuf_tensor(name, shape, dt))
        K.prm = sb("prm_sb", [128, NP], F32)
        K.identf = sb("identf", [128, 128], F32)
        K.cbf = sb("cbf_sb", [128, 256], BF16)
        K.identb = K.cbf[:, 0:128]
        K.onesb = K.cbf[:, 128:256]
        K.modv = sb("modv", [128, 2, 6, 8], F32)
        K.hxT = sb("hxT", [128, 8, NTOK], BF16)
        K.GG = sb("GG", [128, 16384], BF16)
        K.gdnT = K.GG[:].rearrange("p (b h t) -> p b h t", b=4, h=8)
        K.gm = K.GG[:].rearrange("p (i f) -> p i f", i=16)
        K.hcT = sb("hcT", [128, 8, CL], BF16)
        phase_mixer(K)
        if STOP_AFTER in ("p0", "p1"):
            return nc
        if SKIP_GDN is None:
            phase_gdn(K)
        if STOP_AFTER == "a2":
            return nc
        with contextlib.ExitStack() as mid:
            K.AG = mid.enter_context(nc.sbuf_tensor("AG", [128, 16384], BF16))
            K.attnT = K.AG[:].rearrange("p (b h t) -> p b h t", b=4, h=8)
            phase_attn_outer(K)
            if STOP_AFTER == "a1":
                return nc
            phase_merge(K)
        if STOP_AFTER == "b":
            return nc
        phase_ffn(K)
    return nc


def P_new(K, psum_names):
    P = Prog(K.nc)
    for n in psum_names:
        P.whole(n)
    return P


def finish_phase(K, P, st, extra_final=()):
    nc = K.nc
    finals = list(extra_final)
    for name, ap in getattr(P, "tap_list", []):
        shp = list(ap.shape)
        fl = 1
        for v_ in shp[1:]:
            fl *= v_
        d = nc.dram_tensor("tap_" + name, [shp[0], fl], ap.dtype, kind="ExternalOutput").ap()
        K.taps[name] = (shp, ap.dtype)
        letters = "abcdefg"[:len(shp) - 1]
        src = ap if len(shp) == 2 else ap.rearrange("p %s -> p (%s)" % (" ".join(letters), " ".join(letters)))
        finals.append(P.dma("sp", d, src))
    last = {}
    for op in P.ops:
        if not op.is_dma:
            last[op.eng] = op
    dmas = [op for op in P.ops if op.is_dma]
    fw = []
    for e in ENG_NAMES:
        for e2, op in last.items():
            if e2 != e:
                fw.append((e, op))
        for op in dmas:
            fw.append((e, op))
    block = st.enter_context(nc.Block())
    P.build(block, K.sem_stack, final_waits=fw)
    print("phase ops", len(P.ops), "sems", P.n_sems, "waits", P.n_waits, "sbuf_left", nc.sbuf_bytes_remaining)


def tap(P, name, ap):
    if DEBUG_TAPS is not None and name in DEBUG_TAPS:
        if not hasattr(P, "tap_list"):
            P.tap_list = []
        P.tap_list.append((name, ap))


def wview(w, c0, c1):
    return w.rearrange("(k p) n -> p k n", p=128)[:, :, c0:c1]


def rsqrt_act(P, out, in_, scale, bias):
    P.act(out, in_, AF.Ln, scale=scale, bias=bias)
    P.act(out, out, AF.Exp, scale=-0.5)


def phase_mixer(K):
    nc = K.nc
    with contextlib.ExitStack() as st:
        sb = lambda name, shape, dt: st.enter_context(nc.sbuf_tensor(name + "_m", shape, dt))
        pbs = [st.enter_context(nc.psum_tensor("pbm%d" % i, [128, 512], F32)) for i in range(8)]
        P = P_new(K, ["pbm%d" % i for i in range(8)])
        cst = sb("cst_sb", [128, 128], F32)
        cs = lambda n: cst[:, CST[n][0]:CST[n][1]]
        prm = K.prm
        pr = lambda n: prm[:, PRM[n][0]:PRM[n][1]]
        P.dma("sp", prm[:], K.prm_d[:, :])
        P.dma("sp", cst[:], K.cst_d[:, 0:128])
        P.dma("sp", K.cbf[:], K.cbf_d[:, :])
        P.copy("dve", K.identf[:], cs("ident"))
        NWB = 8
        wb = [sb("wb%d" % i, [128, 8, 512], BF16) for i in range(NWB)]
        for bi in range(NWB):
            P.dma("pool", wb[bi][:], wview(K.w_mod, bi * 512, (bi + 1) * 512))
        xts = [sb("xt%d" % i, [128, D], F32) for i in range(2)]
        xns = [sb("xn%d" % i, [128, D], BF16) for i in range(2)]
        junk = sb("junk", [128, D], BF16)
        ss = sb("ss", [128, NTA], F32)
        rs = sb("rs", [128, NTA], F32)
        for t in range(NTA):
            isctx = t < NTC
            src = K.ctx[t * 128:(t + 1) * 128, :] if isctx else K.x[(t - NTC) * 128:(t - NTC + 1) * 128, :]
            dstT = K.hcT if isctx else K.hxT
            c0 = t * 128 if isctx else (t - NTC) * 128
            xt, xn = xts[t % 2], xns[t % 2]
            pT = pbs[1 + (t % 2)][:].bitcast(BF16)
            P.dma("sp", xt[:], src)
            P.act(junk[:], xt[:], AF.Square, accum_out=ss[:, t:t + 1])
            rsqrt_act(P, rs[:, t:t + 1], ss[:, t:t + 1], 1.0 / D, EPS)
            P.ts("dve", xn[:], xt[:], rs[:, t:t + 1], None, ALU.mult)
            for k in range(8):
                P.tr(pT[:, k * 128:(k + 1) * 128], xn[:, k * 128:(k + 1) * 128], K.identb)
            P.copy("act" if t % 2 else "dve", dstT[:, :, c0:c0 + 128], pT[:, 0:1024].rearrange("p (k c) -> p k c", k=8))
        scb = sb("scb", [128, 16], BF16)
        P.act(scb[:], pr("cvec"), AF.Silu)
        scv = scb[:].rearrange("p (k w) -> p k w", w=2)
        pmod = pbs[0]
        for bi in range(12):
            w = wb[bi % NWB]
            if bi >= NWB:
                P.dma("pool", w[:], wview(K.w_mod, bi * 512, (bi + 1) * 512))
            for jj in range(4):
                j = bi * 4 + jj
                for k in range(8):
                    P.mm(pmod[:, 2 * j:2 * j + 2], w[:, k, jj * 128:(jj + 1) * 128], scv[:, k, :],
                         start=(k == 0), stop=(k == 7))
        pmv = pmod[:, 0:96].rearrange("p (j w) -> p j w", w=2)
        for wch in range(2):
            P.tt("dve", K.modv[:, wch].rearrange("p g k -> p (g k)"), pmv[:, :, wch], pr("bmod"), ALU.add)
        for wch in range(2):
            for g in (1, 4):
                P.ts("dve", K.modv[:, wch, g, :], K.modv[:, wch, g, :], 1.0, None, ALU.add)
        tap(P, "modv", K.modv[:])
        for k in range(8):
            P.ts("dve", K.hcT[:, k, :], K.hcT[:, k, :], K.modv[:, 1, 1, k:k + 1], K.modv[:, 1, 0, k:k + 1], ALU.mult, ALU.add)
        for k in range(8):
            for hf in range(2):
                cs_ = slice(hf * 1024, (hf + 1) * 1024)
                P.ts("dve", K.hxT[:, k, cs_], K.hxT[:, k, cs_], K.modv[:, 0, 1, k:k + 1], K.modv[:, 0, 0, k:k + 1],
                     ALU.mult, ALU.add)
        tap(P, "hxT", K.hxT[:])
        tap(P, "hcT", K.hcT[:])
        finish_phase(K, P, st)


def phase_attn_outer(K):
    nc = K.nc
    with contextlib.ExitStack() as st:
        sb = lambda name, shape, dt: st.enter_context(nc.sbuf_tensor(name + "_a", shape, dt))
        pbs = [st.enter_context(nc.psum_tensor("pba%d" % i, [128, 512], F32)) for i in range(8)]
        P = P_new(K, ["pba%d" % i for i in range(8)])
        cst = sb("cst_sb", [128, 4096], F32)
        P.dma("sp", cst[:], K.cst_d[:, CST["cos2"][0]:CST["sins"][1]])
        off = CST["cos2"][0]
        cs = lambda n: cst[:, CST[n][0] - off:CST[n][1] - off]
        pr = lambda n: K.prm[:, PRM[n][0]:PRM[n][1]]
        wb = [sb("wb%d" % i, [128, 8, 512], BF16) for i in range(3)]
        phase_attn(K, P, st, sb, pbs, cs, pr, wb)
        finish_phase(K, P, st)


def prep_qk(P, K, src_ps, H, w_rep, rope_tile, out_bf, bufs, cs):
    xq, sq, t1, t2, ssq, rsq = bufs
    W = H * 128
    xqf = xq[:, 0:W]
    v3 = lambda a: a.rearrange("p (h d) -> p h d", h=H)
    pcs = []
    pcs.append(lambda: P.copy("act", xqf, src_ps))
    pcs.append(lambda: P.tt("dve", sq[:, 0:W], xqf, xqf, ALU.mult))
    pcs.append(lambda: P.add("dve", lambda e: e.tensor_reduce(ssq[:, 0:H], v3(sq[:, 0:W]), AX.X, ALU.add),
                             reads=[sq[:, 0:W]], writes=[ssq[:, 0:H]]))
    pcs.append(lambda: P.act(rsq[:, 0:H], ssq[:, 0:H], AF.Ln, scale=1.0 / 128, bias=EPS))
    pcs.append(lambda: P.act(rsq[:, 0:H], rsq[:, 0:H], AF.Exp, scale=-0.5))
    pcs.append(lambda: P.tt("dve", v3(xqf), v3(xqf), rsq[:, 0:H].unsqueeze(2).broadcast_to([128, H, 128]), ALU.mult))
    if rope_tile is None:
        pcs.append(lambda: P.tt("dve", v3(out_bf), v3(xqf), w_rep.unsqueeze(1).broadcast_to([128, H, 128]), ALU.mult))
        return pcs
    pcs.append(lambda: P.tt("dve", v3(xqf), v3(xqf), w_rep.unsqueeze(1).broadcast_to([128, H, 128]), ALU.mult))
    cos_t = cs("cos2")[:, rope_tile * 128:(rope_tile + 1) * 128]
    sin_t = cs("sins")[:, rope_tile * 128:(rope_tile + 1) * 128]
    pcs.append(lambda: P.tt("dve", v3(t1[:, 0:W]), v3(xqf), cos_t.unsqueeze(1).broadcast_to([128, H, 128]), ALU.mult))
    v5 = lambda a: a.rearrange("p (h a b f) -> p h a b f", h=H, a=2, b=2)
    sv = sin_t.rearrange("p (a b f) -> p a b f", a=2, b=2)
    for half in range(2):
        pcs.append(lambda half=half: P.tt("pool", v5(t2[:, 0:W])[:, :, :, half, :], v5(xqf)[:, :, :, 1 - half, :],
                                          sv[:, :, half, :].unsqueeze(1).broadcast_to([128, H, 2, 32]), ALU.mult))
    pcs.append(lambda: P.tt("dve", out_bf, t1[:, 0:W], t2[:, 0:W], ALU.add))
    return pcs


def interleave(A, B):
    na, nb = len(A), len(B)
    ia = ib = 0
    while ia < na or ib < nb:
        if ib < nb and (ia >= na or (ib + 1) * na <= ia * nb + nb // 2 * 0 + 0 and False):
            B[ib]()
            ib += 1
        elif ib < nb and (ia >= na or ib * na <= ia * nb):
            B[ib]()
            ib += 1
        else:
            A[ia]()
            ia += 1


def phase_attn(K, P, st, sb, pbs, cs, pr, wb):
    kT = sb("kT", [128, 2, NTA * 128], BF16)
    V = sb("V", [128, NTA, 256], BF16)
    bufsK2 = [(sb("xqk%d" % i, [128, 256], F32), sb("sqk%d" % i, [128, 256], F32), sb("t1k%d" % i, [128, 256], F32),
               sb("t2k%d" % i, [128, 256], F32), sb("ssqk%d" % i, [128, 8], F32), sb("rsqk%d" % i, [128, 8], F32))
              for i in range(2)]
    _bq = (sb("xq", [128, 512], F32), sb("sq", [128, 512], F32), sb("t1", [128, 512], F32),
           sb("t2", [128, 512], F32), sb("ssq", [128, 8], F32), sb("rsq", [128, 8], F32))
    bufsQ2 = [_bq, _bq]
    kbf2 = [sb("kbf%d" % i, [128, 256], BF16) for i in range(2)]
    wkv = wb[0]
    P.dma("pool", wkv[:], wview(K.w_in, 0, 512))
    wq = [wb[1], wb[2]]
    for i in range(2):
        P.dma("pool", wq[i][:], wview(K.w_in, C_AQ + i * 512, C_AQ + (i + 1) * 512))
    qT2 = [sb("qT%d" % i, [128, 8, 512], BF16) for i in range(2)]
    _qbf = sb("qbf", [128, 512], BF16)
    qbf2 = [_qbf, _qbf]
    PTs = [sb("PT%d" % i, [128, 512], BF16) for i in range(3)]
    rc = sb("rc", [128, 512], F32)
    scl = 128.0 ** -0.5

    def kv_pieces(par):
        pcs = []
        bufsK = bufsK2[par]
        kbf = kbf2[par]
        for t in range(par, NTA, 2):
            isctx = t < NTC
            pm = pbs[2 + (t % 2)]

            def proj(t=t, isctx=isctx, pm=pm):
                for k in range(8):
                    lhsT = K.hcT[:, k, t * 128:(t + 1) * 128] if isctx else K.hxT[:, k, (t - NTC) * 128:(t - NTC + 1) * 128]
                    P.mm(pm[:, :], lhsT, wkv[:, k, :], start=(k == 0), stop=(k == 7))
                P.copy("act", V[:, t, :], pm[:, 256:512])
            pcs.append(proj)
            pcs += prep_qk(P, K, pm[:, 0:256], 2, pr("knw"), None if isctx else t - NTC, kbf[:], bufsK, cs)

            def trs(t=t, kbf=kbf):
                pT = pbs[4 + (t % 2)][:].bitcast(BF16)
                for g in range(2):
                    P.tr(pT[:, g * 128:(g + 1) * 128], kbf[:, g * 128:(g + 1) * 128], K.identb)
                P.copy("dve", kT[:, :, t * 128:(t + 1) * 128], pT[:, 0:256].rearrange("p (g d) -> p g d", g=2))
            pcs.append(trs)
        return pcs

    def q_pieces_half(qb, half):
        qT = qT2[qb % 2]
        pcs = []
        bufsQ = bufsQ2[half]
        qbf = qbf2[half]
        for tt in range(4):
            t = qb * 4 + tt
            for half in (half,):
                pm = pbs[half]

                def proj(t=t, half=half, pm=pm):
                    for k in range(8):
                        P.mm(pm[:, :], K.hxT[:, k, t * 128:(t + 1) * 128], wq[half][:, k, :], start=(k == 0), stop=(k == 7))
                pcs.append(proj)
                pcs += prep_qk(P, K, pm[:, :], 4, pr("qnw"), t, qbf[:], bufsQ, cs)

                def trs(tt=tt, half=half, pm=pm, qT=qT, qbf=qbf):
                    pT = pm[:].bitcast(BF16)
                    for hh in range(4):
                        P.tr(pT[:, hh * 128:(hh + 1) * 128], qbf[:, hh * 128:(hh + 1) * 128], K.identb)
                    P.copy("act", qT[:, half * 4:(half + 1) * 4, tt * 128:(tt + 1) * 128],
                           pT[:, 0:512].rearrange("p (h d) -> p h d", h=4))
                pcs.append(trs)
        return pcs

    def core_pieces(qb):
        qT = qT2[qb % 2]
        pcs = []
        for h in range(8):
            g = h // 4
            pO = pbs[4 + (h % 2)]
            pR = pbs[6 + (h % 2)]

            def score(j, g=g, h=h):
                P.mm(pbs[2 + (j % 2)][:, :], kT[:, g, j * 128:(j + 1) * 128], qT[:, h, :])
                P.act(PTs[j % 3][:], pbs[2 + (j % 2)][:, :], AF.Exp, scale=scl, bias=-6.0)

            def accum(j, g=g, pO=pO, pR=pR):
                P.mm(pO[:, :], V[:, j, g * 128:(g + 1) * 128], PTs[j % 3][:], start=(j == 0), stop=(j == NTA - 1))
                P.mm(pR[:, :], K.onesb, PTs[j % 3][:], start=(j == 0), stop=(j == NTA - 1))

            pcs.append(lambda score=score: score(0))
            for j in range(NTA):
                def stp(j=j, score=score, accum=accum):
                    if j + 1 < NTA:
                        score(j + 1)
                    accum(j)
                pcs.append(stp)

            def fin(h=h, pO=pO, pR=pR):
                P.add("dve", lambda e, a=rc[:], b=pR[:, :]: e.reciprocal(a, b), reads=[pR[:, :]], writes=[rc[:]])
                P.tt("dve", K.attnT[:, qb, h, :], pO[:, :], rc[:], ALU.mult)
            pcs.append(fin)
        return pcs

    def merge2(A, B):
        out = []
        ia = ib = 0
        while ia < len(A) or ib < len(B):
            if ib < len(B) and (ia >= len(A) or ib * len(A) <= ia * len(B)):
                out.append(B[ib])
                ib += 1
            else:
                out.append(A[ia])
                ia += 1
        return out

    def q_pieces(qb):
        a, b = q_pieces_half(qb, 0), q_pieces_half(qb, 1)
        n = len(a) // 4
        out = []
        for tt in range(4):
            out += a[tt * n:(tt + 1) * n] + b[tt * n:(tt + 1) * n]
        return out

    interleave(merge2(kv_pieces(0), kv_pieces(1)), q_pieces(0))
    for qb in range(4):
        interleave(core_pieces(qb), q_pieces(qb + 1) if qb + 1 < 4 else [])
    tap(P, "kT", kT[:])
    tap(P, "V", V[:])
    tap(P, "attnT", K.attnT)


def make_rowrep(P, K, dst, group, scr, onesf, pbank):
    for k in range(8):
        P.ts("dve", scr[:, k * 128:(k + 1) * 128], K.identf[:], K.modv[:, 0, group, k:k + 1], None, ALU.mult)
    for cb in range(2):
        P.mm(pbank[:, :], onesf[:], scr[:, cb * 512:(cb + 1) * 512])
        P.copy("act", dst[:, cb * 512:(cb + 1) * 512], pbank[:, :])


def phase_merge(K):
    nc = K.nc
    with contextlib.ExitStack() as st:
        sb = lambda name, shape, dt: st.enter_context(nc.sbuf_tensor(name + "_b", shape, dt))
        pbs = [st.enter_context(nc.psum_tensor("pbb%d" % i, [128, 512], F32)) for i in range(8)]
        P = P_new(K, ["pbb%d" % i for i in range(8)])
        wout = sb("wout", [128, 8, 1024], BF16)
        for cb in range(2):
            P.dma("pool", wout[:, :, cb * 512:(cb + 1) * 512], wview(K.w_out, cb * 512, (cb + 1) * 512))
        wm = [[sb("wm%d_%d" % (i, b), [128, 8, 128], BF16) for b in range(2)] for i in range(4)]
        onesf = sb("onesf", [128, 128], F32)
        P.memset("dve", onesf[:], 1.0)
        g1rep = sb("g1rep", [128, D], F32)
        dscr = sb("dscr", [128, D], F32)
        make_rowrep(P, K, g1rep, 2, dscr, onesf, pbs[7])
        yT = sb("yT", [128, 8, 512], BF16)
        s1 = sb("s1", [128, 512], F32)
        s2 = sb("s2", [128, 512], F32)
        u1 = sb("u1", [128, 512], F32)
        u2 = sb("u2", [128, 512], F32)
        xts = [sb("xt%d" % i, [128, D], F32) for i in range(2)]
        gmf = sb("gmf", [128, D], F32)
        xn = sb("xn", [128, D], BF16)
        junk = sb("junk", [128, D], BF16)
        ss = sb("ss", [128, NT], F32)
        rs = sb("rs", [128, NT], F32)
        srcs = ((K.w_in, C_GA), (K.w_in, C_GD), (K.w_pa, 0), (K.w_pd, 0))
        def load_wm(n):
            if n >= 32:
                return
            f_ = n % 8
            for i_, (wsrc, c0) in enumerate(srcs):
                P.dma("pool", wm[i_][n % 2][:], wview(wsrc, c0 + f_ * 128, c0 + (f_ + 1) * 128))

        load_wm(0)
        yT2 = [yT, sb("yTb", [128, 8, 512], BF16)]

        def floop_pieces(tb):
            blk = slice(tb * 512, (tb + 1) * 512)
            yTc = yT2[tb % 2]
            pcs = []
            for f in range(8):
                def pf(f=f):
                    b = (tb * 8 + f) % 2
                    load_wm(tb * 8 + f + 1)
                    rh = (lambda k: K.hxT[:, k, blk], lambda k: K.hxT[:, k, blk],
                          lambda k: K.attnT[:, tb, k, :], lambda k: K.gdnT[:, tb, k, :])
                    for i in range(4):
                        for k in range(8):
                            P.mm(pbs[i][:, :], wm[i][b][:, k, :], rh[i](k), start=(k == 0), stop=(k == 7))
                    P.act(s1[:], pbs[0][:, :], AF.Sigmoid)
                    P.act(s2[:], pbs[1][:, :], AF.Sigmoid)
                    P.tt("dve", u1[:], pbs[2][:, :], s1[:], ALU.mult)
                    P.tt("dve", u2[:], pbs[3][:, :], s2[:], ALU.mult)
                    P.tt("dve", yTc[:, f, :], u1[:], u2[:], ALU.add)
                pcs.append(pf)
            return pcs

        def tail_pieces(tb):
            yTc = yT2[tb % 2]
            pcs = []
            for i in range(4):
                t = tb * 4 + i
                xt = xts[t % 2]

                def p1(t=t, i=i, xt=xt):
                    P.dma("sp", xt[:], K.x[t * 128:(t + 1) * 128, :])
                    for cb in range(2):
                        pm = pbs[4 + cb]
                        for k in range(8):
                            P.mm(pm[:, :], yTc[:, k, i * 128:(i + 1) * 128], wout[:, k, cb * 512:(cb + 1) * 512],
                                 start=(k == 0), stop=(k == 7))
                        P.tt("dve", gmf[:, cb * 512:(cb + 1) * 512], pm[:, :], g1rep[:, cb * 512:(cb + 1) * 512], ALU.mult)
                    P.copy("act", K.gm[:, t, :], gmf[:])
                    P.tt("dve", xt[:], xt[:], K.gm[:, t, :], ALU.add)
                    P.act(junk[:], xt[:], AF.Square, accum_out=ss[:, t:t + 1])
                    rsqrt_act(P, rs[:, t:t + 1], ss[:, t:t + 1], 1.0 / D, EPS)
                    P.ts("dve", xn[:], xt[:], rs[:, t:t + 1], None, ALU.mult)
                pcs.append(p1)

                def p2(t=t):
                    pT = pbs[6 + (t % 2)][:].bitcast(BF16)
                    for k in range(8):
                        P.tr(pT[:, k * 128:(k + 1) * 128], xn[:, k * 128:(k + 1) * 128], K.identb)
                    for k in range(8):
                        P.ts("dve", K.hxT[:, k, t * 128:(t + 1) * 128], pT[:, k * 128:(k + 1) * 128],
                             K.modv[:, 0, 4, k:k + 1], K.modv[:, 0, 3, k:k + 1], ALU.mult, ALU.add)
                pcs.append(p2)
            return pcs

        for pc_ in floop_pieces(0):
            pc_()
        for tb in range(4):
            interleave(floop_pieces(tb + 1) if tb + 1 < 4 else [], tail_pieces(tb)) if tb + 1 < 4 else [pc_() for pc_ in tail_pieces(tb)]
        tap(P, "gm", K.gm)
        tap(P, "h2T", K.hxT[:])
        finish_phase(K, P, st)


def phase_ffn(K):
    nc = K.nc
    with contextlib.ExitStack() as st:
        sb = lambda name, shape, dt: st.enter_context(nc.sbuf_tensor(name + "_c", shape, dt))
        pbs = [st.enter_context(nc.psum_tensor("pbc%d" % i, [128, 512], F32)) for i in range(8)]
        P = P_new(K, ["pbc%d" % i for i in range(8)])
        pr = lambda n: K.prm[:, PRM[n][0]:PRM[n][1]]
        wd = sb("wd", [128, 22, D], BF16)
        wdv = K.w_down.rearrange("(j p) n -> p j n", p=128)
        for j0, j1 in ((0, 6), (6, 12), (12, 17), (17, 22)):
            P.dma("pool", wd[:, j0:j1, :], wdv[:, j0:j1, :])
        onesf = sb("onesf", [128, 128], F32)
        P.memset("dve", onesf[:], 1.0)
        g2rep = sb("g2rep", [128, D], F32)
        dscr = sb("dscr", [128, D], F32)
        make_rowrep(P, K, g2rep, 5, dscr, onesf, pbs[7])
        wu = [[sb("wu%d_%d" % (i, b), [128, 8, 128], BF16) for b in range(2)] for i in range(2)]
        RW = 516
        raws = [[sb("raw%d_%d" % (i, b), [128, RW], F32) for b in range(2)] for i in range(2)]
        accs = [[sb("acc%d_%d" % (i, b), [128, 512], F32) for b in range(2)] for i in range(2)]
        sg = [sb("sg%d" % b, [128, 512], F32) for b in range(2)]
        gvT = sb("gvT", [128, 22, 512], BF16)
        xts = [sb("xt%d" % i, [128, D], F32) for i in range(2)]
        f2 = sb("f2", [128, D], F32)
        ot = [sb("ot%d" % i, [128, D], F32) for i in range(2)]
        junk = sb("junk", [128, D], BF16)
        ss = sb("ss", [128, NT], F32)
        rs = sb("rs", [128, NT], F32)
        outs = []
        nmm = 0

        def load_w(n):
            if n >= 4 * 22:
                return
            j_ = n % 22
            for i_ in range(2):
                c0 = i_ * DFF + j_ * 128
                P.dma("pool", wu[i_][n % 2][:], wview(K.w_up, c0, c0 + 128))

        load_w(0)
        for q in range(4):
            t0 = q * 512
            lo = max(t0 - 2, 0)
            hi = min(t0 + 514, NTOK)
            i0 = lo - (t0 - 2)
            ncol = hi - lo
            for j in range(22):
                b = (q * 22 + j) % 2
                load_w(q * 22 + j + 1)
                for i in range(2):
                    raw = raws[i][b]
                    if q == 0:
                        P.memset("pool", raw[:, 1:2], 0.0)
                    if q == 3:
                        P.memset("pool", raw[:, 514:515], 0.0)
                    for (g0, n) in ((0, 258), (258, ncol - 258)):
                        pm = pbs[nmm % 4]
                        nmm += 1
                        for k in range(8):
                            P.mm(pm[:, 0:n], wu[i][b][:, k, :], K.hxT[:, k, lo + g0:lo + g0 + n],
                                 start=(k == 0), stop=(k == 7))
                        P.copy("act", raw[:, i0 + g0:i0 + g0 + n], pm[:, 0:n])
                    jj = i * 22 + j
                    cw = lambda tap_: K.prm[:, PRM["cffw"][0] + tap_ * 44 + jj:PRM["cffw"][0] + tap_ * 44 + jj + 1]
                    cb_ = K.prm[:, PRM["cffb"][0] + jj:PRM["cffb"][0] + jj + 1]
                    acc = accs[i][b]
                    P.act(acc[:], raw[:, 1:513], AF.Identity, scale=cw(0), bias=cb_)
                    P.stt("dve", acc[:], raw[:, 2:514], cw(1), acc[:], ALU.mult, ALU.add)
                    P.stt("dve", acc[:], raw[:, 3:515], cw(2), acc[:], ALU.mult, ALU.add)
                P.act(sg[b][:], accs[0][b][:], AF.Silu)
                P.tt("dve", gvT[:, j, :], sg[b][:], accs[1][b][:], ALU.mult)
            for i in range(4):
                t = q * 4 + i
                xt = xts[t % 2]
                o = ot[t % 2]
                P.dma("sp", xt[:], K.x[t * 128:(t + 1) * 128, :])
                P.tt("dve", xt[:], xt[:], K.gm[:, t, :], ALU.add)
                for cb in range(2):
                    pm = pbs[4 + cb]
                    for j in range(22):
                        P.mm(pm[:, :], gvT[:, j, i * 128:(i + 1) * 128], wd[:, j, cb * 512:(cb + 1) * 512],
                             start=(j == 0), stop=(j == 21))
                    P.tt("dve", f2[:, cb * 512:(cb + 1) * 512], pm[:, :], g2rep[:, cb * 512:(cb + 1) * 512], ALU.mult)
                P.tt("dve", xt[:], xt[:], f2[:], ALU.add)
                P.act(junk[:], xt[:], AF.Square, accum_out=ss[:, t:t + 1])
                rsqrt_act(P, rs[:, t:t + 1], ss[:, t:t + 1], 1.0 / D, EPS)
                P.stt("dve", o[:], xt[:], rs[:, t:t + 1], pr("fnw"), ALU.mult, ALU.mult)
                outs.append(P.dma("sp", K.out[t * 128:(t + 1) * 128, :], o[:]))
        finish_phase(K, P, st)


_NC_CACHE = {}


def make_in_maps(inputs, cores):
    cst, cbf = host_constants()
    f = lambda a: np.ascontiguousarray(np.asarray(a, np.float32))
    shared = {
        "w_mod": f(inputs["w_mod"][0]), "w_in": f(inputs["w_in"][0]), "w_pa": f(inputs["w_pa"][0]),
        "w_pd": f(inputs["w_pd"][0]), "w_out": f(inputs["w_out"][0]), "w_up": f(inputs["w_up"][0]),
        "w_down": f(inputs["w_down"][0]), "cst": cst, "cbf": cbf,
    }
    maps = []
    for b in cores:
        m = dict(shared)
        m["x"] = f(inputs["x"][b])
        m["ctx"] = f(inputs["ctx"][b])
        m["prm"] = host_params(b, np.asarray(inputs["c"]), np.asarray(inputs["c_ctx"]), np.asarray(inputs["b_mod"]),
                               np.asarray(inputs["conv_qkv_w"]), np.asarray(inputs["ffn_conv_w"]),
                               np.asarray(inputs["ffn_conv_b"]), np.asarray(inputs["gdn_norm_w"]),
                               np.asarray(inputs["q_norm_w"]), np.asarray(inputs["k_norm_w"]),
                               np.asarray(inputs["final_norm_w"]), np.asarray(inputs["a_log"]),
                               np.asarray(inputs["dt_bias"]))
        maps.append(m)
    return maps


def kernel(**inputs):
    if "nc" not in _NC_CACHE:
        _NC_CACHE["nc"] = build_nc()
    nc = _NC_CACHE["nc"]
    maps = make_in_maps(inputs, list(range(8)))
    res = run_bass_kernel_spmd(nc, maps, core_ids=list(range(8)))
    return np.stack([np.asarray(r["out"], np.float32) for r in res.results], axis=0)


def phase_gdn(K):
    nc = K.nc
    with contextlib.ExitStack() as st:
        sb = lambda name, shape, dt: st.enter_context(nc.sbuf_tensor(name + "_g", shape, dt))
        pbs = [st.enter_context(nc.psum_tensor("pbg%d" % i, [128, 512], F32)) for i in range(8)]
        P = P_new(K, ["pbg%d" % i for i in range(8)])
        pr = lambda n: K.prm[:, PRM[n][0]:PRM[n][1]]
        cst = sb("cstg", [128, 768], F32)
        P.dma("sp", cst[:], K.cst_d[:, 0:768])
        cs = lambda n: cst[:, CST[n][0]:CST[n][1]]
        identf = cs("ident")
        onesf = cs("ones")
        mstrict = (cs("ustrict"), cs("lstrict"))
        mincl = (cs("uincl"), cs("lincl"))
        twoI = sb("twoI", [128, 128], F32)
        P.ts("dve", twoI[:], identf, 2.0, None, ALU.mult)

        def hT(k, t):
            return K.hcT[:, k, t * 128:(t + 1) * 128] if t < NTC else K.hxT[:, k, (t - NTC) * 128:(t - NTC + 1) * 128]

        wdd = sb("wdd", [128, 8, 32], BF16)
        P.dma("pool", wdd[:], wview(K.w_in, C_DB, C_DB + 32))
        dbda = sb("dbda", [128, NTA, 32], F32)
        for t in range(NTA):
            pm = pbs[t % 2]
            for k in range(8):
                P.mm(pm[:, 0:32], hT(k, t), wdd[:, k, :], start=(k == 0), stop=(k == 7))
            P.copy("act", dbda[:, t, :], pm[:, 0:32])
        beta = sb("beta", [128, NTA, 16], F32)
        la = sb("la", [128, NTA, 16], F32)
        gam = sb("gam", [128, NTA, 16], F32)
        negeg = sb("negeg", [128, NTA, 16], F32)
        kdsc = sb("kdsc", [128, NTA, 16], F32)
        gl = sb("gl", [128, NTA, 16], F32)
        nexpA = sb("nexpA", [128, 16], F32)
        P.act(beta[:], dbda[:, :, 0:16], AF.Sigmoid)
        P.act(nexpA[:], pr("alog"), AF.Exp)
        P.ts("dve", nexpA[:], nexpA[:], -1.0, None, ALU.mult)
        P.tt("dve", la[:], dbda[:, :, 16:32], pr("dtb").unsqueeze(1).broadcast_to([128, NTA, 16]), ALU.add)
        P.act(la[:], la[:], AF.Exp)
        P.act(la[:], la[:], AF.Ln, bias=1.0)
        P.tt("dve", la[:], la[:], nexpA[:].unsqueeze(1).broadcast_to([128, NTA, 16]), ALU.mult)
        pg, ptot = pbs[2], pbs[3]
        for t in range(NTA):
            P.mm(pg[:, t * 16:t * 16 + 8], mincl[0], la[:, t, 0:8])
            P.mm(pg[:, t * 16 + 8:t * 16 + 16], mincl[1], la[:, t, 8:16])
            P.mm(ptot[:, t * 16:(t + 1) * 16], onesf, la[:, t, :])
        gflat = lambda a: a[:].rearrange("p t c -> p (t c)")
        P.copy("dve", gflat(gam), pg[:, 0:NTA * 16])
        P.act(negeg[:], gam[:], AF.Exp)
        P.ts("dve", gflat(negeg), gflat(negeg), -1.0, None, ALU.mult)
        P.tt("dve", gflat(kdsc), ptot[:, 0:NTA * 16], gflat(gam), ALU.subtract)
        P.act(kdsc[:], kdsc[:], AF.Exp)
        P.act(gflat(gl), ptot[:, 0:NTA * 16], AF.Exp)
        tap(P, "beta", beta[:])
        tap(P, "la", la[:])
        tap(P, "gam", gam[:])
        wh = [sb("wh%d" % i, [128, 8, 128], BF16) for i in range(4)]
        RAWN = 2308
        raw = sb("raw", [128, RAWN], F32)
        acc = sb("acc", [128, RAWN], F32)
        sqb = sb("sqb", [128, 512], BF16)
        rinv = sb("rinv", [128, 512], F32)
        qTn2 = [sb("qTn%d" % i, [128, NTA * 128], BF16) for i in range(2)]
        kTn2 = [sb("kTn%d" % i, [128, NTA * 128], BF16) for i in range(2)]
        vT = sb("vT", [128, NTA * 128], BF16)
        Ktok2 = [sb("Ktok%d" % i, [128, NTA, 128], BF16) for i in range(2)]
        Vtok2 = [sb("Vtok%d" % i, [128, NTA, 128], BF16) for i in range(2)]
        o_acc2 = [sb("o_acc%d" % i, [128, NT, 128], F32) for i in range(2)]
        S = [sb("S%d" % d, [128, 128], F32) for d in range(2)]
        Sb = [sb("Sb%d" % d, [128, 128], BF16) for d in range(2)]
        GS = 3
        NR = 2 * GS
        Zb = [[sb("Zb%d_%d" % (d, i), [128, 128], BF16) for i in range(NR)] for d in range(2)]
        PmT = [[sb("PmT%d_%d" % (d, i), [128, 128], BF16) for i in range(NR)] for d in range(2)]
        qdT = [[sb("qdT%d_%d" % (d, i), [128, 128], BF16) for i in range(NR)] for d in range(2)]
        kd = [[sb("kd%d_%d" % (d, i), [128, 128], BF16) for i in range(NR)] for d in range(2)]
        scrI = [{n: sb("%s%d" % (n, i), [128, 128], F32) for n in ("bA", "bB", "bC", "bD")}
                for i in range(2 * GS)]
        for c_ in scrI:
            c_["bE"] = c_["bA"]
        rt = [sb("rt%d" % d, [128, 128], BF16) for d in range(2)]
        dl = [sb("dl%d" % d, [128, 128], BF16) for d in range(2)]
        yv = sb("yv", [128, 512], F32)
        ysq = yv
        sz = sb("sz", [128, 512], F32)
        ybf = sb("ybf", [128, 512], BF16)
        yss = sb("yss", [128, 4], F32)
        yrs = sb("yrs", [128, 4], F32)
        P.memset("dve", raw[:], 0.0)
        segs = [(1, 0, 256)] + [(259 + b * 512, 256 + b * 512, 512) for b in range(4)]

        def tokcols(c0, n):
            if c0 < 256:
                return lambda k: K.hcT[:, k, c0:c0 + n]
            return lambda k: K.hxT[:, k, c0 - 256:c0 - 256 + n]

        def rawcol(tok):
            return tok + 1 if tok < 256 else tok + 3

        def g1_pieces(hh):
            qT_, kT_, Kt_, Vt_ = qTn2[hh % 2], kTn2[hh % 2], Ktok2[hh % 2], Vtok2[hh % 2]
            pcs = []
            pm = pbs[7]

            def w_():
                for i, cb in enumerate((C_GQ, C_GK, C_GV)):
                    P.dma("pool", wh[i][:], wview(K.w_in, cb + hh * 128, cb + (hh + 1) * 128))
            pcs.append(w_)
            W = RAWN - 2
            for xi, dst in enumerate((qT_, kT_, vT)):
                j = xi * 8 + hh
                cw = lambda tap_, j=j: K.prm[:, PRM["cqkv"][0] + tap_ * 24 + j:PRM["cqkv"][0] + tap_ * 24 + j + 1]
                for (rc0, tc0, n) in segs:
                    def proj(xi=xi, tc0=tc0, n=n, rc0=rc0):
                        rhs = tokcols(tc0, n)
                        for k in range(8):
                            P.mm(pm[:, 0:n], wh[xi][:, k, :], rhs(k), start=(k == 0), stop=(k == 7))
                        P.copy("act", raw[:, rc0:rc0 + n], pm[:, 0:n])
                    pcs.append(proj)
                NCH = 6
                bnds = [1 + (W * i) // NCH for i in range(NCH + 1)]
                for ci in range(NCH):
                    ca, cb_ = bnds[ci], bnds[ci + 1]
                    pcs.append(lambda ca=ca, cb_=cb_, cw=cw: P.act(acc[:, ca:cb_], raw[:, ca - 1:cb_ - 1], AF.Copy, scale=cw(0)))
                    pcs.append(lambda ca=ca, cb_=cb_, cw=cw: P.stt("dve", acc[:, ca:cb_], raw[:, ca:cb_], cw(1), acc[:, ca:cb_], ALU.mult, ALU.add))
                    pcs.append(lambda ca=ca, cb_=cb_, cw=cw: P.stt("dve", acc[:, ca:cb_], raw[:, ca + 1:cb_ + 1], cw(2), acc[:, ca:cb_], ALU.mult, ALU.add))
                if xi == 2:
                    pcs.append(lambda: P.act(vT[:, 0:256], acc[:, 1:257], AF.Silu))
                    for c0 in range(0, 2048, 512):
                        pcs.append(lambda c0=c0: P.act(vT[:, 256 + c0:256 + c0 + 512], acc[:, 259 + c0:259 + c0 + 512], AF.Silu))
                    continue
                for ci in range(NCH):
                    ca, cb_ = bnds[ci], bnds[ci + 1]
                    pcs.append(lambda ca=ca, cb_=cb_: P.act(acc[:, ca:cb_], acc[:, ca:cb_], AF.Silu))
                for (rc0, tc0, n) in segs:
                    pcs.append(lambda rc0=rc0, n=n: P.tt("dve", sqb[:, 0:n], acc[:, rc0:rc0 + n], acc[:, rc0:rc0 + n], ALU.mult))

                    def nrm1(n=n):
                        P.mm(pm[:, 0:n], K.onesb, sqb[:, 0:n])
                        P.act(rinv[:, 0:n], pm[:, 0:n], AF.Ln, bias=EPS)
                    pcs.append(nrm1)
                    pcs.append(lambda n=n, xi=xi: P.act(rinv[:, 0:n], rinv[:, 0:n], AF.Exp, scale=-0.5,
                                                        bias=(-0.5 * float(np.log(128.0)) if xi == 0 else 0.0)))
                    pcs.append(lambda dst=dst, rc0=rc0, tc0=tc0, n=n: P.tt("dve", dst[:, tc0:tc0 + n], acc[:, rc0:rc0 + n], rinv[:, 0:n], ALU.mult))
            for src, dstk in ((kT_, Kt_), (vT, Vt_)):
                for g0 in range(0, NTA, 8):
                    def trp(src=src, dstk=dstk, g0=g0):
                        ng = min(8, NTA - g0)
                        pT = pm[:].bitcast(BF16)
                        for i in range(ng):
                            P.tr(pT[:, i * 128:(i + 1) * 128], src[:, (g0 + i) * 128:(g0 + i + 1) * 128], K.identb)
                        P.copy("act", dstk[:, g0:g0 + ng, :], pT[:, 0:ng * 128].rearrange("p (i d) -> p i d", i=ng))
                    pcs.append(trp)
            return pcs

        def tail_pieces(hh):
            oa = o_acc2[hh % 2]
            pz = pbs[7]
            pcs = [lambda: P.dma("pool", wh[3][:], wview(K.w_in, C_Z + hh * 128, C_Z + (hh + 1) * 128))]
            v3 = lambda a: a.rearrange("p (i d) -> p i d", i=4)
            for tb in range(4):
                ov = oa[:, tb * 4:(tb + 1) * 4, :]
                def zp(tb=tb):
                    for i in range(4):
                        t = tb * 4 + i
                        for k in range(8):
                            P.mm(pz[:, i * 128:(i + 1) * 128], K.hxT[:, k, t * 128:(t + 1) * 128], wh[3][:, k, :],
                                 start=(k == 0), stop=(k == 7))
                    P.act(sz[:], pz[:, :], AF.Silu)
                pcs.append(zp)
                pcs.append(lambda ov=ov: P.tt("dve", v3(ysq[:]), ov, ov, ALU.mult))
                pcs.append(lambda: P.add("dve", lambda e, a=yss[:], b=v3(ysq[:]): e.tensor_reduce(a, b, AX.X, ALU.add),
                                         reads=[ysq[:]], writes=[yss[:]]))
                pcs.append(lambda: P.act(yrs[:], yss[:], AF.Ln, scale=1.0 / 128, bias=EPS))
                pcs.append(lambda: P.act(yrs[:], yrs[:], AF.Exp, scale=-0.5))
                pcs.append(lambda ov=ov: P.tt("dve", v3(yv[:]), ov, yrs[:].unsqueeze(2).broadcast_to([128, 4, 128]), ALU.mult))
                pcs.append(lambda: P.tt("dve", v3(yv[:]), v3(yv[:]), pr("gnw").unsqueeze(1).broadcast_to([128, 4, 128]), ALU.mult))
                pcs.append(lambda: P.tt("dve", ybf[:], yv[:], sz[:], ALU.mult))

                def trs(tb=tb):
                    pT = pz[:].bitcast(BF16)
                    for i in range(4):
                        P.tr(pT[:, i * 128:(i + 1) * 128], ybf[:, i * 128:(i + 1) * 128], K.identb)
                    P.copy("act", K.gdnT[:, tb, hh, :], pT[:, 0:512])
                pcs.append(trs)
            return pcs

        NH = 8 if GDN_HEADS_LIMIT is None else GDN_HEADS_LIMIT
        for pc_ in g1_pieces(0):
            pc_()
        for h in range(NH):
            qTn, kTn, Ktok, Vtok = qTn2[h % 2], kTn2[h % 2], Ktok2[h % 2], Vtok2[h % 2]
            o_acc = o_acc2[h % 2]
            if h == 0:
                tap(P, "qTn0", qTn[:])
                tap(P, "kTn0", kTn[:])
                tap(P, "vT0", vT[:])
            P.memset("dve", o_acc[:], 0.0)
            for d in range(2):
                P.memset("dve", S[d][:], 0.0)
                P.memset("dve", Sb[d][:], 0.0)
            order = [list(range(NTA)), [1, 0] + list(range(NTA - 1, NTC - 1, -1))]

            def prep_stages(g):
                insts = []
                for s_ in range(g * GS, min((g + 1) * GS, NTA)):
                    for d in range(2):
                        insts.append((s_, d, order[d][s_], (s_ - g * GS) * 2 + d))
                stages = []

                def each(fn):
                    def run_():
                        for (s_, d, t, ii) in insts:
                            fn(s_, d, t, ii, scrI[ii], pbs[ii], s_ % NR, d * 8 + h, t >= NTC)
                    stages.append(run_)

                def st1(s_, d, t, ii, c, pb, slot, r, lat):
                    tc = slice(t * 128, (t + 1) * 128)
                    P.act(c["bA"][:], identf, AF.Copy, scale=gam[:, t, r:r + 1])
                    P.mm(pb[:, 0:128], kTn[:, tc], kTn[:, tc])
                    if lat:
                        P.mm(pb[:, 128:256], kTn[:, tc], qTn[:, tc])
                    P.mm(pb[:, 256:384], onesf, c["bA"][:])
                each(st1)

                def st3(s_, d, t, ii, c, pb, slot, r, lat):
                    P.ts("dve", c["bB"][:], pb[:, 256:384], gam[:, t, r:r + 1], 0.0, ALU.subtract, ALU.min)
                    if lat:
                        P.act(c["bD"][:], pb[:, 256:384], AF.Exp)
                    P.act(c["bB"][:], c["bB"][:], AF.Exp)
                each(st3)

                def st5(s_, d, t, ii, c, pb, slot, r, lat):
                    P.tt("dve", c["bC"][:], pb[:, 0:128], c["bB"][:], ALU.mult)
                    if lat:
                        P.tt("dve", c["bE"][:], pb[:, 128:256], c["bB"][:], ALU.mult)
                each(st5)

                def st6(s_, d, t, ii, c, pb, slot, r, lat):
                    P.stt("dve", c["bC"][:], c["bC"][:], beta[:, t, r:r + 1], mstrict[d], ALU.mult, ALU.mult)
                    if lat:
                        P.tt("pool", PmT[d][slot][:], c["bE"][:], mincl[d], ALU.mult)
                        P.tt("pool", qdT[d][slot][:], qTn[:, t * 128:(t + 1) * 128], c["bD"][:], ALU.mult)
                    P.act(kd[d][slot][:], Ktok[:, t, :], AF.Copy, scale=kdsc[:, t, r:r + 1])
                    P.tr(pb[:, 384:512], c["bC"][:], identf)
                each(st6)

                def st8(s_, d, t, ii, c, pb, slot, r, lat):
                    P.tt("dve", c["bD"][:], pb[:, 384:512], identf, ALU.add)
                    P.tt("pool", c["bB"][:], identf, c["bC"][:], ALU.subtract)
                each(st8)
                for it in range(6):
                    def sA(s_, d, t, ii, c, pb, slot, r, lat):
                        P.mm(pb[:, 0:128], c["bB"][:], c["bD"][:])
                    each(sA)

                    def sB(s_, d, t, ii, c, pb, slot, r, lat):
                        P.tt("dve", c["bA"][:], twoI[:], pb[:, 0:128], ALU.subtract)
                    each(sB)

                    def sC(s_, d, t, ii, c, pb, slot, r, lat):
                        P.mm(pb[:, 128:256], c["bA"][:], c["bB"][:])
                    each(sC)

                    def sD(s_, d, t, ii, c, pb, slot, r, lat, it=it):
                        if it < 5:
                            P.copy("act", c["bB"][:], pb[:, 128:256])
                        else:
                            P.copy("act", Zb[d][slot][:], pb[:, 128:256])
                    each(sD)
                return stages

            def scan_micro(s_):
                ms = []
                info = []
                for d in range(2):
                    t = order[d][s_]
                    info.append((d, t, s_ % NR, d * 8 + h, t >= NTC, pbs[6][:, d * 256:(d + 1) * 256], slice(t * 128, (t + 1) * 128)))

                def m1():
                    for (d, t, slot, r, lat, pc, tc) in info:
                        P.mm(pc[:, 0:128], kTn[:, tc], Sb[d][:])
                def m2():
                    for (d, t, slot, r, lat, pc, tc) in info:
                        P.stt("dve", rt[d][:], pc[:, 0:128], negeg[:, t, r:r + 1], Vtok[:, t, :], ALU.mult, ALU.add)
                def m3():
                    for (d, t, slot, r, lat, pc, tc) in info:
                        P.mm(pc[:, 128:256], Zb[d][slot][:], rt[d][:])
                def m4():
                    for (d, t, slot, r, lat, pc, tc) in info:
                        P.ts("dve", dl[d][:], pc[:, 128:256], beta[:, t, r:r + 1], None, ALU.mult)
                def m5():
                    for (d, t, slot, r, lat, pc, tc) in info:
                        if lat:
                            P.mm(pc[:, 0:128], qdT[d][slot][:], Sb[d][:], start=True, stop=False)
                            P.mm(pc[:, 0:128], PmT[d][slot][:], dl[d][:], start=False, stop=True)
                        P.mm(pc[:, 128:256], kd[d][slot][:], dl[d][:])
                def m6():
                    for (d, t, slot, r, lat, pc, tc) in info:
                        if lat:
                            P.tt("dve", o_acc[:, t - NTC, :], pc[:, 0:128], o_acc[:, t - NTC, :], ALU.add)
                        P.stt("dve", S[d][:], S[d][:], gl[:, t, r:r + 1], pc[:, 128:256], ALU.mult, ALU.add)
                def m7():
                    for (d, t, slot, r, lat, pc, tc) in info:
                        P.copy("act", Sb[d][:], S[d][:])
                return [m1, m2, m3, m4, m5, m6, m7]

            NG = (NTA + GS - 1) // GS
            L = list(prep_stages(0))
            for g in range(NG):
                nxt = prep_stages(g + 1) if g + 1 < NG else []
                mic = []
                for s_ in range(g * GS, min((g + 1) * GS, NTA)):
                    mic += scan_micro(s_)
                na, nb = len(nxt), len(mic)
                ia = ib = 0
                while ia < na or ib < nb:
                    if ib < nb and (ia >= na or ib * na <= ia * nb):
                        L.append(mic[ib])
                        ib += 1
                    else:
                        L.append(nxt[ia])
                        ia += 1
            G1n = g1_pieces(h + 1) if h + 1 < NH else []
            Tl = tail_pieces(h - 1) if h >= 1 else []
            mrg = []
            ia = ib = 0
            while ia < len(G1n) or ib < len(Tl):
                if ib < len(Tl) and (ia >= len(G1n) or ib * len(G1n) <= ia * len(Tl)):
                    mrg.append(Tl[ib])
                    ib += 1
                else:
                    mrg.append(G1n[ia])
                    ia += 1
            G1n = mrg
            na, nb = len(L), len(G1n)
            ia = ib = 0
            while ia < na or ib < nb:
                if ib < nb and (ia >= na or (ib + 1) * na <= ia * nb):
                    G1n[ib]()
                    ib += 1
                else:
                    L[ia]()
                    ia += 1
            if h == 0:
                tap(P, "oacc0", o_acc[:])
        for pc_ in tail_pieces(NH - 1):
            pc_()
        tap(P, "gdnT", K.gdnT)
        finish_phase(K, P, st)


GDN_HEADS_LIMIT = None
```

```python
import numpy as np
import concourse.bass as bass
import concourse.mybir as mybir

ENG_NAMES = ("pe", "act", "dve", "pool", "sp")
_SEM_CTR = [0]


def _box(ap):
    t = ap.tensor
    shp = list(t.shape)
    psz = 1
    for s in shp[1:]:
        psz *= int(s)
    off = int(ap.offset)
    dims = ap.ap
    st0, n0 = dims[0]
    if st0 != 0:
        psz = st0
    p0 = off // psz
    f0 = off % psz
    if st0 == 0:
        np_ = 1
    else:
        np_ = n0
    ext = 0
    for st, n in dims[1:]:
        ext += abs(st) * (n - 1)
    return (ap.name, p0, p0 + np_, f0, f0 + ext + 1)


def _ovl(a, b):
    return a[0] == b[0] and a[1] < b[2] and b[1] < a[2] and a[3] < b[4] and b[3] < a[4]


def _cov(a, b):
    return a[0] == b[0] and a[1] <= b[1] and b[2] <= a[2] and a[3] <= b[3] and b[4] <= a[4]


class Op:
    __slots__ = ("eng", "fn", "deps", "is_dma", "sig", "idx", "waits", "pos", "key")

    def __init__(self, eng, fn, is_dma):
        self.eng = eng
        self.fn = fn
        self.deps = set()
        self.is_dma = is_dma
        self.sig = None
        self.waits = []


class Prog:
    def __init__(self, nc):
        self.nc = nc
        self.ops = []
        self.recs = {}
        self.psum_names = set()

    def whole(self, name):
        self.psum_names.add(name)

    def _access(self, ap, op, is_write):
        if ap is None:
            return
        if isinstance(ap, tuple):
            box = ap
        else:
            if str(ap.space) not in ("SB", "PSUM"):
                return
            box = _box(ap)
        is_ps = box[0] in self.psum_names
        if is_ps:
            box = (box[0], 0, 128, 0, 1 << 30)
        recs = self.recs.setdefault(box[0], [])
        new = []
        for r in recs:
            rb, rop, rw = r
            if _ovl(rb, box):
                if (is_write or rw or (is_ps and rop.eng != op.eng)) and rop is not op:
                    op.deps.add(rop)
                if is_write and _cov(box, rb):
                    continue
                if (not is_write) and (not rw) and rb == box and rop.eng == op.eng and not rop.is_dma:
                    continue
            new.append(r)
        new.append([box, op, is_write])
        self.recs[box[0]] = new

    def add(self, eng, fn, reads=(), writes=(), is_dma=False):
        op = Op(eng, fn, is_dma)
        op.idx = len(self.ops)
        for ap in reads:
            self._access(ap, op, False)
        for ap in writes:
            self._access(ap, op, True)
        self.ops.append(op)
        return op

    def mm(self, out, lhsT, rhs, start=True, stop=True, **kw):
        return self.add("pe", lambda e: e.matmul(out, lhsT, rhs, start=start, stop=stop, **kw),
                        reads=[lhsT, rhs], writes=[out])

    def tr(self, out, in_, ident):
        return self.add("pe", lambda e: e.transpose(out, in_, ident), reads=[in_, ident], writes=[out])

    def act(self, out, in_, func, bias=None, scale=None, accum_out=None, eng="act"):
        kw = {}
        rd = [in_]
        if bias is not None:
            kw["bias"] = bias
            if not isinstance(bias, (int, float)):
                rd.append(bias)
        if scale is not None:
            kw["scale"] = scale
            if not isinstance(scale, (int, float)):
                rd.append(scale)
        wr = [out]
        if accum_out is not None:
            kw["accum_out"] = accum_out
            wr.append(accum_out)
        return self.add(eng, lambda e: e.activation(out, in_, func, **kw), reads=rd, writes=wr)

    def tt(self, eng, out, in0, in1, op):
        return self.add(eng, lambda e: e.tensor_tensor(out, in0, in1, op), reads=[in0, in1], writes=[out])

    def ts(self, eng, out, in0, s1, s2, op0, op1=None, accum_out=None):
        rd = [in0]
        if not isinstance(s1, (int, float)):
            rd.append(s1)
        if s2 is not None and not isinstance(s2, (int, float)):
            rd.append(s2)
        kw = {}
        wr = [out]
        if accum_out is not None:
            kw["accum_out"] = accum_out
            wr.append(accum_out)
        if op1 is None:
            return self.add(eng, lambda e: e.tensor_scalar(out, in0, s1, None, op0, **kw), reads=rd, writes=wr)
        return self.add(eng, lambda e: e.tensor_scalar(out, in0, s1, s2, op0, op1, **kw), reads=rd, writes=wr)

    def stt(self, eng, out, in0, scalar, in1, op0, op1):
        rd = [in0, in1]
        if not isinstance(scalar, (int, float)):
            rd.append(scalar)
        return self.add(eng, lambda e: e.scalar_tensor_tensor(out, in0, scalar, in1, op0, op1), reads=rd, writes=[out])

    def copy(self, eng, out, in_):
        if eng == "act":
            return self.add(eng, lambda e: e.copy(out, in_), reads=[in_], writes=[out])
        return self.add(eng, lambda e: e.tensor_copy(out, in_), reads=[in_], writes=[out])

    def memset(self, eng, out, val):
        return self.add(eng, lambda e: e.memset(out, val), reads=[], writes=[out])

    def dma(self, eng, out, in_, **kw):
        sb = out if str(out.space) in ("SB", "PSUM") else in_
        op = self.add(eng, lambda e: e.dma_start(out=out, in_=in_, **kw), reads=[in_], writes=[out], is_dma=True)
        op.key = _box(sb)
        return op

    def build(self, block, stack, final_waits=()):
        nc = self.nc
        ops = self.ops
        SEM_CAP = 30000
        need = set()
        for op in ops:
            for d in op.deps:
                if d.eng == "pe" and op.eng == "pe" and not d.is_dma:
                    continue
                need.add(d.idx)
        for _, op in final_waits:
            need.add(op.idx)
        nsem = [0]

        def new_sem(name):
            nsem[0] += 1
            _SEM_CTR[0] += 1
            return stack.enter_context(nc.semaphore("%s_%d" % (name, _SEM_CTR[0])))

        cur, cnt = {}, {}
        dsem = {}
        for op in ops:
            if op.idx not in need and not op.is_dma:
                continue
            if op.is_dma:
                if op.key not in dsem:
                    dsem[op.key] = [new_sem("d"), 0]
                ent = dsem[op.key]
                ent[1] += 16
                op.sig = (ent[0], ent[1])
            else:
                e = op.eng
                if e not in cur or cnt[e] >= SEM_CAP:
                    cur[e] = new_sem("s" + e)
                    cnt[e] = 0
                cnt[e] += 1
                op.sig = (cur[e], cnt[e])
        self.n_sems = nsem[0]
        seen = {e: {} for e in ENG_NAMES}
        per_eng = {e: [] for e in ENG_NAMES}
        nwaits = 0
        for op in ops:
            sn = seen[op.eng]
            for d in sorted(op.deps, key=lambda o: o.idx):
                if d.eng == "pe" and op.eng == "pe" and not d.is_dma:
                    continue
                sem, val = d.sig
                k = id(sem)
                if sn.get(k, 0) >= val:
                    continue
                sn[k] = val
                op.waits.append((sem, val))
                nwaits += 1
            per_eng[op.eng].append(op)
        self.n_waits = nwaits
        fw = {e: [] for e in ENG_NAMES}
        for e, op in final_waits:
            fw[e].append(op.sig)
        for e in fw:
            best = {}
            for sem, val in fw[e]:
                if id(sem) not in best or best[id(sem)][1] < val:
                    best[id(sem)] = (sem, val)
            fw[e] = [v for v in best.values() if seen[e].get(id(v[0]), 0) < v[1]]

        def emit(eng_obj, name):
            for op in per_eng[name]:
                for sem, val in op.waits:
                    eng_obj.wait_ge(sem, val)
                ins = op.fn(eng_obj)
                if op.sig is not None:
                    ins.then_inc(op.sig[0], 16 if op.is_dma else 1)
            for sem, val in fw[name]:
                eng_obj.wait_ge(sem, val)

        @block.tensor
        def _(e):
            emit(e, "pe")

        @block.scalar
        def _(e):
            emit(e, "act")

        @block.vector
        def _(e):
            emit(e, "dve")

        @block.gpsimd
        def _(e):
            emit(e, "pool")

        @block.sync
        def _(e):
            emit(e, "sp")

import contextlib
import ml_dtypes
from concourse.bass_utils import run_bass_kernel_spmd

F32 = mybir.dt.float32
BF16 = mybir.dt.bfloat16
AF = mybir.ActivationFunctionType
ALU = mybir.AluOpType
AX = mybir.AxisListType

D = 1024
NTOK = 2048
CL = 256
NT = NTOK // 128
NTC = CL // 128
NTA = NT + NTC
DFF = 2816
EPS = 1e-6
IN_COLS = 7712
C_AK, C_AV, C_GQ, C_GK, C_GV, C_DB, C_DA, C_AQ, C_Z, C_GA, C_GD = 0, 256, 512, 1536, 2560, 3584, 3600, 3616, 4640, 5664, 6688

PRM = {}
_o = 0
for _n, _w in (("bmod", 48), ("cqkv", 72), ("cffw", 132), ("cffb", 44), ("gnw", 128), ("qnw", 128),
               ("knw", 128), ("fnw", 1024), ("alog", 16), ("dtb", 16), ("cvec", 16)):
    PRM[_n] = (_o, _o + _w)
    _o += _w
NP = _o
CST = {}
_o = 0
for _n, _w in (("ident", 128), ("ustrict", 128), ("uincl", 128), ("lstrict", 128), ("lincl", 128),
               ("ones", 128), ("sel", 2048), ("cos2", 2048), ("sins", 2048)):
    CST[_n] = (_o, _o + _w)
    _o += _w
NCST = _o

DEBUG_TAPS = None
STOP_AFTER = None
SKIP_GDN = None


def host_constants():
    cst = np.zeros((128, NCST), np.float32)
    i = np.arange(128)
    cst[:, CST["ident"][0]:CST["ident"][1]] = np.eye(128, dtype=np.float32)
    s, c = i[:, None], i[None, :]
    cst[:, CST["ustrict"][0]:CST["ustrict"][1]] = (c > s)
    cst[:, CST["uincl"][0]:CST["uincl"][1]] = (c >= s)
    cst[:, CST["lstrict"][0]:CST["lstrict"][1]] = (c < s)
    cst[:, CST["lincl"][0]:CST["lincl"][1]] = (c <= s)
    cst[:, CST["ones"][0]:CST["ones"][1]] = 1.0
    sel = np.zeros((128, 16, 128), np.float32)
    for r in range(16):
        sel[r, r, :] = 1.0
    cst[:, CST["sel"][0]:CST["sel"][1]] = sel.reshape(128, 2048)
    t = np.arange(NTOK)
    row = (t // 64).astype(np.float32)
    col = (t % 64).astype(np.float32)
    inv = (np.float32(10000.0) ** (-np.arange(0, 64, 2, dtype=np.float32) / np.float32(64))).astype(np.float32)
    ang = np.stack([row[:, None] * inv[None, :], col[:, None] * inv[None, :]], axis=1).astype(np.float32)
    cos = np.cos(ang).astype(np.float32)
    sin = np.sin(ang).astype(np.float32)
    cos2 = np.stack([cos, cos], axis=2).reshape(NTOK, 128)
    sins = np.stack([-sin, sin], axis=2).reshape(NTOK, 128)
    cst[:, CST["cos2"][0]:CST["cos2"][1]] = cos2.reshape(NT, 128, 128).transpose(1, 0, 2).reshape(128, 2048)
    cst[:, CST["sins"][0]:CST["sins"][1]] = sins.reshape(NT, 128, 128).transpose(1, 0, 2).reshape(128, 2048)
    cbf = np.zeros((128, 256), np.float32)
    cbf[:, 0:128] = np.eye(128)
    cbf[:, 128:256] = 1.0
    return cst, cbf.astype(ml_dtypes.bfloat16)


def host_params(b, c, c_ctx, b_mod, conv_qkv_w, ffn_conv_w, ffn_conv_b, gdn_norm_w, q_norm_w, k_norm_w,
                final_norm_w, a_log, dt_bias):
    prm = np.zeros((128, NP), np.float32)

    def put(name, arr):
        a, e = PRM[name]
        prm[:, a:e] = np.asarray(arr, np.float32).reshape(128, e - a)

    put("bmod", b_mod[0].reshape(48, 128).T)
    put("cqkv", conv_qkv_w[0].reshape(3, 24, 128).transpose(2, 0, 1))
    put("cffw", ffn_conv_w[0].reshape(3, 44, 128).transpose(2, 0, 1))
    put("cffb", ffn_conv_b[0].reshape(44, 128).T)
    put("gnw", np.broadcast_to(gdn_norm_w[0][None, :], (128, 128)))
    put("qnw", np.broadcast_to(q_norm_w[0][None, :], (128, 128)))
    put("knw", np.broadcast_to(k_norm_w[0][None, :], (128, 128)))
    put("fnw", np.broadcast_to(final_norm_w[None, :], (128, 1024)))
    put("alog", np.broadcast_to(a_log[0].reshape(1, 16), (128, 16)))
    put("dtb", np.broadcast_to(dt_bias[0].reshape(1, 16), (128, 16)))
    cv = np.stack([c[b].reshape(8, 128).T, c_ctx.reshape(8, 128).T], axis=2)
    put("cvec", cv)
    return prm


class Ctx:
    pass


def build_nc():
    nc = bass.Bass("TRN2", target_bir_lowering=False)
    K = Ctx()
    K.nc = nc
    dt_in = lambda name, shape, dt=F32: nc.dram_tensor(name, shape, dt, kind="ExternalInput").ap()
    K.x = dt_in("x", [NTOK, D])
    K.ctx = dt_in("ctx", [CL, D])
    K.w_mod = dt_in("w_mod", [D, 6 * D])
    K.w_in = dt_in("w_in", [D, IN_COLS])
    K.w_pa = dt_in("w_pa", [D, D])
    K.w_pd = dt_in("w_pd", [D, D])
    K.w_out = dt_in("w_out", [D, D])
    K.w_up = dt_in("w_up", [D, 2 * DFF])
    K.w_down = dt_in("w_down", [DFF, D])
    K.prm_d = dt_in("prm", [128, NP])
    K.cst_d = dt_in("cst", [128, NCST])
    K.cbf_d = dt_in("cbf", [128, 256], BF16)
    K.out = nc.dram_tensor("out", [NTOK, D], F32, kind="ExternalOutput").ap()
    K.taps = {}
    K.sem_stack = contextlib.ExitStack()
    with K.sem_stack, contextlib.ExitStack() as top:
        K.top = top
        sb = lambda name, shape, dt: top.enter_context(nc.sbuf_tensor(name, shape, dt))
        K.prm = sb("prm_sb", [128, NP], F32)
        K.identf = sb("identf", [128, 128], F32)
        K.cbf = sb("cbf_sb", [128, 256], BF16)
        K.identb = K.cbf[:, 0:128]
        K.onesb = K.cbf[:, 128:256]
        K.modv = sb("modv", [128, 2, 6, 8], F32)
        K.hxT = sb("hxT", [128, 8, NTOK], BF16)
        K.GG = sb("GG", [128, 16384], BF16)
        K.gdnT = K.GG[:].rearrange("p (b h t) -> p b h t", b=4, h=8)
        K.gm = K.GG[:].rearrange("p (i f) -> p i f", i=16)
        K.hcT = sb("hcT", [128, 8, CL], BF16)
        phase_mixer(K)
        if STOP_AFTER in ("p0", "p1"):
            return nc
        if SKIP_GDN is None:
            phase_gdn(K)
        if STOP_AFTER == "a2":
            return nc
        with contextlib.ExitStack() as mid:
            K.AG = mid.enter_context(nc.sbuf_tensor("AG", [128, 16384], BF16))
            K.attnT = K.AG[:].rearrange("p (b h t) -> p b h t", b=4, h=8)
            phase_attn_outer(K)
            if STOP_AFTER == "a1":
                return nc
            phase_merge(K)
        if STOP_AFTER == "b":
            return nc
        phase_ffn(K)
    return nc


def P_new(K, psum_names):
    P = Prog(K.nc)
    for n in psum_names:
        P.whole(n)
    return P


def finish_phase(K, P, st, extra_final=()):
    nc = K.nc
    finals = list(extra_final)
    for name, ap in getattr(P, "tap_list", []):
        shp = list(ap.shape)
        fl = 1
        for v_ in shp[1:]:
            fl *= v_
        d = nc.dram_tensor("tap_" + name, [shp[0], fl], ap.dtype, kind="ExternalOutput").ap()
        K.taps[name] = (shp, ap.dtype)
        letters = "abcdefg"[:len(shp) - 1]
        src = ap if len(shp) == 2 else ap.rearrange("p %s -> p (%s)" % (" ".join(letters), " ".join(letters)))
        finals.append(P.dma("sp", d, src))
    last = {}
    for op in P.ops:
        if not op.is_dma:
            last[op.eng] = op
    dmas = [op for op in P.ops if op.is_dma]
    fw = []
    for e in ENG_NAMES:
        for e2, op in last.items():
            if e2 != e:
                fw.append((e, op))
        for op in dmas:
            fw.append((e, op))
    block = st.enter_context(nc.Block())
    P.build(block, K.sem_stack, final_waits=fw)
    print("phase ops", len(P.ops), "sems", P.n_sems, "waits", P.n_waits, "sbuf_left", nc.sbuf_bytes_remaining)


def tap(P, name, ap):
    if DEBUG_TAPS is not None and name in DEBUG_TAPS:
        if not hasattr(P, "tap_list"):
            P.tap_list = []
        P.tap_list.append((name, ap))


def wview(w, c0, c1):
    return w.rearrange("(k p) n -> p k n", p=128)[:, :, c0:c1]


def rsqrt_act(P, out, in_, scale, bias):
    P.act(out, in_, AF.Ln, scale=scale, bias=bias)
    P.act(out, out, AF.Exp, scale=-0.5)


def phase_mixer(K):
    nc = K.nc
    with contextlib.ExitStack() as st:
        sb = lambda name, shape, dt: st.enter_context(nc.sbuf_tensor(name + "_m", shape, dt))
        pbs = [st.enter_context(nc.psum_tensor("pbm%d" % i, [128, 512], F32)) for i in range(8)]
        P = P_new(K, ["pbm%d" % i for i in range(8)])
        cst = sb("cst_sb", [128, 128], F32)
        cs = lambda n: cst[:, CST[n][0]:CST[n][1]]
        prm = K.prm
        pr = lambda n: prm[:, PRM[n][0]:PRM[n][1]]
        P.dma("sp", prm[:], K.prm_d[:, :])
        P.dma("sp", cst[:], K.cst_d[:, 0:128])
        P.dma("sp", K.cbf[:], K.cbf_d[:, :])
        P.copy("dve", K.identf[:], cs("ident"))
        NWB = 8
        wb = [sb("wb%d" % i, [128, 8, 512], BF16) for i in range(NWB)]
        for bi in range(NWB):
            P.dma("pool", wb[bi][:], wview(K.w_mod, bi * 512, (bi + 1) * 512))
        xts = [sb("xt%d" % i, [128, D], F32) for i in range(2)]
        xns = [sb("xn%d" % i, [128, D], BF16) for i in range(2)]
        junk = sb("junk", [128, D], BF16)
        ss = sb("ss", [128, NTA], F32)
        rs = sb("rs", [128, NTA], F32)
        for t in range(NTA):
            isctx = t < NTC
            src = K.ctx[t * 128:(t + 1) * 128, :] if isctx else K.x[(t - NTC) * 128:(t - NTC + 1) * 128, :]
            dstT = K.hcT if isctx else K.hxT
            c0 = t * 128 if isctx else (t - NTC) * 128
            xt, xn = xts[t % 2], xns[t % 2]
            pT = pbs[1 + (t % 2)][:].bitcast(BF16)
            P.dma("sp", xt[:], src)
            P.act(junk[:], xt[:], AF.Square, accum_out=ss[:, t:t + 1])
            rsqrt_act(P, rs[:, t:t + 1], ss[:, t:t + 1], 1.0 / D, EPS)
            P.ts("dve", xn[:], xt[:], rs[:, t:t + 1], None, ALU.mult)
            for k in range(8):
                P.tr(pT[:, k * 128:(k + 1) * 128], xn[:, k * 128:(k + 1) * 128], K.identb)
            P.copy("act" if t % 2 else "dve", dstT[:, :, c0:c0 + 128], pT[:, 0:1024].rearrange("p (k c) -> p k c", k=8))
        scb = sb("scb", [128, 16], BF16)
        P.act(scb[:], pr("cvec"), AF.Silu)
        scv = scb[:].rearrange("p (k w) -> p k w", w=2)
        pmod = pbs[0]
        for bi in range(12):
            w = wb[bi % NWB]
            if bi >= NWB:
                P.dma("pool", w[:], wview(K.w_mod, bi * 512, (bi + 1) * 512))
            for jj in range(4):
                j = bi * 4 + jj
                for k in range(8):
                    P.mm(pmod[:, 2 * j:2 * j + 2], w[:, k, jj * 128:(jj + 1) * 128], scv[:, k, :],
                         start=(k == 0), stop=(k == 7))
        pmv = pmod[:, 0:96].rearrange("p (j w) -> p j w", w=2)
        for wch in range(2):
            P.tt("dve", K.modv[:, wch].rearrange("p g k -> p (g k)"), pmv[:, :, wch], pr("bmod"), ALU.add)
        for wch in range(2):
            for g in (1, 4):
                P.ts("dve", K.modv[:, wch, g, :], K.modv[:, wch, g, :], 1.0, None, ALU.add)
        tap(P, "modv", K.modv[:])
        for k in range(8):
            P.ts("dve", K.hcT[:, k, :], K.hcT[:, k, :], K.modv[:, 1, 1, k:k + 1], K.modv[:, 1, 0, k:k + 1], ALU.mult, ALU.add)
        for k in range(8):
            for hf in range(2):
                cs_ = slice(hf * 1024, (hf + 1) * 1024)
                P.ts("dve", K.hxT[:, k, cs_], K.hxT[:, k, cs_], K.modv[:, 0, 1, k:k + 1], K.modv[:, 0, 0, k:k + 1],
                     ALU.mult, ALU.add)
        tap(P, "hxT", K.hxT[:])
        tap(P, "hcT", K.hcT[:])
        finish_phase(K, P, st)


def phase_attn_outer(K):
    nc = K.nc
    with contextlib.ExitStack() as st:
        sb = lambda name, shape, dt: st.enter_context(nc.sbuf_tensor(name + "_a", shape, dt))
        pbs = [st.enter_context(nc.psum_tensor("pba%d" % i, [128, 512], F32)) for i in range(8)]
        P = P_new(K, ["pba%d" % i for i in range(8)])
        cst = sb("cst_sb", [128, 4096], F32)
        P.dma("sp", cst[:], K.cst_d[:, CST["cos2"][0]:CST["sins"][1]])
        off = CST["cos2"][0]
        cs = lambda n: cst[:, CST[n][0] - off:CST[n][1] - off]
        pr = lambda n: K.prm[:, PRM[n][0]:PRM[n][1]]
        wb = [sb("wb%d" % i, [128, 8, 512], BF16) for i in range(3)]
        phase_attn(K, P, st, sb, pbs, cs, pr, wb)
        finish_phase(K, P, st)


def prep_qk(P, K, src_ps, H, w_rep, rope_tile, out_bf, bufs, cs):
    xq, sq, t1, t2, ssq, rsq = bufs
    W = H * 128
    xqf = xq[:, 0:W]
    v3 = lambda a: a.rearrange("p (h d) -> p h d", h=H)
    pcs = []
    pcs.append(lambda: P.copy("act", xqf, src_ps))
    pcs.append(lambda: P.tt("dve", sq[:, 0:W], xqf, xqf, ALU.mult))
    pcs.append(lambda: P.add("dve", lambda e: e.tensor_reduce(ssq[:, 0:H], v3(sq[:, 0:W]), AX.X, ALU.add),
                             reads=[sq[:, 0:W]], writes=[ssq[:, 0:H]]))
    pcs.append(lambda: P.act(rsq[:, 0:H], ssq[:, 0:H], AF.Ln, scale=1.0 / 128, bias=EPS))
    pcs.append(lambda: P.act(rsq[:, 0:H], rsq[:, 0:H], AF.Exp, scale=-0.5))
    pcs.append(lambda: P.tt("dve", v3(xqf), v3(xqf), rsq[:, 0:H].unsqueeze(2).broadcast_to([128, H, 128]), ALU.mult))
    if rope_tile is None:
        pcs.append(lambda: P.tt("dve", v3(out_bf), v3(xqf), w_rep.unsqueeze(1).broadcast_to([128, H, 128]), ALU.mult))
        return pcs
    pcs.append(lambda: P.tt("dve", v3(xqf), v3(xqf), w_rep.unsqueeze(1).broadcast_to([128, H, 128]), ALU.mult))
    cos_t = cs("cos2")[:, rope_tile * 128:(rope_tile + 1) * 128]
    sin_t = cs("sins")[:, rope_tile * 128:(rope_tile + 1) * 128]
    pcs.append(lambda: P.tt("dve", v3(t1[:, 0:W]), v3(xqf), cos_t.unsqueeze(1).broadcast_to([128, H, 128]), ALU.mult))
    v5 = lambda a: a.rearrange("p (h a b f) -> p h a b f", h=H, a=2, b=2)
    sv = sin_t.rearrange("p (a b f) -> p a b f", a=2, b=2)
    for half in range(2):
        pcs.append(lambda half=half: P.tt("pool", v5(t2[:, 0:W])[:, :, :, half, :], v5(xqf)[:, :, :, 1 - half, :],
                                          sv[:, :, half, :].unsqueeze(1).broadcast_to([128, H, 2, 32]), ALU.mult))
    pcs.append(lambda: P.tt("dve", out_bf, t1[:, 0:W], t2[:, 0:W], ALU.add))
    return pcs


def interleave(A, B):
    na, nb = len(A), len(B)
    ia = ib = 0
    while ia < na or ib < nb:
        if ib < nb and (ia >= na or (ib + 1) * na <= ia * nb + nb // 2 * 0 + 0 and False):
            B[ib]()
            ib += 1
        elif ib < nb and (ia >= na or ib * na <= ia * nb):
            B[ib]()
            ib += 1
        else:
            A[ia]()
            ia += 1


def phase_attn(K, P, st, sb, pbs, cs, pr, wb):
    kT = sb("kT", [128, 2, NTA * 128], BF16)
    V = sb("V", [128, NTA, 256], BF16)
    bufsK2 = [(sb("xqk%d" % i, [128, 256], F32), sb("sqk%d" % i, [128, 256], F32), sb("t1k%d" % i, [128, 256], F32),
               sb("t2k%d" % i, [128, 256], F32), sb("ssqk%d" % i, [128, 8], F32), sb("rsqk%d" % i, [128, 8], F32))
              for i in range(2)]
    _bq = (sb("xq", [128, 512], F32), sb("sq", [128, 512], F32), sb("t1", [128, 512], F32),
           sb("t2", [128, 512], F32), sb("ssq", [128, 8], F32), sb("rsq", [128, 8], F32))
    bufsQ2 = [_bq, _bq]
    kbf2 = [sb("kbf%d" % i, [128, 256], BF16) for i in range(2)]
    wkv = wb[0]
    P.dma("pool", wkv[:], wview(K.w_in, 0, 512))
    wq = [wb[1], wb[2]]
    for i in range(2):
        P.dma("pool", wq[i][:], wview(K.w_in, C_AQ + i * 512, C_AQ + (i + 1) * 512))
    qT2 = [sb("qT%d" % i, [128, 8, 512], BF16) for i in range(2)]
    _qbf = sb("qbf", [128, 512], BF16)
    qbf2 = [_qbf, _qbf]
    PTs = [sb("PT%d" % i, [128, 512], BF16) for i in range(3)]
    rc = sb("rc", [128, 512], F32)
    scl = 128.0 ** -0.5

    def kv_pieces(par):
        pcs = []
        bufsK = bufsK2[par]
        kbf = kbf2[par]
        for t in range(par, NTA, 2):
            isctx = t < NTC
            pm = pbs[2 + (t % 2)]

            def proj(t=t, isctx=isctx, pm=pm):
                for k in range(8):
                    lhsT = K.hcT[:, k, t * 128:(t + 1) * 128] if isctx else K.hxT[:, k, (t - NTC) * 128:(t - NTC + 1) * 128]
                    P.mm(pm[:, :], lhsT, wkv[:, k, :], start=(k == 0), stop=(k == 7))
                P.copy("act", V[:, t, :], pm[:, 256:512])
            pcs.append(proj)
            pcs += prep_qk(P, K, pm[:, 0:256], 2, pr("knw"), None if isctx else t - NTC, kbf[:], bufsK, cs)

            def trs(t=t, kbf=kbf):
                pT = pbs[4 + (t % 2)][:].bitcast(BF16)
                for g in range(2):
                    P.tr(pT[:, g * 128:(g + 1) * 128], kbf[:, g * 128:(g + 1) * 128], K.identb)
                P.copy("dve", kT[:, :, t * 128:(t + 1) * 128], pT[:, 0:256].rearrange("p (g d) -> p g d", g=2))
            pcs.append(trs)
        return pcs

    def q_pieces_half(qb, half):
        qT = qT2[qb % 2]
        pcs = []
        bufsQ = bufsQ2[half]
        qbf = qbf2[half]
        for tt in range(4):
            t = qb * 4 + tt
            for half in (half,):
                pm = pbs[half]

                def proj(t=t, half=half, pm=pm):
                    for k in range(8):
                        P.mm(pm[:, :], K.hxT[:, k, t * 128:(t + 1) * 128], wq[half][:, k, :], start=(k == 0), stop=(k == 7))
                pcs.append(proj)
                pcs += prep_qk(P, K, pm[:, :], 4, pr("qnw"), t, qbf[:], bufsQ, cs)

                def trs(tt=tt, half=half, pm=pm, qT=qT, qbf=qbf):
                    pT = pm[:].bitcast(BF16)
                    for hh in range(4):
                        P.tr(pT[:, hh * 128:(hh + 1) * 128], qbf[:, hh * 128:(hh + 1) * 128], K.identb)
                    P.copy("act", qT[:, half * 4:(half + 1) * 4, tt * 128:(tt + 1) * 128],
                           pT[:, 0:512].rearrange("p (h d) -> p h d", h=4))
                pcs.append(trs)
        return pcs

    def core_pieces(qb):
        qT = qT2[qb % 2]
        pcs = []
        for h in range(8):
            g = h // 4
            pO = pbs[4 + (h % 2)]
            pR = pbs[6 + (h % 2)]

            def score(j, g=g, h=h):
                P.mm(pbs[2 + (j % 2)][:, :], kT[:, g, j * 128:(j + 1) * 128], qT[:, h, :])
                P.act(PTs[j % 3][:], pbs[2 + (j % 2)][:, :], AF.Exp, scale=scl, bias=-6.0)

            def accum(j, g=g, pO=pO, pR=pR):
                P.mm(pO[:, :], V[:, j, g * 128:(g + 1) * 128], PTs[j % 3][:], start=(j == 0), stop=(j == NTA - 1))
                P.mm(pR[:, :], K.onesb, PTs[j % 3][:], start=(j == 0), stop=(j == NTA - 1))

            pcs.append(lambda score=score: score(0))
            for j in range(NTA):
                def stp(j=j, score=score, accum=accum):
                    if j + 1 < NTA:
                        score(j + 1)
                    accum(j)
                pcs.append(stp)

            def fin(h=h, pO=pO, pR=pR):
                P.add("dve", lambda e, a=rc[:], b=pR[:, :]: e.reciprocal(a, b), reads=[pR[:, :]], writes=[rc[:]])
                P.tt("dve", K.attnT[:, qb, h, :], pO[:, :], rc[:], ALU.mult)
            pcs.append(fin)
        return pcs

    def merge2(A, B):
        out = []
        ia = ib = 0
        while ia < len(A) or ib < len(B):
            if ib < len(B) and (ia >= len(A) or ib * len(A) <= ia * len(B)):
                out.append(B[ib])
                ib += 1
            else:
                out.append(A[ia])
                ia += 1
        return out

    def q_pieces(qb):
        a, b = q_pieces_half(qb, 0), q_pieces_half(qb, 1)
        n = len(a) // 4
        out = []
        for tt in range(4):
            out += a[tt * n:(tt + 1) * n] + b[tt * n:(tt + 1) * n]
        return out

    interleave(merge2(kv_pieces(0), kv_pieces(1)), q_pieces(0))
    for qb in range(4):
        interleave(core_pieces(qb), q_pieces(qb + 1) if qb + 1 < 4 else [])
    tap(P, "kT", kT[:])
    tap(P, "V", V[:])
    tap(P, "attnT", K.attnT)


def make_rowrep(P, K, dst, group, scr, onesf, pbank):
    for k in range(8):
        P.ts("dve", scr[:, k * 128:(k + 1) * 128], K.identf[:], K.modv[:, 0, group, k:k + 1], None, ALU.mult)
    for cb in range(2):
        P.mm(pbank[:, :], onesf[:], scr[:, cb * 512:(cb + 1) * 512])
        P.copy("act", dst[:, cb * 512:(cb + 1) * 512], pbank[:, :])


def phase_merge(K):
    nc = K.nc
    with contextlib.ExitStack() as st:
        sb = lambda name, shape, dt: st.enter_context(nc.sbuf_tensor(name + "_b", shape, dt))
        pbs = [st.enter_context(nc.psum_tensor("pbb%d" % i, [128, 512], F32)) for i in range(8)]
        P = P_new(K, ["pbb%d" % i for i in range(8)])
        wout = sb("wout", [128, 8, 1024], BF16)
        for cb in range(2):
            P.dma("pool", wout[:, :, cb * 512:(cb + 1) * 512], wview(K.w_out, cb * 512, (cb + 1) * 512))
        wm = [[sb("wm%d_%d" % (i, b), [128, 8, 128], BF16) for b in range(2)] for i in range(4)]
        onesf = sb("onesf", [128, 128], F32)
        P.memset("dve", onesf[:], 1.0)
        g1rep = sb("g1rep", [128, D], F32)
        dscr = sb("dscr", [128, D], F32)
        make_rowrep(P, K, g1rep, 2, dscr, onesf, pbs[7])
        yT = sb("yT", [128, 8, 512], BF16)
        s1 = sb("s1", [128, 512], F32)
        s2 = sb("s2", [128, 512], F32)
        u1 = sb("u1", [128, 512], F32)
        u2 = sb("u2", [128, 512], F32)
        xts = [sb("xt%d" % i, [128, D], F32) for i in range(2)]
        gmf = sb("gmf", [128, D], F32)
        xn = sb("xn", [128, D], BF16)
        junk = sb("junk", [128, D], BF16)
        ss = sb("ss", [128, NT], F32)
        rs = sb("rs", [128, NT], F32)
        srcs = ((K.w_in, C_GA), (K.w_in, C_GD), (K.w_pa, 0), (K.w_pd, 0))
        def load_wm(n):
            if n >= 32:
                return
            f_ = n % 8
            for i_, (wsrc, c0) in enumerate(srcs):
                P.dma("pool", wm[i_][n % 2][:], wview(wsrc, c0 + f_ * 128, c0 + (f_ + 1) * 128))

        load_wm(0)
        yT2 = [yT, sb("yTb", [128, 8, 512], BF16)]

        def floop_pieces(tb):
            blk = slice(tb * 512, (tb + 1) * 512)
            yTc = yT2[tb % 2]
            pcs = []
            for f in range(8):
                def pf(f=f):
                    b = (tb * 8 + f) % 2
                    load_wm(tb * 8 + f + 1)
                    rh = (lambda k: K.hxT[:, k, blk], lambda k: K.hxT[:, k, blk],
                          lambda k: K.attnT[:, tb, k, :], lambda k: K.gdnT[:, tb, k, :])
                    for i in range(4):
                        for k in range(8):
                            P.mm(pbs[i][:, :], wm[i][b][:, k, :], rh[i](k), start=(k == 0), stop=(k == 7))
                    P.act(s1[:], pbs[0][:, :], AF.Sigmoid)
                    P.act(s2[:], pbs[1][:, :], AF.Sigmoid)
                    P.tt("dve", u1[:], pbs[2][:, :], s1[:], ALU.mult)
                    P.tt("dve", u2[:], pbs[3][:, :], s2[:], ALU.mult)
                    P.tt("dve", yTc[:, f, :], u1[:], u2[:], ALU.add)
                pcs.append(pf)
            return pcs

        def tail_pieces(tb):
            yTc = yT2[tb % 2]
            pcs = []
            for i in range(4):
                t = tb * 4 + i
                xt = xts[t % 2]

                def p1(t=t, i=i, xt=xt):
                    P.dma("sp", xt[:], K.x[t * 128:(t + 1) * 128, :])
                    for cb in range(2):
                        pm = pbs[4 + cb]
                        for k in range(8):
                            P.mm(pm[:, :], yTc[:, k, i * 128:(i + 1) * 128], wout[:, k, cb * 512:(cb + 1) * 512],
                                 start=(k == 0), stop=(k == 7))
                        P.tt("dve", gmf[:, cb * 512:(cb + 1) * 512], pm[:, :], g1rep[:, cb * 512:(cb + 1) * 512], ALU.mult)
                    P.copy("act", K.gm[:, t, :], gmf[:])
                    P.tt("dve", xt[:], xt[:], K.gm[:, t, :], ALU.add)
                    P.act(junk[:], xt[:], AF.Square, accum_out=ss[:, t:t + 1])
                    rsqrt_act(P, rs[:, t:t + 1], ss[:, t:t + 1], 1.0 / D, EPS)
                    P.ts("dve", xn[:], xt[:], rs[:, t:t + 1], None, ALU.mult)
                pcs.append(p1)

                def p2(t=t):
                    pT = pbs[6 + (t % 2)][:].bitcast(BF16)
                    for k in range(8):
                        P.tr(pT[:, k * 128:(k + 1) * 128], xn[:, k * 128:(k + 1) * 128], K.identb)
                    for k in range(8):
                        P.ts("dve", K.hxT[:, k, t * 128:(t + 1) * 128], pT[:, k * 128:(k + 1) * 128],
                             K.modv[:, 0, 4, k:k + 1], K.modv[:, 0, 3, k:k + 1], ALU.mult, ALU.add)
                pcs.append(p2)
            return pcs

        for pc_ in floop_pieces(0):
            pc_()
        for tb in range(4):
            interleave(floop_pieces(tb + 1) if tb + 1 < 4 else [], tail_pieces(tb)) if tb + 1 < 4 else [pc_() for pc_ in tail_pieces(tb)]
        tap(P, "gm", K.gm)
        tap(P, "h2T", K.hxT[:])
        finish_phase(K, P, st)


def phase_ffn(K):
    nc = K.nc
    with contextlib.ExitStack() as st:
        sb = lambda name, shape, dt: st.enter_context(nc.sbuf_tensor(name + "_c", shape, dt))
        pbs = [st.enter_context(nc.psum_tensor("pbc%d" % i, [128, 512], F32)) for i in range(8)]
        P = P_new(K, ["pbc%d" % i for i in range(8)])
        pr = lambda n: K.prm[:, PRM[n][0]:PRM[n][1]]
        wd = sb("wd", [128, 22, D], BF16)
        wdv = K.w_down.rearrange("(j p) n -> p j n", p=128)
        for j0, j1 in ((0, 6), (6, 12), (12, 17), (17, 22)):
            P.dma("pool", wd[:, j0:j1, :], wdv[:, j0:j1, :])
        onesf = sb("onesf", [128, 128], F32)
        P.memset("dve", onesf[:], 1.0)
        g2rep = sb("g2rep", [128, D], F32)
        dscr = sb("dscr", [128, D], F32)
        make_rowrep(P, K, g2rep, 5, dscr, onesf, pbs[7])
        wu = [[sb("wu%d_%d" % (i, b), [128, 8, 128], BF16) for b in range(3)] for i in range(2)]
        RW = 516
        raws = [[sb("raw%d_%d" % (i, b), [128, RW], F32) for b in range(2)] for i in range(2)]
        accs = [[sb("acc%d_%d" % (i, b), [128, 512], F32) for b in range(2)] for i in range(2)]
        sg = [sb("sg%d" % b, [128, 512], F32) for b in range(2)]
        gvT = sb("gvT", [128, 22, 512], BF16)
        xts = [sb("xt%d" % i, [128, D], F32) for i in range(2)]
        f2 = sb("f2", [128, D], F32)
        ot = [sb("ot%d" % i, [128, D], F32) for i in range(2)]
        junk = sb("junk", [128, D], BF16)
        ss = sb("ss", [128, NT], F32)
        rs = sb("rs", [128, NT], F32)
        outs = []
        nmm = 0

        def load_w(n):
            if n >= 4 * 22:
                return
            j_ = n % 22
            for i_ in range(2):
                c0 = i_ * DFF + j_ * 128
                P.dma("pool", wu[i_][n % 3][:], wview(K.w_up, c0, c0 + 128))

        load_w(0)
        load_w(1)
        for q in range(4):
            t0 = q * 512
            lo = max(t0 - 2, 0)
            hi = min(t0 + 514, NTOK)
            i0 = lo - (t0 - 2)
            ncol = hi - lo
            for j in range(22):
                b = (q * 22 + j) % 2
                b3 = (q * 22 + j) % 3
                load_w(q * 22 + j + 2)
                for i in range(2):
                    raw = raws[i][b]
                    if q == 0:
                        P.memset("pool", raw[:, 1:2], 0.0)
                    if q == 3:
                        P.memset("pool", raw[:, 514:515], 0.0)
                    for (g0, n) in ((0, 258), (258, ncol - 258)):
                        pm = pbs[nmm % 4]
                        nmm += 1
                        for k in range(8):
                            P.mm(pm[:, 0:n], wu[i][b3][:, k, :], K.hxT[:, k, lo + g0:lo + g0 + n],
                                 start=(k == 0), stop=(k == 7))
                        P.copy("act", raw[:, i0 + g0:i0 + g0 + n], pm[:, 0:n])
                    jj = i * 22 + j
                    cw = lambda tap_: K.prm[:, PRM["cffw"][0] + tap_ * 44 + jj:PRM["cffw"][0] + tap_ * 44 + jj + 1]
                    cb_ = K.prm[:, PRM["cffb"][0] + jj:PRM["cffb"][0] + jj + 1]
                    acc = accs[i][b]
                    P.act(acc[:], raw[:, 1:513], AF.Identity, scale=cw(0), bias=cb_)
                    P.stt("dve", acc[:], raw[:, 2:514], cw(1), acc[:], ALU.mult, ALU.add)
                    P.stt("dve", acc[:], raw[:, 3:515], cw(2), acc[:], ALU.mult, ALU.add)
                P.act(sg[b][:], accs[0][b][:], AF.Silu)
                P.tt("dve", gvT[:, j, :], sg[b][:], accs[1][b][:], ALU.mult)
            for i in range(4):
                t = q * 4 + i
                xt = xts[t % 2]
                o = ot[t % 2]
                P.dma("sp", xt[:], K.x[t * 128:(t + 1) * 128, :])
                P.tt("dve", xt[:], xt[:], K.gm[:, t, :], ALU.add)
                for cb in range(2):
                    pm = pbs[4 + cb]
                    for j in range(22):
                        P.mm(pm[:, :], gvT[:, j, i * 128:(i + 1) * 128], wd[:, j, cb * 512:(cb + 1) * 512],
                             start=(j == 0), stop=(j == 21))
                    P.tt("dve", f2[:, cb * 512:(cb + 1) * 512], pm[:, :], g2rep[:, cb * 512:(cb + 1) * 512], ALU.mult)
                P.tt("dve", xt[:], xt[:], f2[:], ALU.add)
                P.act(junk[:], xt[:], AF.Square, accum_out=ss[:, t:t + 1])
                rsqrt_act(P, rs[:, t:t + 1], ss[:, t:t + 1], 1.0 / D, EPS)
                P.stt("dve", o[:], xt[:], rs[:, t:t + 1], pr("fnw"), ALU.mult, ALU.mult)
                outs.append(P.dma("sp", K.out[t * 128:(t + 1) * 128, :], o[:]))
        finish_phase(K, P, st)


_NC_CACHE = {}


def make_in_maps(inputs, cores):
    cst, cbf = host_constants()
    f = lambda a: np.ascontiguousarray(np.asarray(a, np.float32))
    shared = {
        "w_mod": f(inputs["w_mod"][0]), "w_in": f(inputs["w_in"][0]), "w_pa": f(inputs["w_pa"][0]),
        "w_pd": f(inputs["w_pd"][0]), "w_out": f(inputs["w_out"][0]), "w_up": f(inputs["w_up"][0]),
        "w_down": f(inputs["w_down"][0]), "cst": cst, "cbf": cbf,
    }
    maps = []
    for b in cores:
        m = dict(shared)
        m["x"] = f(inputs["x"][b])
        m["ctx"] = f(inputs["ctx"][b])
        m["prm"] = host_params(b, np.asarray(inputs["c"]), np.asarray(inputs["c_ctx"]), np.asarray(inputs["b_mod"]),
                               np.asarray(inputs["conv_qkv_w"]), np.asarray(inputs["ffn_conv_w"]),
                               np.asarray(inputs["ffn_conv_b"]), np.asarray(inputs["gdn_norm_w"]),
                               np.asarray(inputs["q_norm_w"]), np.asarray(inputs["k_norm_w"]),
                               np.asarray(inputs["final_norm_w"]), np.asarray(inputs["a_log"]),
                               np.asarray(inputs["dt_bias"]))
        maps.append(m)
    return maps


def kernel(**inputs):
    if "nc" not in _NC_CACHE:
        _NC_CACHE["nc"] = build_nc()
    nc = _NC_CACHE["nc"]
    maps = make_in_maps(inputs, list(range(8)))
    res = run_bass_kernel_spmd(nc, maps, core_ids=list(range(8)))
    return np.stack([np.asarray(r["out"], np.float32) for r in res.results], axis=0)


def phase_gdn(K):
    nc = K.nc
    with contextlib.ExitStack() as st:
        sb = lambda name, shape, dt: st.enter_context(nc.sbuf_tensor(name + "_g", shape, dt))
        pbs = [st.enter_context(nc.psum_tensor("pbg%d" % i, [128, 512], F32)) for i in range(8)]
        P = P_new(K, ["pbg%d" % i for i in range(8)])
        pr = lambda n: K.prm[:, PRM[n][0]:PRM[n][1]]
        cst = sb("cstg", [128, 768], F32)
        P.dma("sp", cst[:], K.cst_d[:, 0:768])
        cs = lambda n: cst[:, CST[n][0]:CST[n][1]]
        identf = cs("ident")
        onesf = cs("ones")
        mstrict = (cs("ustrict"), cs("lstrict"))
        mincl = (cs("uincl"), cs("lincl"))
        twoI = sb("twoI", [128, 128], F32)
        P.ts("dve", twoI[:], identf, 2.0, None, ALU.mult)

        def hT(k, t):
            return K.hcT[:, k, t * 128:(t + 1) * 128] if t < NTC else K.hxT[:, k, (t - NTC) * 128:(t - NTC + 1) * 128]

        wdd = sb("wdd", [128, 8, 32], BF16)
        P.dma("pool", wdd[:], wview(K.w_in, C_DB, C_DB + 32))
        dbda = sb("dbda", [128, NTA, 32], F32)
        for t in range(NTA):
            pm = pbs[t % 2]
            for k in range(8):
                P.mm(pm[:, 0:32], hT(k, t), wdd[:, k, :], start=(k == 0), stop=(k == 7))
            P.copy("act", dbda[:, t, :], pm[:, 0:32])
        beta = sb("beta", [128, NTA, 16], F32)
        la = sb("la", [128, NTA, 16], F32)
        gam = sb("gam", [128, NTA, 16], F32)
        negeg = sb("negeg", [128, NTA, 16], F32)
        kdsc = sb("kdsc", [128, NTA, 16], F32)
        gl = sb("gl", [128, NTA, 16], F32)
        nexpA = sb("nexpA", [128, 16], F32)
        P.act(beta[:], dbda[:, :, 0:16], AF.Sigmoid)
        P.act(nexpA[:], pr("alog"), AF.Exp)
        P.ts("dve", nexpA[:], nexpA[:], -1.0, None, ALU.mult)
        P.tt("dve", la[:], dbda[:, :, 16:32], pr("dtb").unsqueeze(1).broadcast_to([128, NTA, 16]), ALU.add)
        P.act(la[:], la[:], AF.Exp)
        P.act(la[:], la[:], AF.Ln, bias=1.0)
        P.tt("dve", la[:], la[:], nexpA[:].unsqueeze(1).broadcast_to([128, NTA, 16]), ALU.mult)
        pg, ptot = pbs[2], pbs[3]
        for t in range(NTA):
            P.mm(pg[:, t * 16:t * 16 + 8], mincl[0], la[:, t, 0:8])
            P.mm(pg[:, t * 16 + 8:t * 16 + 16], mincl[1], la[:, t, 8:16])
            P.mm(ptot[:, t * 16:(t + 1) * 16], onesf, la[:, t, :])
        gflat = lambda a: a[:].rearrange("p t c -> p (t c)")
        P.copy("dve", gflat(gam), pg[:, 0:NTA * 16])
        P.act(negeg[:], gam[:], AF.Exp)
        P.ts("dve", gflat(negeg), gflat(negeg), -1.0, None, ALU.mult)
        P.tt("dve", gflat(kdsc), ptot[:, 0:NTA * 16], gflat(gam), ALU.subtract)
        P.act(kdsc[:], kdsc[:], AF.Exp)
        P.act(gflat(gl), ptot[:, 0:NTA * 16], AF.Exp)
        tap(P, "beta", beta[:])
        tap(P, "la", la[:])
        tap(P, "gam", gam[:])
        wh = [sb("wh%d" % i, [128, 8, 128], BF16) for i in range(4)]
        RAWN = 2308
        raw = sb("raw", [128, RAWN], F32)
        acc = sb("acc", [128, RAWN], F32)
        sqb = sb("sqb", [128, 512], BF16)
        rinv = sb("rinv", [128, 512], F32)
        qTn2 = [sb("qTn%d" % i, [128, NTA * 128], BF16) for i in range(2)]
        kTn2 = [sb("kTn%d" % i, [128, NTA * 128], BF16) for i in range(2)]
        vT = sb("vT", [128, NTA * 128], BF16)
        Ktok2 = [sb("Ktok%d" % i, [128, NTA, 128], BF16) for i in range(2)]
        Vtok2 = [sb("Vtok%d" % i, [128, NTA, 128], BF16) for i in range(2)]
        o_acc2 = [sb("o_acc%d" % i, [128, NT, 128], F32) for i in range(2)]
        S = [sb("S%d" % d, [128, 128], F32) for d in range(2)]
        Sb = [sb("Sb%d" % d, [128, 128], BF16) for d in range(2)]
        GS = 3
        NR = 2 * GS
        Zb = [[sb("Zb%d_%d" % (d, i), [128, 128], BF16) for i in range(NR)] for d in range(2)]
        PmT = [[sb("PmT%d_%d" % (d, i), [128, 128], BF16) for i in range(NR)] for d in range(2)]
        qdT = [[sb("qdT%d_%d" % (d, i), [128, 128], BF16) for i in range(NR)] for d in range(2)]
        kd = [[sb("kd%d_%d" % (d, i), [128, 128], BF16) for i in range(NR)] for d in range(2)]
        scrI = [{n: sb("%s%d" % (n, i), [128, 128], F32) for n in ("bA", "bB", "bC", "bD")}
                for i in range(2 * GS)]
        for c_ in scrI:
            c_["bE"] = c_["bA"]
        rt = [sb("rt%d" % d, [128, 128], BF16) for d in range(2)]
        dl = [sb("dl%d" % d, [128, 128], BF16) for d in range(2)]
        yv = sb("yv", [128, 512], F32)
        ysq = yv
        sz = sb("sz", [128, 512], F32)
        ybf = sb("ybf", [128, 512], BF16)
        yss = sb("yss", [128, 4], F32)
        yrs = sb("yrs", [128, 4], F32)
        P.memset("dve", raw[:], 0.0)
        segs = [(1, 0, 256)] + [(259 + b * 512, 256 + b * 512, 512) for b in range(4)]

        def tokcols(c0, n):
            if c0 < 256:
                return lambda k: K.hcT[:, k, c0:c0 + n]
            return lambda k: K.hxT[:, k, c0 - 256:c0 - 256 + n]

        def rawcol(tok):
            return tok + 1 if tok < 256 else tok + 3

        def g1_pieces(hh):
            qT_, kT_, Kt_, Vt_ = qTn2[hh % 2], kTn2[hh % 2], Ktok2[hh % 2], Vtok2[hh % 2]
            pcs = []
            pm = pbs[7]

            def w_():
                for i, cb in enumerate((C_GQ, C_GK, C_GV)):
                    P.dma("pool", wh[i][:], wview(K.w_in, cb + hh * 128, cb + (hh + 1) * 128))
            pcs.append(w_)
            W = RAWN - 2
            for xi, dst in enumerate((qT_, kT_, vT)):
                j = xi * 8 + hh
                cw = lambda tap_, j=j: K.prm[:, PRM["cqkv"][0] + tap_ * 24 + j:PRM["cqkv"][0] + tap_ * 24 + j + 1]
                for (rc0, tc0, n) in segs:
                    def proj(xi=xi, tc0=tc0, n=n, rc0=rc0):
                        rhs = tokcols(tc0, n)
                        for k in range(8):
                            P.mm(pm[:, 0:n], wh[xi][:, k, :], rhs(k), start=(k == 0), stop=(k == 7))
                        P.copy("act", raw[:, rc0:rc0 + n], pm[:, 0:n])
                    pcs.append(proj)
                NCH = 6
                bnds = [1 + (W * i) // NCH for i in range(NCH + 1)]
                for ci in range(NCH):
                    ca, cb_ = bnds[ci], bnds[ci + 1]
                    pcs.append(lambda ca=ca, cb_=cb_, cw=cw: P.act(acc[:, ca:cb_], raw[:, ca - 1:cb_ - 1], AF.Copy, scale=cw(0)))
                    pcs.append(lambda ca=ca, cb_=cb_, cw=cw: P.stt("dve", acc[:, ca:cb_], raw[:, ca:cb_], cw(1), acc[:, ca:cb_], ALU.mult, ALU.add))
                    pcs.append(lambda ca=ca, cb_=cb_, cw=cw: P.stt("dve", acc[:, ca:cb_], raw[:, ca + 1:cb_ + 1], cw(2), acc[:, ca:cb_], ALU.mult, ALU.add))
                if xi == 2:
                    pcs.append(lambda: P.act(vT[:, 0:256], acc[:, 1:257], AF.Silu))
                    for c0 in range(0, 2048, 512):
                        pcs.append(lambda c0=c0: P.act(vT[:, 256 + c0:256 + c0 + 512], acc[:, 259 + c0:259 + c0 + 512], AF.Silu))
                    continue
                for ci in range(NCH):
                    ca, cb_ = bnds[ci], bnds[ci + 1]
                    pcs.append(lambda ca=ca, cb_=cb_: P.act(acc[:, ca:cb_], acc[:, ca:cb_], AF.Silu))
                for (rc0, tc0, n) in segs:
                    pcs.append(lambda rc0=rc0, n=n: P.tt("dve", sqb[:, 0:n], acc[:, rc0:rc0 + n], acc[:, rc0:rc0 + n], ALU.mult))

                    def nrm1(n=n):
                        P.mm(pm[:, 0:n], K.onesb, sqb[:, 0:n])
                        P.act(rinv[:, 0:n], pm[:, 0:n], AF.Ln, bias=EPS)
                    pcs.append(nrm1)
                    pcs.append(lambda n=n, xi=xi: P.act(rinv[:, 0:n], rinv[:, 0:n], AF.Exp, scale=-0.5,
                                                        bias=(-0.5 * float(np.log(128.0)) if xi == 0 else 0.0)))
                    pcs.append(lambda dst=dst, rc0=rc0, tc0=tc0, n=n: P.tt("dve", dst[:, tc0:tc0 + n], acc[:, rc0:rc0 + n], rinv[:, 0:n], ALU.mult))
            for src, dstk in ((kT_, Kt_), (vT, Vt_)):
                for g0 in range(0, NTA, 8):
                    def trp(src=src, dstk=dstk, g0=g0):
                        ng = min(8, NTA - g0)
                        pT = pm[:].bitcast(BF16)
                        for i in range(ng):
                            P.tr(pT[:, i * 128:(i + 1) * 128], src[:, (g0 + i) * 128:(g0 + i + 1) * 128], K.identb)
                        P.copy("act", dstk[:, g0:g0 + ng, :], pT[:, 0:ng * 128].rearrange("p (i d) -> p i d", i=ng))
                    pcs.append(trp)
            return pcs

        def tail_pieces(hh):
            oa = o_acc2[hh % 2]
            pz = pbs[7]
            pcs = [lambda: P.dma("pool", wh[3][:], wview(K.w_in, C_Z + hh * 128, C_Z + (hh + 1) * 128))]
            v3 = lambda a: a.rearrange("p (i d) -> p i d", i=4)
            for tb in range(4):
                ov = oa[:, tb * 4:(tb + 1) * 4, :]
                def zp(tb=tb):
                    for i in range(4):
                        t = tb * 4 + i
                        for k in range(8):
                            P.mm(pz[:, i * 128:(i + 1) * 128], K.hxT[:, k, t * 128:(t + 1) * 128], wh[3][:, k, :],
                                 start=(k == 0), stop=(k == 7))
                    P.act(sz[:], pz[:, :], AF.Silu)
                pcs.append(zp)
                pcs.append(lambda ov=ov: P.tt("dve", v3(ysq[:]), ov, ov, ALU.mult))
                pcs.append(lambda: P.add("dve", lambda e, a=yss[:], b=v3(ysq[:]): e.tensor_reduce(a, b, AX.X, ALU.add),
                                         reads=[ysq[:]], writes=[yss[:]]))
                pcs.append(lambda: P.act(yrs[:], yss[:], AF.Ln, scale=1.0 / 128, bias=EPS))
                pcs.append(lambda: P.act(yrs[:], yrs[:], AF.Exp, scale=-0.5))
                pcs.append(lambda ov=ov: P.tt("dve", v3(yv[:]), ov, yrs[:].unsqueeze(2).broadcast_to([128, 4, 128]), ALU.mult))
                pcs.append(lambda: P.tt("dve", v3(yv[:]), v3(yv[:]), pr("gnw").unsqueeze(1).broadcast_to([128, 4, 128]), ALU.mult))
                pcs.append(lambda: P.tt("dve", ybf[:], yv[:], sz[:], ALU.mult))

                def trs(tb=tb):
                    pT = pz[:].bitcast(BF16)
                    for i in range(4):
                        P.tr(pT[:, i * 128:(i + 1) * 128], ybf[:, i * 128:(i + 1) * 128], K.identb)
                    P.copy("act", K.gdnT[:, tb, hh, :], pT[:, 0:512])
                pcs.append(trs)
            return pcs

        NH = 8 if GDN_HEADS_LIMIT is None else GDN_HEADS_LIMIT
        for pc_ in g1_pieces(0):
            pc_()
        for h in range(NH):
            qTn, kTn, Ktok, Vtok = qTn2[h % 2], kTn2[h % 2], Ktok2[h % 2], Vtok2[h % 2]
            o_acc = o_acc2[h % 2]
            if h == 0:
                tap(P, "qTn0", qTn[:])
                tap(P, "kTn0", kTn[:])
                tap(P, "vT0", vT[:])
            P.memset("dve", o_acc[:], 0.0)
            for d in range(2):
                P.memset("dve", S[d][:], 0.0)
                P.memset("dve", Sb[d][:], 0.0)
            order = [list(range(NTA)), [1, 0] + list(range(NTA - 1, NTC - 1, -1))]

            def prep_stages(g):
                insts = []
                for s_ in range(g * GS, min((g + 1) * GS, NTA)):
                    for d in range(2):
                        insts.append((s_, d, order[d][s_], (s_ - g * GS) * 2 + d))
                stages = []

                def each(fn):
                    def run_():
                        for (s_, d, t, ii) in insts:
                            fn(s_, d, t, ii, scrI[ii], pbs[ii], s_ % NR, d * 8 + h, t >= NTC)
                    stages.append(run_)

                def st1(s_, d, t, ii, c, pb, slot, r, lat):
                    tc = slice(t * 128, (t + 1) * 128)
                    P.act(c["bA"][:], identf, AF.Copy, scale=gam[:, t, r:r + 1])
                    P.mm(pb[:, 0:128], kTn[:, tc], kTn[:, tc])
                    if lat:
                        P.mm(pb[:, 128:256], kTn[:, tc], qTn[:, tc])
                    P.mm(pb[:, 256:384], onesf, c["bA"][:])
                each(st1)

                def st3(s_, d, t, ii, c, pb, slot, r, lat):
                    P.ts("dve", c["bB"][:], pb[:, 256:384], gam[:, t, r:r + 1], 0.0, ALU.subtract, ALU.min)
                    if lat:
                        P.act(c["bD"][:], pb[:, 256:384], AF.Exp)
                    P.act(c["bB"][:], c["bB"][:], AF.Exp)
                each(st3)

                def st5(s_, d, t, ii, c, pb, slot, r, lat):
                    P.tt("dve", c["bC"][:], pb[:, 0:128], c["bB"][:], ALU.mult)
                    if lat:
                        P.tt("dve", c["bE"][:], pb[:, 128:256], c["bB"][:], ALU.mult)
                each(st5)

                def st6(s_, d, t, ii, c, pb, slot, r, lat):
                    P.stt("dve", c["bC"][:], c["bC"][:], beta[:, t, r:r + 1], mstrict[d], ALU.mult, ALU.mult)
                    if lat:
                        P.tt("pool", PmT[d][slot][:], c["bE"][:], mincl[d], ALU.mult)
                        P.tt("pool", qdT[d][slot][:], qTn[:, t * 128:(t + 1) * 128], c["bD"][:], ALU.mult)
                    P.act(kd[d][slot][:], Ktok[:, t, :], AF.Copy, scale=kdsc[:, t, r:r + 1])
                    P.tr(pb[:, 384:512], c["bC"][:], identf)
                each(st6)

                def st8(s_, d, t, ii, c, pb, slot, r, lat):
                    P.tt("dve", c["bD"][:], pb[:, 384:512], identf, ALU.add)
                    P.tt("pool", c["bB"][:], identf, c["bC"][:], ALU.subtract)
                each(st8)
                for it in range(6):
                    def sA(s_, d, t, ii, c, pb, slot, r, lat):
                        P.mm(pb[:, 0:128], c["bB"][:], c["bD"][:])
                    each(sA)

                    def sB(s_, d, t, ii, c, pb, slot, r, lat):
                        P.tt("dve", c["bA"][:], twoI[:], pb[:, 0:128], ALU.subtract)
                    each(sB)

                    def sC(s_, d, t, ii, c, pb, slot, r, lat):
                        P.mm(pb[:, 128:256], c["bA"][:], c["bB"][:])
                    each(sC)

                    def sD(s_, d, t, ii, c, pb, slot, r, lat, it=it):
                        if it < 5:
                            P.copy("act", c["bB"][:], pb[:, 128:256])
                        else:
                            P.copy("act", Zb[d][slot][:], pb[:, 128:256])
                    each(sD)
                return stages

            def scan_micro(s_):
                ms = []
                info = []
                for d in range(2):
                    t = order[d][s_]
                    info.append((d, t, s_ % NR, d * 8 + h, t >= NTC, pbs[6][:, d * 256:(d + 1) * 256], slice(t * 128, (t + 1) * 128)))

                def m1():
                    for (d, t, slot, r, lat, pc, tc) in info:
                        P.mm(pc[:, 0:128], kTn[:, tc], Sb[d][:])
                def m2():
                    for (d, t, slot, r, lat, pc, tc) in info:
                        P.stt("dve", rt[d][:], pc[:, 0:128], negeg[:, t, r:r + 1], Vtok[:, t, :], ALU.mult, ALU.add)
                def m3():
                    for (d, t, slot, r, lat, pc, tc) in info:
                        P.mm(pc[:, 128:256], Zb[d][slot][:], rt[d][:])
                def m4():
                    for (d, t, slot, r, lat, pc, tc) in info:
                        P.ts("dve", dl[d][:], pc[:, 128:256], beta[:, t, r:r + 1], None, ALU.mult)
                def m5():
                    for (d, t, slot, r, lat, pc, tc) in info:
                        if lat:
                            P.mm(pc[:, 0:128], qdT[d][slot][:], Sb[d][:], start=True, stop=False)
                            P.mm(pc[:, 0:128], PmT[d][slot][:], dl[d][:], start=False, stop=True)
                        P.mm(pc[:, 128:256], kd[d][slot][:], dl[d][:])
                def m6():
                    for (d, t, slot, r, lat, pc, tc) in info:
                        if lat:
                            P.tt("dve", o_acc[:, t - NTC, :], pc[:, 0:128], o_acc[:, t - NTC, :], ALU.add)
                        P.stt("dve", S[d][:], S[d][:], gl[:, t, r:r + 1], pc[:, 128:256], ALU.mult, ALU.add)
                def m7():
                    for (d, t, slot, r, lat, pc, tc) in info:
                        P.copy("act", Sb[d][:], S[d][:])
                return [m1, m2, m3, m4, m5, m6, m7]

            NG = (NTA + GS - 1) // GS
            L = list(prep_stages(0))
            for g in range(NG):
                nxt = prep_stages(g + 1) if g + 1 < NG else []
                mic = []
                for s_ in range(g * GS, min((g + 1) * GS, NTA)):
                    mic += scan_micro(s_)
                na, nb = len(nxt), len(mic)
                ia = ib = 0
                while ia < na or ib < nb:
                    if ib < nb and (ia >= na or ib * na <= ia * nb):
                        L.append(mic[ib])
                        ib += 1
                    else:
                        L.append(nxt[ia])
                        ia += 1
            G1n = g1_pieces(h + 1) if h + 1 < NH else []
            Tl = tail_pieces(h - 1) if h >= 1 else []
            mrg = []
            ia = ib = 0
            while ia < len(G1n) or ib < len(Tl):
                if ib < len(Tl) and (ia >= len(G1n) or ib * len(G1n) <= ia * len(Tl)):
                    mrg.append(Tl[ib])
                    ib += 1
                else:
                    mrg.append(G1n[ia])
                    ia += 1
            G1n = mrg
            na, nb = len(L), len(G1n)
            ia = ib = 0
            while ia < na or ib < nb:
                if ib < nb and (ia >= na or (ib + 1) * na <= ia * nb):
                    G1n[ib]()
                    ib += 1
                else:
                    L[ia]()
                    ia += 1
            if h == 0:
                tap(P, "oacc0", o_acc[:])
        for pc_ in tail_pieces(NH - 1):
            pc_()
        tap(P, "gdnT", K.gdnT)
        finish_phase(K, P, st)


GDN_HEADS_LIMIT = None
```
